# Optimizing a Trainium2 kernel written in Bass

```python
import jax, jax.numpy as jnp
from jax import lax
import numpy as np

D_MODEL = 2048
BATCH = 4
SEQ = 4096
DEPTH = 1

D_MIX = D_MODEL
D_RNN = D_MIX // 2
RNN_BLOCKS = 16
RNN_BLOCK = D_RNN // RNN_BLOCKS
CONV_WIDTH = 4
LRU_C = 8.0
MLA_HEADS = 8
QK_NOPE = 128
QK_ROPE = 64
V_HEAD = 128
D_ATT = MLA_HEADS * V_HEAD
Q_LORA = 512
KV_LORA = 512
ROPE_THETA = 10000.0
Q_BLOCK = 128
EPS = 1e-6
IN_SPLITS = (D_RNN, D_RNN, Q_LORA, KV_LORA, QK_ROPE, D_ATT)
D_IN = D_RNN + D_RNN + Q_LORA + KV_LORA + QK_ROPE + D_ATT
ADA_SCALE = 0.5

kernel_name = "hymba_rglru_mla_adaln_layer"


def rms_norm(x, gain=None):
    xf = x.astype(jnp.float32)
    y = xf * lax.rsqrt(jnp.mean(xf * xf, axis=-1, keepdims=True) + EPS)
    if gain is not None:
        y = y * gain.astype(jnp.float32)
    return y.astype(x.dtype)


def apply_rope(x, cos, sin):
    half = x.shape[-1] // 2
    x1, x2 = x[..., :half], x[..., half:]
    return jnp.concatenate([x1 * cos - x2 * sin, x1 * sin + x2 * cos], axis=-1).astype(x.dtype)


def split_points(sizes):
    pts, acc = [], 0
    for s in sizes[:-1]:
        acc += s
        pts.append(acc)
    return pts


def _lin_rec_combine(e1, e2):
    a1, b1 = e1
    a2, b2 = e2
    return a1 * a2, a2 * b1 + b2


def rg_lru_branch(xr, conv_w, conv_b, w_rg_a, b_rg_a, w_rg_x, b_rg_x, lru_lambda):
    B, S, C = xr.shape
    xc = lax.conv_general_dilated(
        xr, conv_w[:, None, :].astype(xr.dtype), window_strides=(1,),
        padding=[(CONV_WIDTH - 1, 0)], dimension_numbers=('NWC', 'WIO', 'NWC'),
        feature_group_count=C) + conv_b
    xb = xc.reshape(B, S, RNN_BLOCKS, RNN_BLOCK)
    r = jax.nn.sigmoid(jnp.einsum('bshi,hij->bshj', xb, w_rg_a).reshape(B, S, C) + b_rg_a)
    i = jax.nn.sigmoid(jnp.einsum('bshi,hij->bshj', xb, w_rg_x).reshape(B, S, C) + b_rg_x)
    log_a = -LRU_C * r.astype(jnp.float32) * jax.nn.softplus(-lru_lambda.astype(jnp.float32))
    a = jnp.exp(log_a)
    b = jnp.sqrt(-jnp.expm1(2.0 * log_a)) * (i * xc).astype(jnp.float32)
    _, h = lax.associative_scan(_lin_rec_combine, (a, b), axis=1)
    return h.astype(xr.dtype)


def mla_branch(q_comp, kv_comp, k_rope_raw, positions, q_a_norm, w_uq, kv_a_norm, w_ukv,
               q_norm_nope, q_norm_rope, k_norm_nope, k_norm_rope):
    B, S, _ = q_comp.shape
    q = (rms_norm(q_comp, q_a_norm) @ w_uq).reshape(B, S, MLA_HEADS, QK_NOPE + QK_ROPE)
    kv = (rms_norm(kv_comp, kv_a_norm) @ w_ukv).reshape(B, S, MLA_HEADS, QK_NOPE + V_HEAD)
    q_nope, q_pe = q[..., :QK_NOPE], q[..., QK_NOPE:]
    k_nope, v = kv[..., :QK_NOPE], kv[..., QK_NOPE:]

    inv_freq = 1.0 / (ROPE_THETA ** (jnp.arange(0, QK_ROPE, 2, dtype=jnp.float32) / QK_ROPE))
    ang = positions.astype(jnp.float32)[..., None] * inv_freq
    cos, sin = jnp.cos(ang), jnp.sin(ang)

    q_nope = rms_norm(q_nope, q_norm_nope)
    q_pe = apply_rope(rms_norm(q_pe, q_norm_rope), cos[:, :, None], sin[:, :, None])
    k_nope = rms_norm(k_nope, k_norm_nope)
    k_pe = apply_rope(rms_norm(k_rope_raw, k_norm_rope), cos, sin)

    sm_scale = (QK_NOPE + QK_ROPE) ** -0.5
    n_blk = S // Q_BLOCK
    qn_b = q_nope.reshape(B, n_blk, Q_BLOCK, MLA_HEADS, QK_NOPE).transpose(1, 0, 2, 3, 4)
    qr_b = q_pe.reshape(B, n_blk, Q_BLOCK, MLA_HEADS, QK_ROPE).transpose(1, 0, 2, 3, 4)
    k_idx = jnp.arange(S)

    def attend_block(args):
        qn, qr, blk = args
        s = (jnp.einsum('bqhd,bkhd->bhqk', qn, k_nope)
             + jnp.einsum('bqhr,bkr->bhqk', qr, k_pe)).astype(jnp.float32) * sm_scale
        q_idx = blk * Q_BLOCK + jnp.arange(Q_BLOCK)
        s = jnp.where(q_idx[:, None] >= k_idx[None, :], s, -jnp.inf)
        p = jax.nn.softmax(s, axis=-1).astype(v.dtype)
        return jnp.einsum('bhqk,bkhd->bqhd', p, v)

    o = lax.map(attend_block, (qn_b, qr_b, jnp.arange(n_blk)))
    return o.transpose(1, 0, 2, 3, 4).reshape(B, S, D_ATT)


def setup_inputs(seed: int = 0) -> dict:
    key = jax.random.key(seed)
    ks = jax.random.split(key, 24)
    f32 = jnp.float32
    nrm = lambda k, shape, s: jax.random.normal(k, shape, f32) * s
    gain = lambda k, n: 1.0 + 0.02 * jax.random.normal(k, (DEPTH, n), f32)

    x = jax.random.normal(ks[0], (BATCH, SEQ, D_MODEL), f32)
    c = jax.random.normal(ks[1], (BATCH, D_MODEL), f32)
    offsets = jax.random.randint(ks[2], (BATCH, 1), 0, 1024, dtype=jnp.int32)
    positions = offsets + jnp.arange(SEQ, dtype=jnp.int32)[None, :]

    u = jax.random.uniform(ks[3], (DEPTH, D_RNN), f32, 0.9, 0.999)
    sl = u ** (1.0 / LRU_C)
    lru_lambda = jnp.log(sl) - jnp.log1p(-sl)

    return {
        "x": x,
        "c": c,
        "positions": positions,
        "w_ada": nrm(ks[4], (DEPTH, D_MODEL, 3 * D_MODEL), ADA_SCALE * D_MODEL ** -0.5),
        "b_ada": nrm(ks[5], (DEPTH, 3 * D_MODEL), 0.01),
        "w_in": nrm(ks[6], (DEPTH, D_MODEL, D_IN), D_MODEL ** -0.5),
        "conv_w": nrm(ks[7], (DEPTH, CONV_WIDTH, D_RNN), CONV_WIDTH ** -0.5),
        "conv_b": nrm(ks[8], (DEPTH, D_RNN), 0.01),
        "w_rg_a": nrm(ks[9], (DEPTH, RNN_BLOCKS, RNN_BLOCK, RNN_BLOCK), RNN_BLOCK ** -0.5),
        "b_rg_a": nrm(ks[10], (DEPTH, D_RNN), 0.01),
        "w_rg_x": nrm(ks[11], (DEPTH, RNN_BLOCKS, RNN_BLOCK, RNN_BLOCK), RNN_BLOCK ** -0.5),
        "b_rg_x": nrm(ks[12], (DEPTH, D_RNN), 0.01),
        "lru_lambda": lru_lambda,
        "q_a_norm": gain(ks[13], Q_LORA),
        "w_uq": nrm(ks[14], (DEPTH, Q_LORA, MLA_HEADS * (QK_NOPE + QK_ROPE)), Q_LORA ** -0.5),
        "kv_a_norm": gain(ks[15], KV_LORA),
        "w_ukv": nrm(ks[16], (DEPTH, KV_LORA, MLA_HEADS * (QK_NOPE + V_HEAD)), KV_LORA ** -0.5),
        "q_norm_nope": gain(ks[17], QK_NOPE),
        "q_norm_rope": gain(ks[18], QK_ROPE),
        "k_norm_nope": gain(ks[19], QK_NOPE),
        "k_norm_rope": gain(ks[20], QK_ROPE),
        "w_out": nrm(ks[21], (DEPTH, D_MIX, D_MODEL), D_MIX ** -0.5),
    }


def reference(x, c, positions, w_ada, b_ada, w_in, conv_w, conv_b, w_rg_a, b_rg_a, w_rg_x,
              b_rg_x, lru_lambda, q_a_norm, w_uq, kv_a_norm, w_ukv, q_norm_nope, q_norm_rope,
              k_norm_nope, k_norm_rope, w_out):
    c_act = jax.nn.silu(c)
    for l in range(DEPTH):
        mod = c_act @ w_ada[l] + b_ada[l]
        shift, scale, gate = jnp.split(mod, 3, axis=-1)
        h = rms_norm(x) * (1.0 + scale[:, None, :]) + shift[:, None, :]

        proj = h @ w_in[l]
        xr, gr, qc, kvc, kr, ga = jnp.split(proj, split_points(IN_SPLITS), axis=-1)

        y_rnn = rg_lru_branch(xr, conv_w[l], conv_b[l], w_rg_a[l], b_rg_a[l], w_rg_x[l],
                              b_rg_x[l], lru_lambda[l]) * jax.nn.silu(gr)
        y_att = mla_branch(qc, kvc, kr, positions, q_a_norm[l], w_uq[l], kv_a_norm[l], w_ukv[l],
                           q_norm_nope[l], q_norm_rope[l], k_norm_nope[l],
                           k_norm_rope[l]) * jax.nn.silu(ga)

        y = jnp.concatenate([y_rnn, y_att], axis=-1) @ w_out[l]
        x = x + gate[:, None, :] * y
    return x
```

```python
import bisect
import contextlib
import math
import numpy as np
import concourse.bass as bass
import concourse.mybir as mybir
from concourse.bass_utils import run_bass_kernel_spmd

F32 = mybir.dt.float32
BF16 = mybir.dt.bfloat16
I32 = mybir.dt.int32
AF = mybir.ActivationFunctionType
ALU = mybir.AluOpType
AX = mybir.AxisListType

D = 2048
SEQ = 4096
NT = 32
NTO = 16
NSUP = 8
NSUPO = 4
EPS = 1e-6
SM_SCALE = 192.0 ** -0.5
USE_POW = True
STOP = 99


class _Op:
    __slots__ = ("eng", "fn", "deps", "needs_inc", "cum", "dma_sem", "is_dma", "dma_grp")


class _Rng:
    __slots__ = ("lo", "hi", "last_w", "readers")


class Sched:
    PAGE = 1024

    def __init__(self, nc):
        self.nc = nc
        self.ops = {e: [] for e in ("pe", "act", "dve", "pool", "sp")}
        self.named = {}
        self.ranges = {}
        self.pages = {}
        self.streams = {}
        self.n_streams = 0

    def _resolve(self, k):
        if not (isinstance(k, tuple) and len(k) == 3 and k[0] in ("sb", "ps")):
            r = self.named.get(k)
            if r is None:
                r = _Rng()
                r.lo = r.hi = 0
                r.last_w = None
                r.readers = []
                self.named[k] = r
            return [r]
        space, lo, hi = k
        sm = self.ranges.get(space)
        if sm is None:
            r = _Rng()
            r.last_w, r.readers = None, []
            sm = ([0, 1 << 30], [r])
            self.ranges[space] = sm
        bounds, segs = sm
        for x in (lo, hi):
            i = bisect.bisect_right(bounds, x) - 1
            if bounds[i] != x:
                o = segs[i]
                n = _Rng()
                n.last_w, n.readers = o.last_w, list(o.readers)
                bounds.insert(i + 1, x)
                segs.insert(i + 1, n)
        i = bisect.bisect_left(bounds, lo)
        j = bisect.bisect_left(bounds, hi)
        return segs[i:j]

    def _add(self, eng, fn, reads, writes):
        op = _Op()
        op.eng, op.fn, op.deps = eng, fn, []
        op.needs_inc = False
        op.is_dma = False
        op.dma_sem = None
        op.dma_grp = None
        rr = []
        ww = []
        for k in reads:
            if isinstance(k, tuple) and len(k) == 3 and k[0] == "ps":
                ww.extend(self._resolve(k))
            else:
                rr.extend(self._resolve(k))
        for k in writes:
            ww.extend(self._resolve(k))
        deps = []
        for r in rr + ww:
            if r.last_w is not None:
                deps.append(r.last_w)
        for r in ww:
            deps.extend(r.readers)
        seen = set()
        for d in deps:
            if d is op or id(d) in seen:
                continue
            seen.add(id(d))
            if (not d.is_dma) and d.eng == "pe" and eng == "pe":
                continue
            op.deps.append(d)
            if not d.is_dma:
                d.needs_inc = True
        for r in ww:
            r.last_w = op
            r.readers = []
        wset = set(id(r) for r in ww)
        for r in rr:
            if id(r) not in wset:
                r.readers.append(op)
        self.ops[eng].append(op)
        return op

    def op(self, eng, fn, reads=(), writes=()):
        return self._add(eng, fn, reads, writes)

    def dma(self, queue, stream, out, in_, reads=(), writes=(), cont=False, **kw):
        st = self.streams.get(stream)
        if st is None:
            st = dict(idx=self.n_streams, total=0, last=None, group_prev=None, grp=None)
            self.n_streams += 1
            self.streams[stream] = st

        def fn(e, out=out, in_=in_, kw=kw):
            return e.dma_start(out=out, in_=in_, **kw)

        op = self._add(queue, fn, reads, writes)
        op.is_dma = True
        op.dma_sem = st["idx"]
        st["total"] += 16
        if cont and st["grp"] is not None:
            prev_group = st["group_prev"]
        else:
            prev_group = st["last"]
            st["group_prev"] = prev_group
            st["grp"] = [0]
        op.dma_grp = st["grp"]
        op.dma_grp[0] = st["total"]
        if prev_group is not None and all(d is not prev_group for d in op.deps):
            op.deps.append(prev_group)
        st["last"] = op
        return op

    def final_wait(self):
        op = self._add("sp", lambda e: None, (), ())
        for st in self.streams.values():
            if st["last"] is not None:
                op.deps.append(st["last"])
        for e in ("pe", "act", "dve", "pool"):
            if self.ops[e]:
                last = self.ops[e][-1]
                if not last.is_dma:
                    last.needs_inc = True
                    op.deps.append(last)

    def emit(self):
        nc = self.nc
        with contextlib.ExitStack() as es:
            esem = {e: es.enter_context(nc.semaphore("s_" + e)) for e in ("pe", "act", "dve", "pool")}
            dsem = [es.enter_context(nc.semaphore("d_%d" % i)) for i in range(self.n_streams)]
            for e in self.ops:
                c = 0
                for op in self.ops[e]:
                    if op.is_dma:
                        continue
                    if op.needs_inc:
                        c += 1
                    op.cum = c
            block = es.enter_context(nc.Block())

            def run(ename, e):
                waited = {}
                for op in self.ops[ename]:
                    for d in op.deps:
                        if d.is_dma and op.is_dma and d.dma_grp is op.dma_grp:
                            continue
                        if d.is_dma:
                            key, val, sem = ("d", d.dma_sem), d.dma_grp[0], dsem[d.dma_sem]
                        else:
                            key, val, sem = ("e", d.eng), d.cum, esem[d.eng]
                        if waited.get(key, 0) >= val:
                            continue
                        waited[key] = val
                        e.wait_ge(sem, val)
                    ins = op.fn(e)
                    if ins is None:
                        assert not op.needs_inc and not op.is_dma
                        continue
                    if op.is_dma:
                        ins.then_inc(dsem[op.dma_sem], 16)
                    elif op.needs_inc:
                        ins.then_inc(esem[op.eng], 1)

            @block.sync
            def _(e):
                run("sp", e)

            @block.tensor
            def _(e):
                run("pe", e)

            @block.scalar
            def _(e):
                run("act", e)

            @block.vector
            def _(e):
                run("dve", e)

            @block.gpsimd
            def _(e):
                run("pool", e)


class Buf:
    def __init__(self, ap, key):
        self.ap = ap
        self.k = key

    def sub(self, off, n):
        return (self.k[0], self.k[1] + off, self.k[1] + off + n)


def build_program():
    nc = bass.Bass("TRN2", target_bir_lowering=False)

    def din(name, shape, dt=F32):
        return nc.dram_tensor(name, list(shape), dt, kind="ExternalInput").ap()

    def dscr(name, shape, dt=BF16):
        return nc.dram_tensor(name, list(shape), dt, kind="Internal").ap()

    x_all = din("x_all", [SEQ, D])
    x_own = din("x_own", [NTO * 128, D])
    par_d = din("par", [128, 256])
    cst_d = din("cst", [128, 384])
    posi_d = din("posi", [128, 48], I32)
    w_ada = din("w_ada", [D, 3 * D])
    b_ada = din("b_ada", [128, 3 * D])
    w_in = din("w_in", [D, 4160])
    w_rg_a = din("w_rg_a", [16, 64, 64])
    w_rg_x = din("w_rg_x", [16, 64, 64])
    w_uq = din("w_uq", [512, 1536])
    w_ukv = din("w_ukv", [512, 2048])
    w_out = din("w_out", [D, D])
    out_d = nc.dram_tensor("out", [NTO * 128, D], F32, kind="ExternalOutput").ap()

    KTd = dscr("KTd", [8, 128, SEQ])
    Vd = dscr("Vd", [8, 128, NT, 128])
    QTd = dscr("QTd", [8, 128, NTO * 128])
    QRd = dscr("QRd", [8, 64, NTO * 128])
    SGd = dscr("SGd", [8, 128, NTO * 128])
    HOd = dscr("HOd", [128, 8, NTO * 128])
    YCd = dscr("YCd", [16, 128, NTO * 128])
    GBd = dscr("GBd", [128, 2048], F32)

    ARENA_BYTES = 207 * 1024
    arena = nc.alloc_sbuf_tensor("arena", [128, ARENA_BYTES // 4], F32)
    A = arena.ap() if hasattr(arena, "ap") else arena[:]
    banks = []
    for i in range(8):
        t = nc.alloc_psum_tensor("bank%d" % i, [128, 512], F32)
        banks.append(t.ap() if hasattr(t, "ap") else t[:])

    S = Sched(nc)

    def finish():
        S.final_wait()
        S.emit()
        return nc

    state = dict(ptr=0)

    def sb(shape, dt=F32, parts=128):
        n = 1
        for d_ in shape:
            n *= d_
        esz = 4 if dt in (F32, I32) else 2
        nbytes = (n * esz + 31) // 32 * 32
        lo = state["ptr"]
        hi = lo + nbytes
        assert hi <= ARENA_BYTES, ("SBUF overflow", hi)
        state["ptr"] = hi
        ap = A[:, lo // 4: lo // 4 + (n * esz + 3) // 4]
        if dt != F32:
            ap = ap.bitcast(dt)
            if esz == 2 and (n * esz) % 4:
                ap = ap[:, 0:n]
        if len(shape) == 2:
            ap = ap.rearrange("p (a b) -> p a b", a=shape[0])
        elif len(shape) == 3:
            ap = ap.rearrange("p (a b c) -> p a b c", a=shape[0], b=shape[1])
        return Buf(ap, ("sb", lo, hi))

    def ps(bank, nbanks=1, dt=F32, shape=None, off=0, nbytes=None):
        lo = bank * 2048 + off
        if nbytes is None:
            nbytes = nbanks * 2048 - off
        if nbanks == 1:
            ap = banks[bank]
            ap = ap[:, off // 4: (off + nbytes) // 4]
        else:
            raise AssertionError
        if dt != F32:
            ap = ap.bitcast(dt)
        if shape is not None and len(shape) == 2:
            ap = ap.rearrange("p (a b) -> p a b", a=shape[0])
        return Buf(ap, ("ps", bank * 2048, bank * 2048 + 2048))

    par = sb([256])
    cstf = sb([384])
    posi = sb([48], I32)
    identb = sb([128], BF16)
    maskA = sb([128], BF16)
    maskB = sb([128], BF16)
    onesb = sb([128], BF16)
    neghalf = sb([16])
    onesf = sb([8])
    hb_a = sb([8])
    hb_x = sb([8])
    scale1 = sb([16])
    shiftp = sb([16])
    cl_half = sb([8])
    cl_one = sb([8])
    KpeT = sb([SEQ], BF16)
    cosA = sb([32, 32])
    sinA = sb([32, 32])
    cosO = sb([16, 32])
    sinO = sb([16, 32])
    hist = sb([8, 3])
    hstate = sb([8])
    PERS_END = state["ptr"]

    P_ = par.ap
    c_par = P_[:, 0:16]

    def rnn_par(kidx):
        return P_[:, 16:80].rearrange("p (c k) -> p c k", k=8)[:, :, kidx]

    gq_a = P_[:, 80:84]
    gkv_a = P_[:, 84:88]
    gqn = P_[:, 88:89]
    gkn = P_[:, 89:90]
    sel0 = P_[:, 90:91]
    sel1 = P_[:, 91:92]
    gqr_bc = P_[:, 92:156]
    gkr_bc = P_[:, 156:220]
    invf = P_[:, 220:252]
    identf = cstf.ap[:, 0:128]

    S.dma("sp", "ld0", par.ap, par_d, writes=[par.k])
    S.dma("sp", "ld0", cstf.ap, cst_d, writes=[cstf.k], cont=True)
    S.dma("sp", "ld0", posi.ap, posi_d, writes=[posi.k], cont=True)
    S.op("dve", lambda e: e.tensor_copy(out=identb.ap, in_=cstf.ap[:, 0:128]), reads=[cstf.k], writes=[identb.k])
    S.op("dve", lambda e: e.tensor_copy(out=maskA.ap, in_=cstf.ap[:, 128:256]), reads=[cstf.k], writes=[maskA.k])
    S.op("dve", lambda e: e.tensor_copy(out=maskB.ap, in_=cstf.ap[:, 256:384]), reads=[cstf.k], writes=[maskB.k])
    S.op("pool", lambda e: e.memset(onesb.ap, 1.0), writes=[onesb.k])
    S.op("pool", lambda e: e.memset(neghalf.ap, -0.5), writes=[neghalf.k])
    S.op("pool", lambda e: e.memset(onesf.ap, 1.0), writes=[onesf.k])
    S.op("pool", lambda e: e.memset(hist.ap, 0.0), writes=[hist.k])
    S.op("pool", lambda e: e.memset(hstate.ap, 0.0), writes=[hstate.k])
    S.op("pool", lambda e: e.memset(KpeT.ap[64:128, :], 0.0), writes=["kpeT_pad"])

    def rsqrt_small(out_ap, in_ap, n, rk, wk):
        if USE_POW:
            S.op("pool", lambda e: e.tensor_tensor(out=out_ap, in0=in_ap, in1=neghalf.ap[:, 0:n], op=ALU.pow),
                 reads=list(rk) + [neghalf.k], writes=wk)
        else:
            S.op("act", lambda e: e.activation(out=out_ap, in_=in_ap, func=AF.Sqrt), reads=rk, writes=wk)
            S.op("dve", lambda e: e.reciprocal(out=out_ap, in_=out_ap), reads=wk, writes=wk)

    state["ptr"] = PERS_END + 71680
    cact = sb([16])
    cbc = sb([16, 128], BF16)
    tmpe = sb([8])
    wad = [sb([16, 512], BF16), sb([16, 512], BF16)]
    bad = [sb([512]), sb([512])]
    modbc = sb([4096])
    gate_tmp = sb([2048])
    dtmp = sb([16, 128])
    angt = sb([32, 32])
    angk = sb([32, 32], I32)
    angf = sb([32, 32])
    posf = sb([48])

    lam = rnn_par(7)
    S.op("act", lambda e: e.activation(out=tmpe.ap, in_=lam, func=AF.Exp, scale=-1.0), reads=[par.k], writes=[tmpe.k])
    S.op("act", lambda e: e.activation(out=tmpe.ap, in_=tmpe.ap, func=AF.Ln, bias=onesf.ap[:, 0:1]), reads=[tmpe.k, onesf.k], writes=[tmpe.k])
    S.op("dve", lambda e: e.tensor_scalar(out=hb_a.ap, in0=rnn_par(5), scalar1=0.5, scalar2=None, op0=ALU.mult),
         reads=[par.k], writes=[hb_a.k])
    S.op("dve", lambda e: e.tensor_scalar(out=hb_x.ap, in0=rnn_par(6), scalar1=0.5, scalar2=None, op0=ALU.mult),
         reads=[par.k], writes=[hb_x.k])
    S.op("dve", lambda e: e.tensor_scalar(out=cl_one.ap, in0=tmpe.ap, scalar1=-8.0, scalar2=None, op0=ALU.mult),
         reads=[tmpe.k], writes=[cl_one.k])
    S.op("dve", lambda e: e.tensor_scalar(out=cl_half.ap, in0=tmpe.ap, scalar1=-4.0, scalar2=None, op0=ALU.mult),
         reads=[tmpe.k], writes=[cl_half.k])
    S.op("act", lambda e: e.activation(out=cact.ap, in_=c_par, func=AF.Silu), reads=[par.k], writes=[cact.k])
    S.op("dve", lambda e: e.tensor_copy(out=cbc.ap, in_=cact.ap.unsqueeze(2).broadcast_to([128, 16, 128])),
         reads=[cact.k], writes=[cbc.k])

    def rope_tables(pos_ap, n, cos_b, sin_b):
        pf = posf.ap[:, 0:n]
        S.op("dve", lambda e: e.tensor_copy(out=pf, in_=pos_ap), reads=[posi.k], writes=[posf.k])
        a3 = angt.ap[:, 0:n, :]
        k3 = angk.ap[:, 0:n, :]
        f3 = angf.ap[:, 0:n, :]
        S.op("dve", lambda e: e.tensor_tensor(out=a3, in0=pf.unsqueeze(2).broadcast_to([128, n, 32]),
                                              in1=invf.unsqueeze(1).broadcast_to([128, n, 32]), op=ALU.mult),
             reads=[posf.k, par.k], writes=[angt.k])
        inv2pi = 1.0 / (2.0 * math.pi)
        for (dst, shift) in ((sin_b, 0.0), (cos_b, 0.25)):
            S.op("dve", lambda e, shift=shift: e.tensor_scalar(out=f3, in0=a3, scalar1=inv2pi, scalar2=shift,
                                                              op0=ALU.mult, op1=ALU.add),
                 reads=[angt.k], writes=[angf.k])
            S.op("dve", lambda e: e.tensor_copy(out=k3, in_=f3), reads=[angf.k], writes=[angk.k])
            S.op("dve", lambda e, dst=dst: e.tensor_copy(out=dst.ap, in_=k3), reads=[angk.k], writes=[dst.k])
            S.op("dve", lambda e, dst=dst: e.tensor_tensor(out=dst.ap, in0=f3, in1=dst.ap, op=ALU.subtract),
                 reads=[angf.k, dst.k], writes=[dst.k])
            S.op("act", lambda e, dst=dst: e.activation(out=dst.ap, in_=dst.ap, func=AF.Sin, scale=6.283185),
                 reads=[dst.k], writes=[dst.k])

    rope_tables(posi.ap[:, 0:32], 32, cosA, sinA)
    rope_tables(posi.ap[:, 32:48], 16, cosO, sinO)

    w_ada_v = w_ada.rearrange("(p j) n -> p j n", j=16)
    pmod = [ps(0), ps(1)]
    for nci in range(12):
        sl = nci % 2
        S.dma("pool", "wad%d" % sl, wad[sl].ap, w_ada_v[:, :, nci * 512:(nci + 1) * 512], writes=[wad[sl].k])
        S.dma("sp", "bad%d" % sl, bad[sl].ap, b_ada[:, nci * 512:(nci + 1) * 512], writes=[bad[sl].k])

        def mm(e, sl=sl):
            for j in range(16):
                r = e.matmul(pmod[sl].ap, lhsT=cbc.ap[:, j, :], rhs=wad[sl].ap[:, j, :], start=(j == 0), stop=(j == 15))
            return r
        S.op("pe", mm, reads=[cbc.k, wad[sl].k], writes=[pmod[sl].k])
        if nci < 8:
            dst, dk = modbc.ap[:, nci * 512:(nci + 1) * 512], modbc.k
        else:
            dst, dk = gate_tmp.ap[:, (nci - 8) * 512:(nci - 7) * 512], gate_tmp.k
        S.op("dve", lambda e, sl=sl, dst=dst: e.tensor_tensor(out=dst, in0=pmod[sl].ap, in1=bad[sl].ap, op=ALU.add),
             reads=[pmod[sl].k, bad[sl].k], writes=[dk])
    S.dma("sp", "gbw", GBd, gate_tmp.ap, reads=[gate_tmp.k], writes=["GBd"])
    for (dst, off, addone) in ((shiftp, 0, False), (scale1, 2048, True)):
        src = modbc.ap[:, off:off + 2048].rearrange("p (j n) -> p j n", j=16)
        S.op("dve", lambda e, src=src: e.tensor_tensor(out=dtmp.ap, in0=src,
                                                        in1=identf.unsqueeze(1).broadcast_to([128, 16, 128]), op=ALU.mult),
             reads=[modbc.k, cstf.k], writes=[dtmp.k])
        S.op("dve", lambda e, dst=dst: e.tensor_reduce(out=dst.ap, in_=dtmp.ap, axis=AX.X, op=ALU.add),
             reads=[dtmp.k], writes=[dst.k])
        if addone:
            S.op("dve", lambda e, dst=dst: e.tensor_scalar(out=dst.ap, in0=dst.ap, scalar1=1.0, scalar2=None, op0=ALU.add),
                 reads=[dst.k], writes=[dst.k])

    if STOP == 0:
        return finish()

    def interleave(gens, weights=None):
        pairs = [(g, (weights[i] if weights else 1)) for i, g in enumerate(gens) if g is not None]
        while pairs:
            for (g, w) in list(pairs):
                for _ in range(w):
                    try:
                        next(g)
                    except StopIteration:
                        pairs.remove((g, w))
                        break

    def gen_norm_tile(xsrc_ap, xt, xn, ssb, hT, ti, pst):
        S.dma("sp", "x%d" % id(xt), xt.ap, xsrc_ap, writes=[xt.k])
        S.op("act", lambda e: e.activation(out=xn.ap, in_=xt.ap, func=AF.Square, accum_out=ssb.ap[:, 0:1]),
             reads=[xt.k], writes=[xn.k, ssb.k])
        yield
        S.op("dve", lambda e: e.tensor_scalar(out=ssb.ap[:, 1:2], in0=ssb.ap[:, 0:1], scalar1=1.0 / D, scalar2=EPS,
                                              op0=ALU.mult, op1=ALU.add), reads=[ssb.k], writes=[ssb.k])
        rsqrt_small(ssb.ap[:, 2:3], ssb.ap[:, 1:2], 1, [ssb.k], [ssb.k])
        yield
        S.op("pool", lambda e: e.tensor_scalar(out=xn.ap, in0=xt.ap, scalar1=ssb.ap[:, 2:3], scalar2=0.0, op0=ALU.mult, op1=ALU.add),
             reads=[xt.k, ssb.k], writes=[xn.k])
        yield
        hTe, hTo = hT

        def tr0(e):
            for j in range(8):
                r = e.transpose(out=pst[0].ap[:, j, :], in_=xn.ap[:, j * 128:(j + 1) * 128], identity=identb.ap)
            return r

        def tr1(e):
            for j in range(8, 16):
                r = e.transpose(out=pst[1].ap[:, j - 8, :], in_=xn.ap[:, j * 128:(j + 1) * 128], identity=identb.ap)
            return r
        S.op("pe", tr0, reads=[xn.k, identb.k], writes=[pst[0].k])
        S.op("pe", tr1, reads=[xn.k, identb.k], writes=[pst[1].k])
        yield

        def ev_act(e):
            for j in range(0, 8):
                r = e.activation(out=hTe.ap[:, j, ti * 128:(ti + 1) * 128], in_=pst[0].ap[:, j, :],
                                 func=AF.Identity, scale=scale1.ap[:, j:j + 1], bias=shiftp.ap[:, j:j + 1])
            return r

        def ev_dve(e):
            for j in range(8, 16):
                r = e.tensor_scalar(out=hTo.ap[:, j - 8, ti * 128:(ti + 1) * 128], in0=pst[1].ap[:, j - 8, :],
                                    scalar1=scale1.ap[:, j:j + 1], scalar2=shiftp.ap[:, j:j + 1],
                                    op0=ALU.mult, op1=ALU.add)
            return r
        S.op("act", ev_act, reads=[pst[0].k, scale1.k, shiftp.k], writes=[hTe.k])
        S.op("dve", ev_dve, reads=[pst[1].k, scale1.k, shiftp.k], writes=[hTo.k])
        yield

    def hsel(hT, j):
        return hT[j // 8].ap[:, j % 8]

    state["ptr"] = PERS_END
    w1 = sb([16, 1600], BF16)
    wukv = sb([4, 2048], BF16)
    wrg = sb([8, 2, 128], BF16)
    xts = [sb([2048]), sb([2048])]
    xn = sb([2048], BF16)
    ssb = sb([4])
    hTs = [(sb([8, 512], BF16), sb([8, 512], BF16)), (sb([8, 512], BF16), sb([8, 512], BF16))]
    XR = sb([2, 515])
    XC = sb([2, 512])
    XCB = sb([2, 512], BF16)
    THR = sb([2, 512])
    THI = sb([2, 512])
    A2 = sb([2, 512])
    HB = sb([2, 512])
    tmpo = sb([2, 2, 128])
    hst = sb([8, 256], BF16)
    kvcb = sb([512], BF16)
    kvcT = sb([4, 128], BF16)
    U = sb([1024])
    sqs = sb([4, 128])
    kvs = sb([16])
    ssn = sb([8])
    rkn = sb([8])
    kn = sb([8, 128], BF16)
    vst = sb([8, 4, 128], BF16)
    KTst = sb([8, 512], BF16)
    krn = sb([2, 32])
    ropA = sb([2, 32])
    ropB = sb([2, 32])
    kpe = sb([64], BF16)

    def cik(buf, ci, n, esz=4):
        return buf.sub(ci * n * esz, n * esz)

    w_in_v = w_in.rearrange("(j p) n -> p j n", p=128)
    S.dma("pool", "w1", w1.ap[:, :, 0:1024], w_in_v[:, :, 0:1024], writes=[w1.k])
    S.dma("pool", "w1", w1.ap[:, :, 1024:1600], w_in_v[:, :, 2560:3136], writes=[w1.k], cont=True)
    S.op("pool", lambda e: e.memset(wrg.ap, 0.0), writes=[wrg.k])
    for gi, wsrc in enumerate((w_rg_a, w_rg_x)):
        for hh in range(2):
            src = wsrc.rearrange("(c two) i j -> two i c j", two=2)[hh]
            S.dma("pool", "wrg", wrg.ap[hh * 64:(hh + 1) * 64, :, gi, hh * 64:(hh + 1) * 64], src, writes=[wrg.k])
    w_ukv_v = w_ukv.rearrange("(c p) n -> p c n", p=128)
    for ck in range(4):
        stg = xts[ck % 2]
        S.dma("sp", "wst%d" % (ck % 2), stg.ap, w_ukv_v[:, ck, :], writes=[stg.k])
        S.op("act", lambda e, ck=ck, stg=stg: e.activation(out=wukv.ap[:, ck, :], in_=stg.ap, func=AF.Identity,
                                                          scale=gkv_a[:, ck:ck + 1]),
             reads=[stg.k, par.k], writes=[wukv.k])

    pst = [ps(0, dt=BF16, shape=[8, 128]), ps(1, dt=BF16, shape=[8, 128])]
    pxr = [ps(2), ps(3)]
    pgr = ps(4)
    pgi = ps(5)
    pkv = ps(6)
    pmisc_kr = ps(7, off=0, nbytes=256)
    pmisc_kT = ps(7, off=256, nbytes=1024, dt=BF16, shape=[4, 128])
    pmisc_kp = ps(7, off=1280, nbytes=256, dt=BF16)
    pup = [ps(6), ps(7)]
    pKT = ps(6, dt=BF16, shape=[8, 128])

    cw = [rnn_par(k) for k in range(4)]
    cb_ = rnn_par(4)

    def gen_norm_super(u):
        for ti in range(4):
            t = 4 * u + ti
            yield from gen_norm_tile(x_all[t * 128:(t + 1) * 128, :], xts[t % 2], xn, ssb, hTs[u % 2], ti, pst)

    def gen_rnn(u, ci):
        hT = hTs[u % 2]
        hkeys = [hT[0].k, hT[1].k]
        px = pxr[ci]
        for g in range(4):
            cc = 2 * g + ci
            S.op("pool", lambda e, cc=cc: e.tensor_copy(out=XR.ap[:, ci, 0:3], in_=hist.ap[:, cc, :]),
                 reads=[hist.sub(cc * 12, 12)], writes=[XR.sub(ci * 2060, 12)])

            def mmx(e, cc=cc):
                for j in range(16):
                    r = e.matmul(px.ap, lhsT=w1.ap[:, j, cc * 128:(cc + 1) * 128], rhs=hsel(hT, j),
                                 start=(j == 0), stop=(j == 15))
                return r
            S.op("pe", mmx, reads=[w1.k] + hkeys, writes=[px.k])
            yield
            S.op("act", lambda e: e.activation(out=XR.ap[:, ci, 3:515], in_=px.ap, func=AF.Copy),
                 reads=[px.k], writes=[XR.sub(ci * 2060 + 12, 2048)])
            yield
            S.op("pool", lambda e, cc=cc: e.tensor_copy(out=hist.ap[:, cc, :], in_=XR.ap[:, ci, 512:515]),
                 reads=[XR.sub(ci * 2060 + 12, 2048)], writes=[hist.sub(cc * 12, 12)])
            S.op("dve", lambda e, cc=cc: e.tensor_scalar(out=XC.ap[:, ci, :], in0=XR.ap[:, ci, 3:515],
                                                       scalar1=cw[3][:, cc:cc + 1], scalar2=cb_[:, cc:cc + 1],
                                                       op0=ALU.mult, op1=ALU.add),
                 reads=[cik(XR, ci, 515), par.k], writes=[cik(XC, ci, 512)])
            yield
            for k in (2, 1, 0):
                S.op("dve", lambda e, cc=cc, k=k: e.scalar_tensor_tensor(
                    out=XC.ap[:, ci, :], in0=XR.ap[:, ci, k:k + 512], scalar=cw[k][:, cc:cc + 1],
                    in1=XC.ap[:, ci, :], op0=ALU.mult, op1=ALU.add),
                    reads=[cik(XR, ci, 515), cik(XC, ci, 512), par.k], writes=[cik(XC, ci, 512)])
                yield
            S.op("dve", lambda e: e.tensor_copy(out=XCB.ap[:, ci, :], in_=XC.ap[:, ci, :]),
                 reads=[cik(XC, ci, 512)], writes=[cik(XCB, ci, 512, 2)])
            yield
            S.op("pe", lambda e, cc=cc: e.matmul(pgr.ap, lhsT=wrg.ap[:, cc, 0, :], rhs=XCB.ap[:, ci, :],
                                                 start=True, stop=True),
                 reads=[wrg.k, cik(XCB, ci, 512, 2)], writes=[pgr.k])
            S.op("pe", lambda e, cc=cc: e.matmul(pgi.ap, lhsT=wrg.ap[:, cc, 1, :], rhs=XCB.ap[:, ci, :],
                                                 start=True, stop=True),
                 reads=[wrg.k, cik(XCB, ci, 512, 2)], writes=[pgi.k])
            S.op("act", lambda e, cc=cc: e.activation(out=THR.ap[:, ci, :], in_=pgr.ap, func=AF.Tanh, scale=0.5,
                                                     bias=hb_a.ap[:, cc:cc + 1]),
                 reads=[pgr.k, hb_a.k], writes=[cik(THR, ci, 512)])
            S.op("act", lambda e, cc=cc: e.activation(out=THI.ap[:, ci, :], in_=pgi.ap, func=AF.Tanh, scale=0.5,
                                                     bias=hb_x.ap[:, cc:cc + 1]),
                 reads=[pgi.k, hb_x.k], writes=[cik(THI, ci, 512)])
            yield
            S.op("act", lambda e, cc=cc: e.activation(out=A2.ap[:, ci, :], in_=THR.ap[:, ci, :], func=AF.Exp,
                                                     scale=cl_one.ap[:, cc:cc + 1], bias=cl_one.ap[:, cc:cc + 1]),
                 reads=[cik(THR, ci, 512), cl_one.k], writes=[cik(A2, ci, 512)])
            S.op("act", lambda e, cc=cc: e.activation(out=THR.ap[:, ci, :], in_=THR.ap[:, ci, :], func=AF.Exp,
                                                     scale=cl_half.ap[:, cc:cc + 1], bias=cl_half.ap[:, cc:cc + 1]),
                 reads=[cik(THR, ci, 512), cl_half.k], writes=[cik(THR, ci, 512)])
            S.op("dve", lambda e: e.scalar_tensor_tensor(out=THI.ap[:, ci, :], in0=THI.ap[:, ci, :], scalar=1.0,
                                                         in1=XC.ap[:, ci, :], op0=ALU.add, op1=ALU.mult),
                 reads=[cik(THI, ci, 512), cik(XC, ci, 512)], writes=[cik(THI, ci, 512)])
            yield
            S.op("act", lambda e: e.activation(out=A2.ap[:, ci, :], in_=A2.ap[:, ci, :], func=AF.Sqrt, scale=-1.0,
                                               bias=onesf.ap[:, 0:1]),
                 reads=[cik(A2, ci, 512), onesf.k], writes=[cik(A2, ci, 512)])
            yield
            S.op("dve", lambda e: e.scalar_tensor_tensor(out=THI.ap[:, ci, :], in0=A2.ap[:, ci, :], scalar=0.5,
                                                         in1=THI.ap[:, ci, :], op0=ALU.mult, op1=ALU.mult),
                 reads=[cik(A2, ci, 512), cik(THI, ci, 512)], writes=[cik(THI, ci, 512)])
            yield
            S.op("dve", lambda e, cc=cc: e.tensor_tensor_scan(
                out=HB.ap[:, ci, :], data0=THR.ap[:, ci, :], data1=THI.ap[:, ci, :],
                initial=hstate.ap[:, cc:cc + 1], op0=ALU.mult, op1=ALU.add),
                reads=[cik(THR, ci, 512), cik(THI, ci, 512), hstate.sub(cc * 4, 4)], writes=[cik(HB, ci, 512)])
            yield
            S.op("pool", lambda e, cc=cc: e.tensor_copy(out=hstate.ap[:, cc:cc + 1], in_=HB.ap[:, ci, 511:512]),
                 reads=[cik(HB, ci, 512)], writes=[hstate.sub(cc * 4, 4)])
            hv = HB.ap[:, ci, :].rearrange("p (a two n) -> p a two n", two=2, n=128)
            S.op("pool", lambda e, hv=hv: e.tensor_scalar(out=tmpo.ap[:, ci], in0=hv[:, :, 0, :], scalar1=sel0,
                                                          scalar2=0.0, op0=ALU.mult, op1=ALU.add),
                 reads=[cik(HB, ci, 512), par.k], writes=[cik(tmpo, ci, 256)])
            yield
            S.op("dve", lambda e, hv=hv, cc=cc: e.scalar_tensor_tensor(
                out=hst.ap[:, cc, :].rearrange("p (a n) -> p a n", n=128), in0=hv[:, :, 1, :], scalar=sel1,
                in1=tmpo.ap[:, ci], op0=ALU.mult, op1=ALU.add),
                reads=[cik(HB, ci, 512), cik(tmpo, ci, 256), par.k], writes=[hst.sub(cc * 512, 512)])
            yield

    def gen_kr_tile(u, ti):
        hT = hTs[u % 2]
        hkeys = [hT[0].k, hT[1].k]
        if True:
            t = 4 * u + ti
            tcol = slice(ti * 128, (ti + 1) * 128)

            def mkr(e, tcol=tcol):
                for j in range(16):
                    r = e.matmul(pmisc_kr.ap, lhsT=hsel(hT, j)[:, tcol], rhs=w1.ap[:, j, 1536:1600], start=(j == 0), stop=(j == 15))
                return r
            S.op("pe", mkr, reads=[w1.k] + hkeys, writes=[pmisc_kr.k])
            yield
            S.op("act", lambda e: e.activation(out=ropA.ap, in_=pmisc_kr.ap.rearrange("p (a n) -> p a n", a=2),
                                               func=AF.Square, accum_out=kvs.ap[:, 4:5]),
                 reads=[pmisc_kr.k], writes=[ropA.k, kvs.sub(16, 4)])
            S.op("dve", lambda e: e.tensor_scalar(out=kvs.ap[:, 5:6], in0=kvs.ap[:, 4:5], scalar1=1.0 / 64, scalar2=EPS,
                                                  op0=ALU.mult, op1=ALU.add), reads=[kvs.sub(16, 4)], writes=[kvs.sub(20, 4)])
            yield
            rsqrt_small(kvs.ap[:, 6:7], kvs.ap[:, 5:6], 1, [kvs.sub(20, 4)], [kvs.sub(24, 4)])
            yield
            S.op("dve", lambda e: e.scalar_tensor_tensor(out=krn.ap.rearrange("p a n -> p (a n)"), in0=pmisc_kr.ap,
                                                         scalar=kvs.ap[:, 6:7], in1=gkr_bc, op0=ALU.mult, op1=ALU.mult),
                 reads=[pmisc_kr.k, kvs.sub(24, 4), par.k], writes=[krn.k])
            yield
            cos_t = cosA.ap[:, t, :].unsqueeze(1).broadcast_to([128, 2, 32])
            sin_t = sinA.ap[:, t, :].unsqueeze(1).broadcast_to([128, 2, 32])
            S.op("pool", lambda e, cos_t=cos_t: e.tensor_tensor(out=ropA.ap, in0=krn.ap, in1=cos_t, op=ALU.mult),
                 reads=[krn.k, cosA.k], writes=[ropA.k])
            S.op("pool", lambda e, sin_t=sin_t: e.tensor_tensor(out=ropB.ap, in0=krn.ap, in1=sin_t, op=ALU.mult),
                 reads=[krn.k, sinA.k], writes=[ropB.k])
            yield
            S.op("pool", lambda e: e.tensor_tensor(out=kpe.ap[:, 0:32], in0=ropA.ap[:, 0, :], in1=ropB.ap[:, 1, :], op=ALU.subtract),
                 reads=[ropA.k, ropB.k], writes=[kpe.k])
            S.op("pool", lambda e: e.tensor_tensor(out=kpe.ap[:, 32:64], in0=ropB.ap[:, 0, :], in1=ropA.ap[:, 1, :], op=ALU.add),
                 reads=[ropA.k, ropB.k], writes=[kpe.k])
            yield
            S.op("pe", lambda e: e.transpose(out=pmisc_kp.ap[0:64, :], in_=kpe.ap, identity=identb.ap),
                 reads=[kpe.k, identb.k], writes=[pmisc_kp.k])
            yield
            S.op("dve", lambda e, t=t: e.tensor_copy(out=KpeT.ap[0:64, t * 128:(t + 1) * 128], in_=pmisc_kp.ap[0:64, :]),
                 reads=[pmisc_kp.k], writes=[("kpeT", t)])
            yield

    def gen_kv(u):
        hT = hTs[u % 2]
        hkeys = [hT[0].k, hT[1].k]
        for ti in range(4):
            t = 4 * u + ti
            tcol = slice(ti * 128, (ti + 1) * 128)

            def mkv(e, tcol=tcol):
                for j in range(16):
                    r = e.matmul(pkv.ap, lhsT=hsel(hT, j)[:, tcol], rhs=w1.ap[:, j, 1024:1536], start=(j == 0), stop=(j == 15))
                return r
            S.op("pe", mkv, reads=[w1.k] + hkeys, writes=[pkv.k])
            yield

            S.op("act", lambda e: e.activation(out=U.ap[:, 0:512], in_=pkv.ap, func=AF.Square, accum_out=kvs.ap[:, 0:1]),
                 reads=[pkv.k], writes=[U.k, kvs.sub(0, 4)])
            S.op("act", lambda e: e.activation(out=kvcb.ap, in_=pkv.ap, func=AF.Copy), reads=[pkv.k], writes=[kvcb.k])
            yield
            yield from gen_kr_tile(u, ti)
            S.op("dve", lambda e: e.tensor_scalar(out=kvs.ap[:, 1:2], in0=kvs.ap[:, 0:1], scalar1=1.0 / 512, scalar2=EPS,
                                                  op0=ALU.mult, op1=ALU.add), reads=[kvs.sub(0, 4)], writes=[kvs.sub(4, 4)])
            rsqrt_small(kvs.ap[:, 2:3], kvs.ap[:, 1:2], 1, [kvs.sub(4, 4)], [kvs.sub(8, 4)])
            S.op("dve", lambda e: e.tensor_scalar(out=kvs.ap[:, 3:4], in0=kvs.ap[:, 1:2], scalar1=EPS, scalar2=None,
                                                  op0=ALU.mult), reads=[kvs.sub(4, 4)], writes=[kvs.sub(12, 4)])

            def trk(e):
                for ck in range(4):
                    r = e.transpose(out=pmisc_kT.ap[:, ck, :], in_=kvcb.ap[:, ck * 128:(ck + 1) * 128], identity=identb.ap)
                return r
            S.op("pe", trk, reads=[kvcb.k, identb.k], writes=[pmisc_kT.k])
            yield
            S.op("dve", lambda e: e.tensor_copy(out=kvcT.ap, in_=pmisc_kT.ap), reads=[pmisc_kT.k], writes=[kvcT.k])
            yield
            for hf in range(2):
                for n in range(2):
                    def mup(e, hf=hf, n=n):
                        for ck in range(4):
                            c0 = hf * 1024 + n * 512
                            r = e.matmul(pup[n].ap, lhsT=kvcT.ap[:, ck, :], rhs=wukv.ap[:, ck, c0:c0 + 512],
                                         start=(ck == 0), stop=(ck == 3))
                        return r
                    S.op("pe", mup, reads=[kvcT.k, wukv.k], writes=[pup[n].k])
                    S.op("act", lambda e, n=n: e.activation(out=U.ap[:, n * 512:(n + 1) * 512], in_=pup[n].ap, func=AF.Copy),
                         reads=[pup[n].k], writes=[U.sub(n * 2048, 2048)])
                    yield
                U3 = U.ap.rearrange("p (h c) -> p h c", h=4)
                Un = U3[:, :, 0:128]
                Uv = U3[:, :, 128:256]
                hs = slice(4 * hf, 4 * hf + 4)
                S.op("dve", lambda e, Un=Un: e.tensor_tensor(out=sqs.ap, in0=Un, in1=Un, op=ALU.mult), reads=[U.k], writes=[sqs.k])
                S.op("dve", lambda e, hs=hs: e.tensor_reduce(out=ssn.ap[:, hs], in_=sqs.ap, axis=AX.X, op=ALU.add),
                     reads=[sqs.k], writes=[ssn.k])
                yield
                S.op("dve", lambda e, hs=hs: e.tensor_scalar(out=ssn.ap[:, hs], in0=ssn.ap[:, hs], scalar1=1.0 / 128,
                                                             scalar2=kvs.ap[:, 3:4], op0=ALU.mult, op1=ALU.add),
                     reads=[ssn.k, kvs.sub(12, 4)], writes=[ssn.k])
                rsqrt_small(rkn.ap[:, hs], ssn.ap[:, hs], 4, [ssn.k], [rkn.k])
                yield
                S.op("dve", lambda e, Un=Un, hs=hs: e.tensor_tensor(out=kn.ap[:, hs, :], in0=Un,
                                                                    in1=rkn.ap[:, hs].unsqueeze(2).broadcast_to([128, 4, 128]),
                                                                    op=ALU.mult), reads=[U.k, rkn.k], writes=[kn.k])
                S.op("pool", lambda e, Uv=Uv, ti=ti, hs=hs: e.tensor_scalar(out=vst.ap[:, hs, ti, :], in0=Uv, scalar1=kvs.ap[:, 2:3],
                                                                            scalar2=0.0, op0=ALU.mult, op1=ALU.add),
                     reads=[U.k, kvs.sub(8, 4)], writes=[vst.k])
                yield

            def trK(e):
                for h in range(8):
                    r = e.transpose(out=pKT.ap[:, h, :], in_=kn.ap[:, h, :], identity=identb.ap)
                return r
            S.op("pe", trK, reads=[kn.k, identb.k], writes=[pKT.k])
            yield
            S.op("act", lambda e, tcol=tcol: e.activation(out=KTst.ap[:, :, tcol], in_=pKT.ap, func=AF.Identity, scale=gkn),
                 reads=[pKT.k, par.k], writes=[KTst.k])
            yield
        S.dma("sp", "ktd", KTd[:, :, u * 512:(u + 1) * 512].rearrange("h p n -> p h n"), KTst.ap,
              reads=[KTst.k], writes=[("KTd", u)])
        S.dma("sp", "vd", Vd[:, :, 4 * u:4 * u + 4, :].rearrange("h p t d -> p h t d"), vst.ap,
              reads=[vst.k], writes=[("Vd", u)])
        yield

    interleave([gen_norm_super(0)])
    for u in range(NSUP):
        interleave([gen_rnn(u, 0), gen_kv(u), gen_rnn(u, 1), gen_norm_super(u + 1) if u + 1 < NSUP else None])
        S.dma("sp", "hod", HOd[:, :, u * 256:(u + 1) * 256], hst.ap, reads=[hst.k], writes=[("HOd", u)])

    if STOP == 1:
        return finish()

    HTd = dscr("HTd", [NSUPO, 2, 128, 8, 512])
    state["ptr"] = PERS_END
    w2 = sb([16, 1536], BF16)
    wuq = sb([4, 1536], BF16)
    xts2 = [sb([2048]), sb([2048])]
    xn2 = sb([2048], BF16)
    ssb2 = sb([4])
    hT2s = [(sb([8, 512], BF16), sb([8, 512], BF16)), (sb([8, 512], BF16), sb([8, 512], BF16))]
    hown = sb([8, 512], BF16)
    sgt = [sb([512]), sb([512])]
    yst = sb([8, 512], BF16)
    qcb = sb([512], BF16)
    qcT = sb([4, 128], BF16)
    Uq = sb([1536])
    sqq = sb([1536])
    qs = sb([8])
    ssqn = sb([8])
    ssqr = sb([8])
    rqn = sb([8])
    rqr = sb([8])
    qn = sb([8, 128], BF16)
    qrn = sb([8, 2, 32])
    rqA = sb([8, 2, 32])
    rqB = sb([8, 2, 32])
    qpe = sb([8, 2, 32], BF16)
    QTst = sb([8, 512], BF16)
    QRst = sb([4, 512], BF16)

    S.dma("pool", "w2", w2.ap, w_in_v[:, :, 1024:2560], writes=[w2.k])
    w_uq_v = w_uq.rearrange("(c p) n -> p c n", p=128)
    for ck in range(4):
        stg = xts2[ck % 2]
        S.dma("sp", "wst2%d" % (ck % 2), stg.ap[:, 0:1536], w_uq_v[:, ck, :], writes=[stg.k])
        S.op("act", lambda e, ck=ck, stg=stg: e.activation(out=wuq.ap[:, ck, :], in_=stg.ap[:, 0:1536], func=AF.Identity,
                                                          scale=gq_a[:, ck:ck + 1]),
             reads=[stg.k, par.k], writes=[wuq.k])

    pst2 = [ps(0, dt=BF16, shape=[8, 128]), ps(1, dt=BF16, shape=[8, 128])]
    pfm = [ps(2), ps(3)]
    pqc = ps(4)
    pqT = ps(5, off=0, nbytes=1024, dt=BF16, shape=[4, 128])
    pqrT = ps(5, off=1024, nbytes=1024, dt=BF16, shape=[4, 128])
    puq = [ps(6), ps(7), ps(4)]
    pQT = ps(5, dt=BF16, shape=[8, 128])

    def gen_norm_own(v):
        for ti in range(4):
            j = 4 * v + ti
            yield from gen_norm_tile(x_own[j * 128:(j + 1) * 128, :], xts2[j % 2], xn2, ssb2, hT2s[v % 2], ti, pst2)
        for par_ in range(2):
            S.dma("sp", "htw%d" % par_, HTd[v, par_], hT2s[v % 2][par_].ap, reads=[hT2s[v % 2][par_].k], writes=[("HTd", v, par_)])
        yield

    def gen_gr(v):
        hT2 = hT2s[v % 2]
        hkeys = [hT2[0].k, hT2[1].k]
        S.dma("sp", "hol", hown.ap, HOd[:, :, v * 512:(v + 1) * 512], reads=[("HOd", 2 * v), ("HOd", 2 * v + 1)],
              writes=[hown.k])
        for cc in range(8):
            pf = pfm[cc % 2]

            def mmf(e, cc=cc, pf=pf):
                for j in range(16):
                    r = e.matmul(pf.ap, lhsT=w2.ap[:, j, cc * 128:(cc + 1) * 128], rhs=hsel(hT2, j),
                                 start=(j == 0), stop=(j == 15))
                return r
            S.op("pe", mmf, reads=[w2.k] + hkeys, writes=[pf.k])
            yield
            sg = sgt[cc % 2]
            S.op("act", lambda e, pf=pf, sg=sg: e.activation(out=sg.ap, in_=pf.ap, func=AF.Silu),
                 reads=[pf.k], writes=[sg.k])
            yield
            S.op("dve", lambda e, sg=sg, cc=cc: e.tensor_tensor(out=yst.ap[:, cc, :], in0=sg.ap, in1=hown.ap[:, cc, :],
                                                               op=ALU.mult),
                 reads=[sg.k, hown.k], writes=[yst.sub(cc * 1024, 1024)])
            yield
        S.dma("sp", "ycw", YCd[0:8, :, v * 512:(v + 1) * 512].rearrange("c p n -> p c n"), yst.ap,
              reads=[yst.k], writes=[("YCd", 0, v)])
        yield

    def gen_q(v):
        hT2 = hT2s[v % 2]
        hkeys = [hT2[0].k, hT2[1].k]
        for ti in range(4):
            j = 4 * v + ti
            tcol = slice(ti * 128, (ti + 1) * 128)

            def mqc(e, tcol=tcol):
                for jj in range(16):
                    r = e.matmul(pqc.ap, lhsT=hsel(hT2, jj)[:, tcol], rhs=w2.ap[:, jj, 1024:1536], start=(jj == 0), stop=(jj == 15))
                return r
            S.op("pe", mqc, reads=[w2.k] + hkeys, writes=[pqc.k])
            yield
            S.op("act", lambda e: e.activation(out=sqq.ap[:, 0:512], in_=pqc.ap, func=AF.Square, accum_out=qs.ap[:, 0:1]),
                 reads=[pqc.k], writes=[sqq.k, qs.sub(0, 4)])
            S.op("act", lambda e: e.activation(out=qcb.ap, in_=pqc.ap, func=AF.Copy), reads=[pqc.k], writes=[qcb.k])
            yield
            S.op("dve", lambda e: e.tensor_scalar(out=qs.ap[:, 1:2], in0=qs.ap[:, 0:1], scalar1=1.0 / 512, scalar2=EPS,
                                                  op0=ALU.mult, op1=ALU.add), reads=[qs.sub(0, 4)], writes=[qs.sub(4, 4)])
            S.op("dve", lambda e: e.tensor_scalar(out=qs.ap[:, 3:4], in0=qs.ap[:, 1:2], scalar1=EPS, scalar2=None,
                                                  op0=ALU.mult), reads=[qs.sub(4, 4)], writes=[qs.sub(12, 4)])

            def trq(e):
                for ck in range(4):
                    r = e.transpose(out=pqT.ap[:, ck, :], in_=qcb.ap[:, ck * 128:(ck + 1) * 128], identity=identb.ap)
                return r
            S.op("pe", trq, reads=[qcb.k, identb.k], writes=[pqT.k])
            yield
            S.op("dve", lambda e: e.tensor_copy(out=qcT.ap, in_=pqT.ap), reads=[pqT.k], writes=[qcT.k])
            yield
            for n in range(3):
                def muq(e, n=n):
                    for ck in range(4):
                        r = e.matmul(puq[n].ap, lhsT=qcT.ap[:, ck, :], rhs=wuq.ap[:, ck, n * 512:(n + 1) * 512],
                                     start=(ck == 0), stop=(ck == 3))
                    return r
                S.op("pe", muq, reads=[qcT.k, wuq.k], writes=[puq[n].k])
                S.op("act", lambda e, n=n: e.activation(out=Uq.ap[:, n * 512:(n + 1) * 512], in_=puq[n].ap, func=AF.Copy),
                     reads=[puq[n].k], writes=[Uq.sub(n * 2048, 2048)])
                yield
            U3 = Uq.ap.rearrange("p (h c) -> p h c", h=8)
            S3 = sqq.ap.rearrange("p (h c) -> p h c", h=8)
            S.op("dve", lambda e: e.tensor_tensor(out=sqq.ap, in0=Uq.ap, in1=Uq.ap, op=ALU.mult), reads=[Uq.k], writes=[sqq.k])
            yield
            S.op("dve", lambda e, S3=S3: e.tensor_reduce(out=ssqn.ap, in_=S3[:, :, 0:128], axis=AX.X, op=ALU.add),
                 reads=[sqq.k], writes=[ssqn.k])
            S.op("dve", lambda e, S3=S3: e.tensor_reduce(out=ssqr.ap, in_=S3[:, :, 128:192], axis=AX.X, op=ALU.add),
                 reads=[sqq.k], writes=[ssqr.k])
            yield
            S.op("dve", lambda e: e.tensor_scalar(out=ssqn.ap, in0=ssqn.ap, scalar1=1.0 / 128, scalar2=qs.ap[:, 3:4],
                                                  op0=ALU.mult, op1=ALU.add), reads=[ssqn.k, qs.sub(12, 4)], writes=[ssqn.k])
            S.op("dve", lambda e: e.tensor_scalar(out=ssqr.ap, in0=ssqr.ap, scalar1=1.0 / 64, scalar2=qs.ap[:, 3:4],
                                                  op0=ALU.mult, op1=ALU.add), reads=[ssqr.k, qs.sub(12, 4)], writes=[ssqr.k])
            yield
            rsqrt_small(rqn.ap, ssqn.ap, 8, [ssqn.k], [rqn.k])
            rsqrt_small(rqr.ap, ssqr.ap, 8, [ssqr.k], [rqr.k])
            yield
            S.op("dve", lambda e, U3=U3: e.tensor_tensor(out=qn.ap, in0=U3[:, :, 0:128],
                                                         in1=rqn.ap.unsqueeze(2).broadcast_to([128, 8, 128]), op=ALU.mult),
                 reads=[Uq.k, rqn.k], writes=[qn.k])
            qr2 = qrn.ap.rearrange("p h a n -> p h (a n)")
            S.op("dve", lambda e, U3=U3, qr2=qr2: e.tensor_tensor(out=qr2, in0=U3[:, :, 128:192],
                                                                  in1=rqr.ap.unsqueeze(2).broadcast_to([128, 8, 64]), op=ALU.mult),
                 reads=[Uq.k, rqr.k], writes=[qrn.k])
            yield

            def trQ(e):
                for h in range(8):
                    r = e.transpose(out=pQT.ap[:, h, :], in_=qn.ap[:, h, :], identity=identb.ap)
                return r
            S.op("pe", trQ, reads=[qn.k, identb.k], writes=[pQT.k])
            S.op("pool", lambda e, qr2=qr2: e.tensor_tensor(out=qr2, in0=qr2, in1=gqr_bc.unsqueeze(1).broadcast_to([128, 8, 64]),
                                                            op=ALU.mult), reads=[qrn.k, par.k], writes=[qrn.k])
            yield
            S.op("act", lambda e, tcol=tcol: e.activation(out=QTst.ap[:, :, tcol], in_=pQT.ap, func=AF.Identity, scale=gqn),
                 reads=[pQT.k, par.k], writes=[QTst.k])
            cos_t = cosO.ap[:, j, :].unsqueeze(1).unsqueeze(1).broadcast_to([128, 8, 2, 32])
            sin_t = sinO.ap[:, j, :].unsqueeze(1).unsqueeze(1).broadcast_to([128, 8, 2, 32])
            S.op("pool", lambda e, cos_t=cos_t: e.tensor_tensor(out=rqA.ap, in0=qrn.ap, in1=cos_t, op=ALU.mult),
                 reads=[qrn.k, cosO.k], writes=[rqA.k])
            S.op("pool", lambda e, sin_t=sin_t: e.tensor_tensor(out=rqB.ap, in0=qrn.ap, in1=sin_t, op=ALU.mult),
                 reads=[qrn.k, sinO.k], writes=[rqB.k])
            yield
            S.op("pool", lambda e: e.tensor_tensor(out=qpe.ap[:, :, 0, :], in0=rqA.ap[:, :, 0, :], in1=rqB.ap[:, :, 1, :],
                                                   op=ALU.subtract), reads=[rqA.k, rqB.k], writes=[qpe.k])
            S.op("pool", lambda e: e.tensor_tensor(out=qpe.ap[:, :, 1, :], in0=rqB.ap[:, :, 0, :], in1=rqA.ap[:, :, 1, :],
                                                   op=ALU.add), reads=[rqA.k, rqB.k], writes=[qpe.k])
            yield
            qpf = qpe.ap.rearrange("p h a n -> p (h a n)")

            def trR(e, qpf=qpf):
                for i in range(4):
                    r = e.transpose(out=pqrT.ap[:, i, :], in_=qpf[:, i * 128:(i + 1) * 128], identity=identb.ap)
                return r
            S.op("pe", trR, reads=[qpe.k, identb.k], writes=[pqrT.k])
            yield
            S.op("dve", lambda e, tcol=tcol: e.tensor_copy(out=QRst.ap[:, :, tcol], in_=pqrT.ap),
                 reads=[pqrT.k], writes=[QRst.k])
            yield
        S.dma("sp", "qtw", QTd[:, :, v * 512:(v + 1) * 512].rearrange("h p n -> p h n"), QTst.ap,
              reads=[QTst.k], writes=[("QTd", v)])
        S.dma("sp", "qrw", QRd[:, :, v * 512:(v + 1) * 512].rearrange("(i hh) d n -> (hh d) i n", hh=2), QRst.ap,
              reads=[QRst.k], writes=[("QRd", v)])
        yield

    interleave([gen_norm_own(0)])
    for v in range(NSUPO):
        interleave([gen_gr(v), gen_q(v), gen_norm_own(v + 1) if v + 1 < NSUPO else None])

    if STOP == 2:
        return finish()

    state["ptr"] = PERS_END
    w2b = sb([16, 1024], BF16)
    hT3 = [(sb([8, 512], BF16), sb([8, 512], BF16)), (sb([8, 512], BF16), sb([8, 512], BF16))]
    sgst = [sb([8, 512], BF16), sb([8, 512], BF16)]
    S.dma("pool", "w2b", w2b.ap, w_in_v[:, :, 3136:4160], writes=[w2b.k])
    pfb = [ps(i) for i in range(4)]

    def ht3_loads(v):
        hb = hT3[v % 2]
        for par_ in range(2):
            S.dma("sp", "htr%d%d" % (v % 2, par_), hb[par_].ap, HTd[v, par_], reads=[("HTd", v, par_)], writes=[hb[par_].k])
    ht3_loads(0)
    for v in range(NSUPO):
        hb3 = hT3[v % 2]
        if v + 1 < NSUPO:
            ht3_loads(v + 1)
        sgb = sgst[v % 2]
        for cc in range(8):
            pf = pfb[cc % 4]

            def mmg(e, cc=cc, pf=pf, hb3=hb3):
                for j in range(16):
                    r = e.matmul(pf.ap, lhsT=w2b.ap[:, j, cc * 128:(cc + 1) * 128], rhs=hsel(hb3, j),
                                 start=(j == 0), stop=(j == 15))
                return r
            S.op("pe", mmg, reads=[w2b.k, hb3[0].k, hb3[1].k], writes=[pf.k])
            S.op("act", lambda e, pf=pf, cc=cc, sgb=sgb: e.activation(out=sgb.ap[:, cc, :], in_=pf.ap, func=AF.Silu),
                 reads=[pf.k], writes=[sgb.sub(cc * 1024, 1024)])
        S.dma("sp", "sgw%d" % (v % 2), SGd[:, :, v * 512:(v + 1) * 512].rearrange("c p n -> p c n"), sgb.ap,
              reads=[sgb.k], writes=[("SGd", v)])

    if STOP == 3:
        return finish()

    state["ptr"] = ARENA_BYTES - 65536
    wout = sb([16, 2048], BF16)
    w_out_v = w_out.rearrange("(c p) n -> p c n", p=128)
    S.dma("pool", "wout", wout.ap[:, 0:8, :], w_out_v[:, 0:8, :], writes=[wout.k])
    S.dma("pool", "wout", wout.ap[:, 8:16, :], w_out_v[:, 8:16, :], writes=[wout.k], cont=True)
    state["ptr"] = PERS_END
    KTh = [sb([SEQ], BF16), sb([SEQ], BF16)]
    Vh = [sb([NT, 128], BF16), sb([NT, 128], BF16)]
    QN = [sb([2048], BF16), sb([2048], BF16)]
    QR = [sb([2048], BF16), sb([2048], BF16)]
    SG = [sb([2048], BF16), sb([2048], BF16)]
    PT = [sb([512], BF16) for _ in range(8)]
    rden = sb([512])
    o1 = sb([512])
    yast = [sb([512], BF16), sb([512], BF16)]
    acc = [sb([512]), sb([512])]
    acc_hi = sb([512], BF16)
    acc_lo = sb([512], BF16)
    pS = [ps(i) for i in range(4)]
    pO = [ps(4), ps(5)]
    pD = [ps(6), ps(7)]
    for sl_ in range(2):
        S.op("pool", lambda e, sl_=sl_: e.memset(QR[sl_].ap[64:128, :], 0.0), writes=[QR[sl_].k])
    all_kv_keys = [("KTd", u) for u in range(NSUP)] + [("Vd", u) for u in range(NSUP)]
    all_q_keys = [("QTd", v) for v in range(NSUPO)] + [("QRd", v) for v in range(NSUPO)] + [("SGd", v) for v in range(NSUPO)]
    kpe_keys = [("kpeT", t) for t in range(NT)]

    def head_loads(h):
        sl = h % 2
        S.dma("sp", "kth%d" % sl, KTh[sl].ap, KTd[h], reads=all_kv_keys, writes=[KTh[sl].k])
        S.dma("sp", "vh%d" % sl, Vh[sl].ap, Vd[h], reads=all_kv_keys, writes=[Vh[sl].k])
        S.dma("sp", "qn%d" % sl, QN[sl].ap, QTd[h], reads=all_q_keys, writes=[QN[sl].k])
        S.dma("sp", "qr%d" % sl, QR[sl].ap[0:64, :], QRd[h], reads=all_q_keys + [QR[sl].k], writes=[("qrlo", sl)])
        S.dma("sp", "sg%d" % sl, SG[sl].ap, SGd[h], reads=all_q_keys, writes=[SG[sl].k])

    blk = 0
    head_loads(0)
    for h in range(8):
        sl = h % 2
        for qb in range(4):
            items = []
            for kt in range(8 * qb):
                items.append((kt, 0, None))
            for i in range(4):
                for e_ in range(2):
                    items.append((8 * qb + 2 * i + e_, i * 128, maskA if e_ == 0 else maskB))
            n_it = len(items)
            po = pO[blk % 2]
            pd = pD[blk % 2]
            blk += 1
            LAG = 2

            def qk(idx, kt, c0, sl=sl, qb=qb):
                N = 512 - c0
                psb = pS[idx % 4]
                pt = PT[idx % 8]
                q0 = qb * 512 + c0

                def f(e):
                    e.matmul(psb.ap[:, 0:N], lhsT=KTh[sl].ap[:, kt * 128:(kt + 1) * 128], rhs=QN[sl].ap[:, q0:q0 + N],
                             start=True, stop=False)
                    return e.matmul(psb.ap[:, 0:N], lhsT=KpeT.ap[:, kt * 128:(kt + 1) * 128],
                                    rhs=QR[sl].ap[:, q0:q0 + N], start=False, stop=True)
                S.op("pe", f, reads=[KTh[sl].k, QN[sl].k, QR[sl].k, ("qrlo", sl), "kpeT_pad"] + kpe_keys, writes=[psb.k])
                S.op("act", lambda e: e.activation(out=pt.ap[:, 0:N], in_=psb.ap[:, 0:N], func=AF.Exp, scale=SM_SCALE),
                     reads=[psb.k], writes=[pt.k])

            def pv(idx, kt, c0, mask, sl=sl, po=po, pd=pd, n_it=n_it, blkpar=(blk % 2)):
                N = 512 - c0
                pt = PT[idx % 8]
                if mask is not None:
                    S.op("pool", lambda e: e.tensor_tensor(out=pt.ap[:, 0:128], in0=pt.ap[:, 0:128], in1=mask.ap, op=ALU.mult),
                         reads=[pt.k, mask.k], writes=[pt.k])
                first = (idx == 0)
                last = (idx == n_it - 1)
                ac = acc[blkpar]
                if first:
                    S.op("dve", lambda e: e.tensor_copy(out=ac.ap, in_=pt.ap), reads=[pt.k], writes=[ac.k])
                else:
                    S.op("dve", lambda e: e.tensor_tensor(out=ac.ap[:, c0:512], in0=ac.ap[:, c0:512], in1=pt.ap[:, 0:N], op=ALU.add),
                         reads=[pt.k, ac.k], writes=[ac.k])

                def f(e):
                    return e.matmul(po.ap[:, c0:512], lhsT=Vh[sl].ap[:, kt, :], rhs=pt.ap[:, 0:N], start=first, stop=last,
                                    skip_group_check=True)
                S.op("pe", f, reads=[Vh[sl].k, pt.k], writes=[po.k])
                if last:
                    S.op("dve", lambda e: e.tensor_copy(out=acc_hi.ap, in_=ac.ap), reads=[ac.k], writes=[acc_hi.k])
                    S.op("dve", lambda e: e.tensor_tensor(out=acc_lo.ap, in0=ac.ap, in1=acc_hi.ap, op=ALU.subtract),
                         reads=[ac.k, acc_hi.k], writes=[acc_lo.k])

                    def fd(e):
                        e.matmul(pd.ap, lhsT=onesb.ap, rhs=acc_hi.ap, start=True, stop=False)
                        return e.matmul(pd.ap, lhsT=onesb.ap, rhs=acc_lo.ap, start=False, stop=True)
                    S.op("pe", fd, reads=[acc_hi.k, acc_lo.k, onesb.k], writes=[pd.k])

            for idx in range(n_it + LAG):
                if idx < n_it:
                    qk(idx, items[idx][0], items[idx][1])
                if idx - LAG >= 0:
                    it = items[idx - LAG]
                    pv(idx - LAG, it[0], it[1], it[2])
            ya = yast[qb % 2]
            S.op("dve", lambda e, pd=pd: e.reciprocal(out=rden.ap, in_=pd.ap), reads=[pd.k], writes=[rden.k])
            S.op("dve", lambda e, po=po: e.tensor_tensor(out=o1.ap, in0=po.ap, in1=rden.ap, op=ALU.mult),
                 reads=[po.k, rden.k], writes=[o1.k])
            S.op("pool", lambda e, ya=ya, qb=qb, sl=sl: e.tensor_tensor(out=ya.ap, in0=o1.ap,
                                                                       in1=SG[sl].ap[:, qb * 512:(qb + 1) * 512], op=ALU.mult),
                 reads=[o1.k, SG[sl].k], writes=[ya.k])
            S.dma("sp", "yaw%d" % (qb % 2), YCd[8 + h, :, qb * 512:(qb + 1) * 512], ya.ap, reads=[ya.k],
                  writes=[("YCd", 8 + h, qb)])
            if qb == 0 and h + 1 < 8:
                head_loads(h + 1)

    if STOP == 4:
        return finish()

    state["ptr"] = PERS_END
    gate_bc = sb([2048])
    S.dma("sp", "gbr", gate_bc.ap, GBd, reads=["GBd"], writes=[gate_bc.k])
    ycT = [sb([16, 512], BF16), sb([16, 512], BF16)]
    xo = [sb([2048]), sb([2048])]
    ot = [sb([2048]), sb([2048])]
    pout = [ps(i) for i in range(8)]
    yc_keys = [("YCd", 0, v) for v in range(NSUPO)] + [("YCd", 8 + h, qb) for h in range(8) for qb in range(4)]
    out_keys = []
    def yc_load(v):
        S.dma("sp", "ycl%d" % (v % 2), ycT[v % 2].ap, YCd[:, :, v * 512:(v + 1) * 512].rearrange("c p n -> p c n"),
              reads=yc_keys, writes=[ycT[v % 2].k])

    def xo_load(j):
        S.dma("sp", "xo%d" % (j % 2), xo[j % 2].ap, x_own[j * 128:(j + 1) * 128, :], writes=[xo[j % 2].k])
    yc_load(0)
    xo_load(0)
    for v in range(NSUPO):
        yb = ycT[v % 2]
        if v + 1 < NSUPO:
            yc_load(v + 1)
        for ti in range(4):
            j = 4 * v + ti
            xb = xo[j % 2]
            ob = ot[j % 2]
            if j + 1 < NTO:
                xo_load(j + 1)
            for n in range(4):
                pb = pout[(j % 2) * 4 + n]

                def mo(e, n=n, pb=pb, ti=ti, yb=yb):
                    for c in range(16):
                        r = e.matmul(pb.ap, lhsT=yb.ap[:, c, ti * 128:(ti + 1) * 128], rhs=wout.ap[:, c, n * 512:(n + 1) * 512],
                                     start=(c == 0), stop=(c == 15))
                    return r
                S.op("pe", mo, reads=[yb.k, wout.k], writes=[pb.k])
                cs = slice(n * 512, (n + 1) * 512)
                S.op("dve", lambda e, pb=pb, ob=ob, cs=cs: e.tensor_tensor(out=ob.ap[:, cs], in0=pb.ap, in1=gate_bc.ap[:, cs],
                                                                          op=ALU.mult),
                     reads=[pb.k, gate_bc.k], writes=[ob.sub(n * 2048, 2048)])
                S.op("pool", lambda e, ob=ob, xb=xb, cs=cs: e.tensor_tensor(out=ob.ap[:, cs], in0=ob.ap[:, cs], in1=xb.ap[:, cs],
                                                                           op=ALU.add),
                     reads=[ob.sub(n * 2048, 2048), xb.k], writes=[ob.sub(n * 2048, 2048)])
            S.dma("sp", "ow%d" % (j % 2), out_d[j * 128:(j + 1) * 128, :], ob.ap,
                  reads=[ob.k], writes=[("out", j)])
            out_keys.append(("out", j))
    return finish()


_NC_CACHE = {}


def _own_tiles(s):
    return [2 * j + s for j in range(NTO)]


def _prep(x, c, positions, w_ada, b_ada, w_in, conv_w, conv_b, w_rg_a, b_rg_a, w_rg_x, b_rg_x, lru_lambda,
           q_a_norm, w_uq, kv_a_norm, w_ukv, q_norm_nope, q_norm_rope, k_norm_nope, k_norm_rope, w_out):
    f32 = np.float32
    x = np.asarray(x, f32)
    c = np.asarray(c, f32)
    positions = np.asarray(positions, np.int32)
    g = lambda a: np.ascontiguousarray(np.asarray(a, f32)[0])
    w_ada_, b_ada_, w_in_ = g(w_ada), g(b_ada), g(w_in)
    conv_w_, conv_b_ = g(conv_w), g(conv_b)
    w_rg_a_, b_rg_a_, w_rg_x_, b_rg_x_, lam_ = g(w_rg_a), g(b_rg_a), g(w_rg_x), g(b_rg_x), g(lru_lambda)
    q_a_, w_uq_, kv_a_, w_ukv_ = g(q_a_norm), g(w_uq), g(kv_a_norm), g(w_ukv)
    qnn_, qnr_, knn_, knr_, w_out_ = g(q_norm_nope), g(q_norm_rope), g(k_norm_nope), g(k_norm_rope), g(w_out)

    ident = np.eye(128, dtype=f32)
    tri = (np.arange(128)[:, None] <= np.arange(128)[None, :]).astype(f32)
    ones = np.ones((128, 128), f32)
    zeros = np.zeros((128, 128), f32)
    inv_freq = (1.0 / (10000.0 ** (np.arange(0, 64, 2, dtype=np.float32) / 64.0))).astype(f32)

    chan = lambda v: np.ascontiguousarray(v.reshape(8, 128).T)
    rnn = np.stack([chan(conv_w_[0]), chan(conv_w_[1]), chan(conv_w_[2]), chan(conv_w_[3]), chan(conv_b_),
                    chan(b_rg_a_), chan(b_rg_x_), chan(lam_)], axis=2)
    b_ada_rep = np.ascontiguousarray(np.broadcast_to(b_ada_[None, :], (128, 3 * D)))

    in_maps = []
    for core in range(8):
        b, s = core // 2, core % 2
        own = _own_tiles(s)
        xb = x[b]
        x_own = np.ascontiguousarray(xb.reshape(NT, 128, D)[own].reshape(NTO * 128, D))
        par = np.zeros((128, 256), f32)
        par[:, 0:16] = c[b].reshape(128, 16)
        par[:, 16:80] = rnn.reshape(128, 64)
        par[:, 80:84] = q_a_.reshape(4, 128).T
        par[:, 84:88] = kv_a_.reshape(4, 128).T
        par[:, 88] = qnn_
        par[:, 89] = knn_
        par[:, 90] = 1.0 if s == 0 else 0.0
        par[:, 91] = 0.0 if s == 0 else 1.0
        par[:, 92:156] = qnr_[None, :]
        par[:, 156:220] = knr_[None, :]
        par[:, 220:252] = inv_freq[None, :]
        cst = np.concatenate([ident, tri if s == 0 else ones, zeros if s == 0 else tri], axis=1)
        pos_t = positions[b].reshape(NT, 128).T
        posi = np.ascontiguousarray(np.concatenate([pos_t, pos_t[:, own]], axis=1)).astype(np.int32)
        in_maps.append(dict(
            x_all=np.ascontiguousarray(xb), x_own=x_own, par=par, cst=np.ascontiguousarray(cst), posi=posi,
            w_ada=w_ada_, b_ada=b_ada_rep, w_in=w_in_, w_rg_a=w_rg_a_, w_rg_x=w_rg_x_, w_uq=w_uq_, w_ukv=w_ukv_,
            w_out=w_out_))
    return in_maps


def kernel(**inputs):
    f32 = np.float32
    in_maps = _prep(**inputs)
    if "nc" not in _NC_CACHE:
        _NC_CACHE["nc"] = build_program()
    nc = _NC_CACHE["nc"]
    res = run_bass_kernel_spmd(nc, in_maps, core_ids=list(range(8)))
    out = np.empty((4, SEQ, D), f32)
    for core in range(8):
        b, s = core // 2, core % 2
        o = np.asarray(res.results[core]["out"], f32).reshape(NTO, 128, D)
        out[b].reshape(NT, 128, D)[_own_tiles(s)] = o
    return out
```

```python
import bisect
import contextlib
import math
import numpy as np
import concourse.bass as bass
import concourse.mybir as mybir
from concourse.bass_utils import run_bass_kernel_spmd

F32 = mybir.dt.float32
BF16 = mybir.dt.bfloat16
I32 = mybir.dt.int32
AF = mybir.ActivationFunctionType
ALU = mybir.AluOpType
AX = mybir.AxisListType

D = 2048
SEQ = 4096
NT = 32
NTO = 16
NSUP = 8
NSUPO = 4
EPS = 1e-6
SM_SCALE = 192.0 ** -0.5
USE_POW = True
STOP = 99


class _Op:
    __slots__ = ("eng", "fn", "deps", "needs_inc", "cum", "dma_sem", "is_dma", "dma_grp", "tfin")


class _Rng:
    __slots__ = ("lo", "hi", "last_w", "readers")


class Sched:
    PAGE = 1024

    def __init__(self, nc):
        self.nc = nc
        self.ops = {e: [] for e in ("pe", "act", "dve", "pool", "sp")}
        self.named = {}
        self.ranges = {}
        self.pages = {}
        self.streams = {}
        self.n_streams = 0
        self.eng_free = {e: 0.0 for e in ("pe", "act", "dve", "pool", "sp")}
        self.step_fin = 0.0

    DEF_COST = {"pe": 1.0, "act": 0.65, "dve": 0.55, "pool": 0.8, "sp": 0.1}
    HOP = 0.4

    def _resolve(self, k):
        if not (isinstance(k, tuple) and len(k) == 3 and k[0] in ("sb", "ps")):
            r = self.named.get(k)
            if r is None:
                r = _Rng()
                r.lo = r.hi = 0
                r.last_w = None
                r.readers = []
                self.named[k] = r
            return [r]
        space, lo, hi = k
        sm = self.ranges.get(space)
        if sm is None:
            r = _Rng()
            r.last_w, r.readers = None, []
            sm = ([0, 1 << 30], [r])
            self.ranges[space] = sm
        bounds, segs = sm
        for x in (lo, hi):
            i = bisect.bisect_right(bounds, x) - 1
            if bounds[i] != x:
                o = segs[i]
                n = _Rng()
                n.last_w, n.readers = o.last_w, list(o.readers)
                bounds.insert(i + 1, x)
                segs.insert(i + 1, n)
        i = bisect.bisect_left(bounds, lo)
        j = bisect.bisect_left(bounds, hi)
        return segs[i:j]

    def _add(self, eng, fn, reads, writes, c=None, lat=0.0):
        op = _Op()
        op.eng, op.fn, op.deps = eng, fn, []
        op.needs_inc = False
        op.is_dma = False
        op.dma_sem = None
        op.dma_grp = None
        rr = []
        ww = []
        for k in reads:
            if isinstance(k, tuple) and len(k) == 3 and k[0] == "ps":
                ww.extend(self._resolve(k))
            else:
                rr.extend(self._resolve(k))
        for k in writes:
            ww.extend(self._resolve(k))
        deps = []
        for r in rr + ww:
            if r.last_w is not None:
                deps.append(r.last_w)
        for r in ww:
            deps.extend(r.readers)
        seen = set()
        for d in deps:
            if d is op or id(d) in seen:
                continue
            seen.add(id(d))
            if (not d.is_dma) and d.eng == "pe" and eng == "pe":
                continue
            op.deps.append(d)
            if not d.is_dma:
                d.needs_inc = True
        for r in ww:
            r.last_w = op
            r.readers = []
        wset = set(id(r) for r in ww)
        for r in rr:
            if id(r) not in wset:
                r.readers.append(op)
        self.ops[eng].append(op)
        ready = 0.0
        for d in op.deps:
            if d.tfin > ready:
                ready = d.tfin
        if op.deps:
            ready += self.HOP
        start = max(ready, self.eng_free[eng])
        cost = self.DEF_COST[eng] if c is None else c
        self.eng_free[eng] = start + cost
        op.tfin = start + cost + lat
        if op.tfin > self.step_fin:
            self.step_fin = op.tfin
        return op

    def op(self, eng, fn, reads=(), writes=(), c=None):
        return self._add(eng, fn, reads, writes, c=c)

    def dma(self, queue, stream, out, in_, reads=(), writes=(), cont=False, **kw):
        st = self.streams.get(stream)
        if st is None:
            st = dict(idx=self.n_streams, total=0, last=None, group_prev=None, grp=None)
            self.n_streams += 1
            self.streams[stream] = st

        def fn(e, out=out, in_=in_, kw=kw):
            return e.dma_start(out=out, in_=in_, **kw)

        op = self._add(queue, fn, reads, writes, c=0.1, lat=2.5)
        op.is_dma = True
        op.dma_sem = st["idx"]
        st["total"] += 16
        if cont and st["grp"] is not None:
            prev_group = st["group_prev"]
        else:
            prev_group = st["last"]
            st["group_prev"] = prev_group
            st["grp"] = [0]
        op.dma_grp = st["grp"]
        op.dma_grp[0] = st["total"]
        if prev_group is not None and all(d is not prev_group for d in op.deps):
            op.deps.append(prev_group)
        st["last"] = op
        return op

    def final_wait(self):
        op = self._add("sp", lambda e: None, (), ())
        for st in self.streams.values():
            if st["last"] is not None:
                op.deps.append(st["last"])
        for e in ("pe", "act", "dve", "pool"):
            if self.ops[e]:
                last = self.ops[e][-1]
                if not last.is_dma:
                    last.needs_inc = True
                    op.deps.append(last)

    def emit(self):
        nc = self.nc
        with contextlib.ExitStack() as es:
            esem = {e: es.enter_context(nc.semaphore("s_" + e)) for e in ("pe", "act", "dve", "pool")}
            dsem = [es.enter_context(nc.semaphore("d_%d" % i)) for i in range(self.n_streams)]
            for e in self.ops:
                c = 0
                for op in self.ops[e]:
                    if op.is_dma:
                        continue
                    if op.needs_inc:
                        c += 1
                    op.cum = c
            block = es.enter_context(nc.Block())

            def run(ename, e):
                waited = {}
                for op in self.ops[ename]:
                    for d in op.deps:
                        if d.is_dma and op.is_dma and d.dma_grp is op.dma_grp:
                            continue
                        if d.is_dma:
                            key, val, sem = ("d", d.dma_sem), d.dma_grp[0], dsem[d.dma_sem]
                        else:
                            key, val, sem = ("e", d.eng), d.cum, esem[d.eng]
                        if waited.get(key, 0) >= val:
                            continue
                        waited[key] = val
                        e.wait_ge(sem, val)
                    ins = op.fn(e)
                    if ins is None:
                        assert not op.needs_inc and not op.is_dma
                        continue
                    if op.is_dma:
                        ins.then_inc(dsem[op.dma_sem], 16)
                    elif op.needs_inc:
                        ins.then_inc(esem[op.eng], 1)

            @block.sync
            def _(e):
                run("sp", e)

            @block.tensor
            def _(e):
                run("pe", e)

            @block.scalar
            def _(e):
                run("act", e)

            @block.vector
            def _(e):
                run("dve", e)

            @block.gpsimd
            def _(e):
                run("pool", e)


class Buf:
    def __init__(self, ap, key):
        self.ap = ap
        self.k = key

    def sub(self, off, n):
        return (self.k[0], self.k[1] + off, self.k[1] + off + n)


def build_program():
    nc = bass.Bass("TRN2", target_bir_lowering=False)

    def din(name, shape, dt=F32):
        return nc.dram_tensor(name, list(shape), dt, kind="ExternalInput").ap()

    def dscr(name, shape, dt=BF16):
        return nc.dram_tensor(name, list(shape), dt, kind="Internal").ap()

    x_all = din("x_all", [SEQ, D])
    x_own = din("x_own", [NTO * 128, D])
    par_d = din("par", [128, 256])
    cst_d = din("cst", [128, 384])
    posi_d = din("posi", [128, 48], I32)
    w_ada = din("w_ada", [D, 3 * D])
    b_ada = din("b_ada", [128, 3 * D])
    w_in = din("w_in", [D, 4160])
    w_rg_a = din("w_rg_a", [16, 64, 64])
    w_rg_x = din("w_rg_x", [16, 64, 64])
    w_uq = din("w_uq", [512, 1536])
    w_ukv = din("w_ukv", [512, 2048])
    w_out = din("w_out", [D, D])
    out_d = nc.dram_tensor("out", [NTO * 128, D], F32, kind="ExternalOutput").ap()

    KTd = dscr("KTd", [8, 128, SEQ])
    Vd = dscr("Vd", [8, 128, NT, 128])
    QTd = dscr("QTd", [8, 128, NTO * 128])
    QRd = dscr("QRd", [8, 64, NTO * 128])
    SGd = dscr("SGd", [8, 128, NTO * 128])
    HOd = dscr("HOd", [128, 8, NTO * 128])
    YCd = dscr("YCd", [16, 128, NTO * 128])
    GBd = dscr("GBd", [128, 2048], F32)

    ARENA_BYTES = 207 * 1024
    arena = nc.alloc_sbuf_tensor("arena", [128, ARENA_BYTES // 4], F32)
    A = arena.ap() if hasattr(arena, "ap") else arena[:]
    banks = []
    for i in range(8):
        t = nc.alloc_psum_tensor("bank%d" % i, [128, 512], F32)
        banks.append(t.ap() if hasattr(t, "ap") else t[:])

    S = Sched(nc)

    def finish():
        S.final_wait()
        S.emit()
        return nc

    state = dict(ptr=0)

    def sb(shape, dt=F32, parts=128):
        n = 1
        for d_ in shape:
            n *= d_
        esz = 4 if dt in (F32, I32) else 2
        nbytes = (n * esz + 31) // 32 * 32
        lo = state["ptr"]
        hi = lo + nbytes
        assert hi <= ARENA_BYTES, ("SBUF overflow", hi)
        state["ptr"] = hi
        ap = A[:, lo // 4: lo // 4 + (n * esz + 3) // 4]
        if dt != F32:
            ap = ap.bitcast(dt)
            if esz == 2 and (n * esz) % 4:
                ap = ap[:, 0:n]
        if len(shape) == 2:
            ap = ap.rearrange("p (a b) -> p a b", a=shape[0])
        elif len(shape) == 3:
            ap = ap.rearrange("p (a b c) -> p a b c", a=shape[0], b=shape[1])
        return Buf(ap, ("sb", lo, hi))

    def ps(bank, nbanks=1, dt=F32, shape=None, off=0, nbytes=None):
        lo = bank * 2048 + off
        if nbytes is None:
            nbytes = nbanks * 2048 - off
        if nbanks == 1:
            ap = banks[bank]
            ap = ap[:, off // 4: (off + nbytes) // 4]
        else:
            raise AssertionError
        if dt != F32:
            ap = ap.bitcast(dt)
        if shape is not None and len(shape) == 2:
            ap = ap.rearrange("p (a b) -> p a b", a=shape[0])
        return Buf(ap, ("ps", bank * 2048, bank * 2048 + 2048))

    par = sb([256])
    cstf = sb([384])
    posi = sb([48], I32)
    identb = sb([128], BF16)
    maskA = sb([128], BF16)
    maskB = sb([128], BF16)
    onesb = sb([128], BF16)
    neghalf = sb([16])
    onesf = sb([8])
    hb_a = sb([8])
    hb_x = sb([8])
    scale1 = sb([16])
    shiftp = sb([16])
    cl_half = sb([8])
    cl_one = sb([8])
    KpeT = sb([SEQ], BF16)
    cosA = sb([32, 32])
    sinA = sb([32, 32])
    cosO = sb([16, 32])
    sinO = sb([16, 32])
    hist = sb([8, 3])
    hstate = sb([8])
    PERS_END = state["ptr"]

    P_ = par.ap
    c_par = P_[:, 0:16]

    def rnn_par(kidx):
        return P_[:, 16:80].rearrange("p (c k) -> p c k", k=8)[:, :, kidx]

    gq_a = P_[:, 80:84]
    gkv_a = P_[:, 84:88]
    gqn = P_[:, 88:89]
    gkn = P_[:, 89:90]
    sel0 = P_[:, 90:91]
    sel1 = P_[:, 91:92]
    gqr_bc = P_[:, 92:156]
    gkr_bc = P_[:, 156:220]
    invf = P_[:, 220:252]
    identf = cstf.ap[:, 0:128]

    S.dma("sp", "ld0", par.ap, par_d, writes=[par.k])
    S.dma("sp", "ld0", cstf.ap, cst_d, writes=[cstf.k], cont=True)
    S.dma("sp", "ld0", posi.ap, posi_d, writes=[posi.k], cont=True)
    S.op("dve", lambda e: e.tensor_copy(out=identb.ap, in_=cstf.ap[:, 0:128]), reads=[cstf.k], writes=[identb.k])
    S.op("dve", lambda e: e.tensor_copy(out=maskA.ap, in_=cstf.ap[:, 128:256]), reads=[cstf.k], writes=[maskA.k])
    S.op("dve", lambda e: e.tensor_copy(out=maskB.ap, in_=cstf.ap[:, 256:384]), reads=[cstf.k], writes=[maskB.k])
    S.op("pool", lambda e: e.memset(onesb.ap, 1.0), writes=[onesb.k])
    S.op("pool", lambda e: e.memset(neghalf.ap, -0.5), writes=[neghalf.k])
    S.op("pool", lambda e: e.memset(onesf.ap, 1.0), writes=[onesf.k])
    S.op("pool", lambda e: e.memset(hist.ap, 0.0), writes=[hist.k])
    S.op("pool", lambda e: e.memset(hstate.ap, 0.0), writes=[hstate.k])
    S.op("pool", lambda e: e.memset(KpeT.ap[64:128, :], 0.0), writes=["kpeT_pad"])

    def rsqrt_small(out_ap, in_ap, n, rk, wk):
        if USE_POW:
            S.op("pool", lambda e: e.tensor_tensor(out=out_ap, in0=in_ap, in1=neghalf.ap[:, 0:n], op=ALU.pow),
                 reads=list(rk) + [neghalf.k], writes=wk)
        else:
            S.op("act", lambda e: e.activation(out=out_ap, in_=in_ap, func=AF.Sqrt), reads=rk, writes=wk)
            S.op("dve", lambda e: e.reciprocal(out=out_ap, in_=out_ap), reads=wk, writes=wk)

    state["ptr"] = PERS_END + 71680
    cact = sb([16])
    cbc = sb([16, 128], BF16)
    tmpe = sb([8])
    wad = [sb([16, 512], BF16), sb([16, 512], BF16)]
    bad = [sb([512]), sb([512])]
    modbc = sb([4096])
    gate_tmp = sb([2048])
    dtmp = sb([16, 128])
    angt = sb([32, 32])
    angk = sb([32, 32], I32)
    angf = sb([32, 32])
    posf = sb([48])

    lam = rnn_par(7)
    S.op("act", lambda e: e.activation(out=tmpe.ap, in_=lam, func=AF.Exp, scale=-1.0), reads=[par.k], writes=[tmpe.k])
    S.op("act", lambda e: e.activation(out=tmpe.ap, in_=tmpe.ap, func=AF.Ln, bias=onesf.ap[:, 0:1]), reads=[tmpe.k, onesf.k], writes=[tmpe.k])
    S.op("dve", lambda e: e.tensor_scalar(out=hb_a.ap, in0=rnn_par(5), scalar1=0.5, scalar2=None, op0=ALU.mult),
         reads=[par.k], writes=[hb_a.k])
    S.op("dve", lambda e: e.tensor_scalar(out=hb_x.ap, in0=rnn_par(6), scalar1=0.5, scalar2=None, op0=ALU.mult),
         reads=[par.k], writes=[hb_x.k])
    S.op("dve", lambda e: e.tensor_scalar(out=cl_one.ap, in0=tmpe.ap, scalar1=-8.0, scalar2=None, op0=ALU.mult),
         reads=[tmpe.k], writes=[cl_one.k])
    S.op("dve", lambda e: e.tensor_scalar(out=cl_half.ap, in0=tmpe.ap, scalar1=-4.0, scalar2=None, op0=ALU.mult),
         reads=[tmpe.k], writes=[cl_half.k])
    S.op("act", lambda e: e.activation(out=cact.ap, in_=c_par, func=AF.Silu), reads=[par.k], writes=[cact.k])
    S.op("dve", lambda e: e.tensor_copy(out=cbc.ap, in_=cact.ap.unsqueeze(2).broadcast_to([128, 16, 128])),
         reads=[cact.k], writes=[cbc.k])

    def rope_tables(pos_ap, n, cos_b, sin_b):
        pf = posf.ap[:, 0:n]
        S.op("dve", lambda e: e.tensor_copy(out=pf, in_=pos_ap), reads=[posi.k], writes=[posf.k])
        a3 = angt.ap[:, 0:n, :]
        k3 = angk.ap[:, 0:n, :]
        f3 = angf.ap[:, 0:n, :]
        S.op("dve", lambda e: e.tensor_tensor(out=a3, in0=pf.unsqueeze(2).broadcast_to([128, n, 32]),
                                              in1=invf.unsqueeze(1).broadcast_to([128, n, 32]), op=ALU.mult),
             reads=[posf.k, par.k], writes=[angt.k])
        inv2pi = 1.0 / (2.0 * math.pi)
        for (dst, shift) in ((sin_b, 0.0), (cos_b, 0.25)):
            S.op("dve", lambda e, shift=shift: e.tensor_scalar(out=f3, in0=a3, scalar1=inv2pi, scalar2=shift,
                                                              op0=ALU.mult, op1=ALU.add),
                 reads=[angt.k], writes=[angf.k])
            S.op("dve", lambda e: e.tensor_copy(out=k3, in_=f3), reads=[angf.k], writes=[angk.k])
            S.op("dve", lambda e, dst=dst: e.tensor_copy(out=dst.ap, in_=k3), reads=[angk.k], writes=[dst.k])
            S.op("dve", lambda e, dst=dst: e.tensor_tensor(out=dst.ap, in0=f3, in1=dst.ap, op=ALU.subtract),
                 reads=[angf.k, dst.k], writes=[dst.k])
            S.op("act", lambda e, dst=dst: e.activation(out=dst.ap, in_=dst.ap, func=AF.Sin, scale=6.283185),
                 reads=[dst.k], writes=[dst.k])

    rope_tables(posi.ap[:, 0:32], 32, cosA, sinA)
    rope_tables(posi.ap[:, 32:48], 16, cosO, sinO)

    w_ada_v = w_ada.rearrange("(p j) n -> p j n", j=16)
    pmod = [ps(0), ps(1)]
    for nci in range(12):
        sl = nci % 2
        S.dma("pool", "wad%d" % sl, wad[sl].ap, w_ada_v[:, :, nci * 512:(nci + 1) * 512], writes=[wad[sl].k])
        S.dma("sp", "bad%d" % sl, bad[sl].ap, b_ada[:, nci * 512:(nci + 1) * 512], writes=[bad[sl].k])

        def mm(e, sl=sl):
            for j in range(16):
                r = e.matmul(pmod[sl].ap, lhsT=cbc.ap[:, j, :], rhs=wad[sl].ap[:, j, :], start=(j == 0), stop=(j == 15))
            return r
        S.op("pe", mm, c=4.8, reads=[cbc.k, wad[sl].k], writes=[pmod[sl].k])
        if nci < 8:
            dst, dk = modbc.ap[:, nci * 512:(nci + 1) * 512], modbc.k
        else:
            dst, dk = gate_tmp.ap[:, (nci - 8) * 512:(nci - 7) * 512], gate_tmp.k
        S.op("dve", lambda e, sl=sl, dst=dst: e.tensor_tensor(out=dst, in0=pmod[sl].ap, in1=bad[sl].ap, op=ALU.add),
             reads=[pmod[sl].k, bad[sl].k], writes=[dk])
    S.dma("sp", "gbw", GBd, gate_tmp.ap, reads=[gate_tmp.k], writes=["GBd"])
    for (dst, off, addone) in ((shiftp, 0, False), (scale1, 2048, True)):
        src = modbc.ap[:, off:off + 2048].rearrange("p (j n) -> p j n", j=16)
        S.op("dve", lambda e, src=src: e.tensor_tensor(out=dtmp.ap, in0=src,
                                                        in1=identf.unsqueeze(1).broadcast_to([128, 16, 128]), op=ALU.mult),
             reads=[modbc.k, cstf.k], writes=[dtmp.k])
        S.op("dve", lambda e, dst=dst: e.tensor_reduce(out=dst.ap, in_=dtmp.ap, axis=AX.X, op=ALU.add),
             reads=[dtmp.k], writes=[dst.k])
        if addone:
            S.op("dve", lambda e, dst=dst: e.tensor_scalar(out=dst.ap, in0=dst.ap, scalar1=1.0, scalar2=None, op0=ALU.add),
                 reads=[dst.k], writes=[dst.k])

    if STOP == 0:
        return finish()

    def interleave(gens, weights=None):
        active = [g for g in gens if g is not None]
        now = min(S.eng_free[e] for e in ("pe", "act", "dve", "pool"))
        chain_t = {id(g): now for g in active}
        while active:
            g = min(active, key=lambda gg: chain_t[id(gg)])
            S.step_fin = chain_t[id(g)]
            try:
                next(g)
            except StopIteration:
                active.remove(g)
                continue
            chain_t[id(g)] = S.step_fin

    def gen_norm_tile(xsrc_ap, xt, xn, ssb, hT, ti, pst):
        S.dma("sp", "x%d" % id(xt), xt.ap, xsrc_ap, writes=[xt.k])
        S.op("act", lambda e: e.activation(out=xn.ap, in_=xt.ap, func=AF.Square, accum_out=ssb.ap[:, 0:1]),
             reads=[xt.k], writes=[xn.k, ssb.k], c=1.6)
        yield
        S.op("dve", lambda e: e.tensor_scalar(out=ssb.ap[:, 1:2], in0=ssb.ap[:, 0:1], scalar1=1.0 / D, scalar2=EPS,
                                              op0=ALU.mult, op1=ALU.add), reads=[ssb.k], writes=[ssb.k])
        rsqrt_small(ssb.ap[:, 2:3], ssb.ap[:, 1:2], 1, [ssb.k], [ssb.k])
        yield
        S.op("pool", lambda e: e.tensor_scalar(out=xn.ap, in0=xt.ap, scalar1=ssb.ap[:, 2:3], scalar2=0.0, op0=ALU.mult, op1=ALU.add),
             reads=[xt.k, ssb.k], writes=[xn.k], c=4.0)
        yield
        hTe, hTo = hT

        def tr0(e):
            for j in range(8):
                r = e.transpose(out=pst[0].ap[:, j, :], in_=xn.ap[:, j * 128:(j + 1) * 128], identity=identb.ap)
            return r

        def tr1(e):
            for j in range(8, 16):
                r = e.transpose(out=pst[1].ap[:, j - 8, :], in_=xn.ap[:, j * 128:(j + 1) * 128], identity=identb.ap)
            return r
        S.op("pe", tr0, c=0.9, reads=[xn.k, identb.k], writes=[pst[0].k])
        S.op("pe", tr1, c=0.9, reads=[xn.k, identb.k], writes=[pst[1].k])
        yield

        def ev_act(e):
            for j in range(0, 8):
                r = e.activation(out=hTe.ap[:, j, ti * 128:(ti + 1) * 128], in_=pst[0].ap[:, j, :],
                                 func=AF.Identity, scale=scale1.ap[:, j:j + 1], bias=shiftp.ap[:, j:j + 1])
            return r

        def ev_dve(e):
            for j in range(8, 16):
                r = e.tensor_scalar(out=hTo.ap[:, j - 8, ti * 128:(ti + 1) * 128], in0=pst[1].ap[:, j - 8, :],
                                    scalar1=scale1.ap[:, j:j + 1], scalar2=shiftp.ap[:, j:j + 1],
                                    op0=ALU.mult, op1=ALU.add)
            return r
        S.op("act", ev_act, reads=[pst[0].k, scale1.k, shiftp.k], writes=[hTe.k], c=2.0)
        S.op("dve", ev_dve, reads=[pst[1].k, scale1.k, shiftp.k], writes=[hTo.k], c=2.0)
        yield

    def hsel(hT, j):
        return hT[j // 8].ap[:, j % 8]

    def pe_group(out_ap, key, lhs_fn, rhs_fn, reads, nsplit=4):
        per = 16 // nsplit
        for q in range(nsplit):
            def f(e, q=q):
                for j in range(q * per, (q + 1) * per):
                    r = e.matmul(out_ap, lhsT=lhs_fn(j), rhs=rhs_fn(j), start=(j == 0), stop=(j == 15))
                return r
            S.op("pe", f, c=0.3 * per, reads=reads, writes=[key])
            yield

    state["ptr"] = PERS_END
    w1 = sb([16, 1600], BF16)
    wukv = sb([4, 2048], BF16)
    wrg = sb([8, 2, 128], BF16)
    xts = [sb([2048]), sb([2048])]
    xn = sb([2048], BF16)
    ssb = sb([4])
    hTs = [(sb([8, 512], BF16), sb([8, 512], BF16)), (sb([8, 512], BF16), sb([8, 512], BF16))]
    XR = sb([2, 515])
    XC = sb([2, 512])
    XCB = sb([2, 512], BF16)
    THR = sb([2, 512])
    THI = sb([2, 512])
    A2 = sb([2, 512])
    HB = sb([2, 512])
    tmpo = sb([2, 2, 128])
    hst = sb([8, 256], BF16)
    kvcb = sb([512], BF16)
    kvcT = sb([4, 128], BF16)
    U = sb([1024])
    sqs = sb([4, 128])
    kvs = sb([16])
    ssn = sb([8])
    rkn = sb([8])
    kn = sb([8, 128], BF16)
    vst = sb([8, 4, 128], BF16)
    KTst = sb([8, 512], BF16)
    krn = sb([2, 32])
    ropA = sb([2, 32])
    ropB = sb([2, 32])
    kpe = sb([64], BF16)

    def cik(buf, ci, n, esz=4):
        return buf.sub(ci * n * esz, n * esz)

    w_in_v = w_in.rearrange("(j p) n -> p j n", p=128)
    S.dma("pool", "w1", w1.ap[:, :, 0:1024], w_in_v[:, :, 0:1024], writes=[w1.k])
    S.dma("pool", "w1", w1.ap[:, :, 1024:1600], w_in_v[:, :, 2560:3136], writes=[w1.k], cont=True)
    S.op("pool", lambda e: e.memset(wrg.ap, 0.0), writes=[wrg.k])
    for gi, wsrc in enumerate((w_rg_a, w_rg_x)):
        for hh in range(2):
            src = wsrc.rearrange("(c two) i j -> two i c j", two=2)[hh]
            S.dma("pool", "wrg", wrg.ap[hh * 64:(hh + 1) * 64, :, gi, hh * 64:(hh + 1) * 64], src, writes=[wrg.k])
    w_ukv_v = w_ukv.rearrange("(c p) n -> p c n", p=128)
    for ck in range(4):
        stg = xts[ck % 2]
        S.dma("sp", "wst%d" % (ck % 2), stg.ap, w_ukv_v[:, ck, :], writes=[stg.k])
        S.op("act", lambda e, ck=ck, stg=stg: e.activation(out=wukv.ap[:, ck, :], in_=stg.ap, func=AF.Identity,
                                                          scale=gkv_a[:, ck:ck + 1]),
             reads=[stg.k, par.k], writes=[wukv.k])

    pst = [ps(0, dt=BF16, shape=[8, 128]), ps(1, dt=BF16, shape=[8, 128])]
    pxr = [ps(2), ps(3)]
    pgr = ps(4)
    pgi = ps(5)
    pkv = ps(6)
    pmisc_kr = ps(7, off=0, nbytes=256)
    pmisc_kT = ps(7, off=256, nbytes=1024, dt=BF16, shape=[4, 128])
    pmisc_kp = ps(7, off=1280, nbytes=256, dt=BF16)
    pup = [ps(6), ps(7)]
    pKT = ps(6, dt=BF16, shape=[8, 128])

    cw = [rnn_par(k) for k in range(4)]
    cb_ = rnn_par(4)

    def gen_norm_super(u):
        for ti in range(4):
            t = 4 * u + ti
            yield from gen_norm_tile(x_all[t * 128:(t + 1) * 128, :], xts[t % 2], xn, ssb, hTs[u % 2], ti, pst)

    def gen_rnn(u):
        hT = hTs[u % 2]
        hkeys = [hT[0].k, hT[1].k]
        for g in range(4):
            ccs = [2 * g, 2 * g + 1]
            for ci, cc in enumerate(ccs):
                S.op("pool", lambda e, ci=ci, cc=cc: e.tensor_copy(out=XR.ap[:, ci, 0:3], in_=hist.ap[:, cc, :]),
                     reads=[hist.sub(cc * 12, 12)], writes=[XR.sub(ci * 2060, 12)])
                px = pxr[ci]

                yield from pe_group(px.ap, px.k, (lambda j, cc=cc: w1.ap[:, j, cc * 128:(cc + 1) * 128]),
                                    (lambda j: hsel(hT, j)), [w1.k] + hkeys)
            for ci, cc in enumerate(ccs):
                px = pxr[ci]
                S.op("act", lambda e, ci=ci, px=px: e.activation(out=XR.ap[:, ci, 3:515], in_=px.ap, func=AF.Copy),
                     reads=[px.k], writes=[XR.sub(ci * 2060 + 12, 2048)])
                yield
            for ci, cc in enumerate(ccs):
                S.op("pool", lambda e, ci=ci, cc=cc: e.tensor_copy(out=hist.ap[:, cc, :], in_=XR.ap[:, ci, 512:515]),
                     reads=[XR.sub(ci * 2060 + 12, 2048)], writes=[hist.sub(cc * 12, 12)])
                S.op("dve", lambda e, ci=ci, cc=cc: e.tensor_scalar(out=XC.ap[:, ci, :], in0=XR.ap[:, ci, 3:515],
                                                                  scalar1=cw[3][:, cc:cc + 1], scalar2=cb_[:, cc:cc + 1],
                                                                  op0=ALU.mult, op1=ALU.add),
                     reads=[cik(XR, ci, 515), par.k], writes=[cik(XC, ci, 512)])
                yield
            for k in (2, 1, 0):
                for ci, cc in enumerate(ccs):
                    S.op("dve", lambda e, ci=ci, cc=cc, k=k: e.scalar_tensor_tensor(
                        out=XC.ap[:, ci, :], in0=XR.ap[:, ci, k:k + 512], scalar=cw[k][:, cc:cc + 1],
                        in1=XC.ap[:, ci, :], op0=ALU.mult, op1=ALU.add),
                        reads=[cik(XR, ci, 515), cik(XC, ci, 512), par.k], writes=[cik(XC, ci, 512)])
                yield
            for ci, cc in enumerate(ccs):
                S.op("dve", lambda e, ci=ci: e.tensor_copy(out=XCB.ap[:, ci, :], in_=XC.ap[:, ci, :]),
                     reads=[cik(XC, ci, 512)], writes=[cik(XCB, ci, 512, 2)])
                yield
            for ci, cc in enumerate(ccs):
                S.op("pe", lambda e, ci=ci, cc=cc: e.matmul(pgr.ap, lhsT=wrg.ap[:, cc, 0, :], rhs=XCB.ap[:, ci, :],
                                                            start=True, stop=True),
                     reads=[wrg.k, cik(XCB, ci, 512, 2)], writes=[pgr.k])
                S.op("act", lambda e, ci=ci, cc=cc: e.activation(out=THR.ap[:, ci, :], in_=pgr.ap, func=AF.Tanh, scale=0.5,
                                                                bias=hb_a.ap[:, cc:cc + 1]),
                     reads=[pgr.k, hb_a.k], writes=[cik(THR, ci, 512)])
                S.op("pe", lambda e, ci=ci, cc=cc: e.matmul(pgi.ap, lhsT=wrg.ap[:, cc, 1, :], rhs=XCB.ap[:, ci, :],
                                                            start=True, stop=True),
                     reads=[wrg.k, cik(XCB, ci, 512, 2)], writes=[pgi.k])
                S.op("act", lambda e, ci=ci, cc=cc: e.activation(out=THI.ap[:, ci, :], in_=pgi.ap, func=AF.Tanh, scale=0.5,
                                                                bias=hb_x.ap[:, cc:cc + 1]),
                     reads=[pgi.k, hb_x.k], writes=[cik(THI, ci, 512)])
                yield
            for ci, cc in enumerate(ccs):
                S.op("act", lambda e, ci=ci, cc=cc: e.activation(out=A2.ap[:, ci, :], in_=THR.ap[:, ci, :], func=AF.Exp,
                                                               scale=cl_one.ap[:, cc:cc + 1], bias=cl_one.ap[:, cc:cc + 1]),
                     reads=[cik(THR, ci, 512), cl_one.k], writes=[cik(A2, ci, 512)])
                S.op("act", lambda e, ci=ci, cc=cc: e.activation(out=THR.ap[:, ci, :], in_=THR.ap[:, ci, :], func=AF.Exp,
                                                               scale=cl_half.ap[:, cc:cc + 1], bias=cl_half.ap[:, cc:cc + 1]),
                     reads=[cik(THR, ci, 512), cl_half.k], writes=[cik(THR, ci, 512)])
                S.op("dve", lambda e, ci=ci: e.scalar_tensor_tensor(out=THI.ap[:, ci, :], in0=THI.ap[:, ci, :], scalar=1.0,
                                                                  in1=XC.ap[:, ci, :], op0=ALU.add, op1=ALU.mult),
                     reads=[cik(THI, ci, 512), cik(XC, ci, 512)], writes=[cik(THI, ci, 512)])
                yield
            S.op("act", lambda e: e.activation(out=A2.ap, in_=A2.ap, func=AF.Sqrt, scale=-1.0, bias=onesf.ap[:, 0:1]),
                 reads=[A2.k, onesf.k], writes=[A2.k])
            yield
            S.op("dve", lambda e: e.scalar_tensor_tensor(out=THI.ap, in0=A2.ap, scalar=0.5, in1=THI.ap,
                                                         op0=ALU.mult, op1=ALU.mult),
                 reads=[A2.k, THI.k], writes=[THI.k])
            yield
            for ci, cc in enumerate(ccs):
                S.op("dve", lambda e, ci=ci, cc=cc: e.tensor_tensor_scan(
                    out=HB.ap[:, ci, :], data0=THR.ap[:, ci, :], data1=THI.ap[:, ci, :],
                    initial=hstate.ap[:, cc:cc + 1], op0=ALU.mult, op1=ALU.add),
                    reads=[cik(THR, ci, 512), cik(THI, ci, 512), hstate.sub(cc * 4, 4)], writes=[cik(HB, ci, 512)])
                yield
            for ci, cc in enumerate(ccs):
                S.op("pool", lambda e, ci=ci, cc=cc: e.tensor_copy(out=hstate.ap[:, cc:cc + 1], in_=HB.ap[:, ci, 511:512]),
                     reads=[cik(HB, ci, 512)], writes=[hstate.sub(cc * 4, 4)])
                hv = HB.ap[:, ci, :].rearrange("p (a two n) -> p a two n", two=2, n=128)
                S.op("pool", lambda e, hv=hv, ci=ci: e.tensor_scalar(out=tmpo.ap[:, ci], in0=hv[:, :, 0, :], scalar1=sel0,
                                                                    scalar2=0.0, op0=ALU.mult, op1=ALU.add),
                     reads=[cik(HB, ci, 512), par.k], writes=[cik(tmpo, ci, 256)])
                yield
            for ci, cc in enumerate(ccs):
                hv = HB.ap[:, ci, :].rearrange("p (a two n) -> p a two n", two=2, n=128)
                S.op("dve", lambda e, hv=hv, cc=cc, ci=ci: e.scalar_tensor_tensor(
                    out=hst.ap[:, cc, :].rearrange("p (a n) -> p a n", n=128), in0=hv[:, :, 1, :], scalar=sel1,
                    in1=tmpo.ap[:, ci], op0=ALU.mult, op1=ALU.add),
                    reads=[cik(HB, ci, 512), cik(tmpo, ci, 256), par.k], writes=[hst.sub(cc * 512, 512)])
                yield
        S.dma("sp", "hod", HOd[:, :, u * 256:(u + 1) * 256], hst.ap, reads=[hst.k], writes=[("HOd", u)])
        yield

    def gen_kr_tile(u, ti):
        hT = hTs[u % 2]
        hkeys = [hT[0].k, hT[1].k]
        if True:
            t = 4 * u + ti
            tcol = slice(ti * 128, (ti + 1) * 128)

            def mkr(e, tcol=tcol):
                for j in range(16):
                    r = e.matmul(pmisc_kr.ap, lhsT=hsel(hT, j)[:, tcol], rhs=w1.ap[:, j, 1536:1600], start=(j == 0), stop=(j == 15))
                return r
            S.op("pe", mkr, c=0.6, reads=[w1.k] + hkeys, writes=[pmisc_kr.k])
            yield
            S.op("act", lambda e: e.activation(out=ropA.ap, in_=pmisc_kr.ap.rearrange("p (a n) -> p a n", a=2),
                                               func=AF.Square, accum_out=kvs.ap[:, 4:5]),
                 reads=[pmisc_kr.k], writes=[ropA.k, kvs.sub(16, 4)])
            S.op("dve", lambda e: e.tensor_scalar(out=kvs.ap[:, 5:6], in0=kvs.ap[:, 4:5], scalar1=1.0 / 64, scalar2=EPS,
                                                  op0=ALU.mult, op1=ALU.add), reads=[kvs.sub(16, 4)], writes=[kvs.sub(20, 4)])
            yield
            rsqrt_small(kvs.ap[:, 6:7], kvs.ap[:, 5:6], 1, [kvs.sub(20, 4)], [kvs.sub(24, 4)])
            yield
            S.op("dve", lambda e: e.scalar_tensor_tensor(out=krn.ap.rearrange("p a n -> p (a n)"), in0=pmisc_kr.ap,
                                                         scalar=kvs.ap[:, 6:7], in1=gkr_bc, op0=ALU.mult, op1=ALU.mult),
                 reads=[pmisc_kr.k, kvs.sub(24, 4), par.k], writes=[krn.k])
            yield
            cos_t = cosA.ap[:, t, :].unsqueeze(1).broadcast_to([128, 2, 32])
            sin_t = sinA.ap[:, t, :].unsqueeze(1).broadcast_to([128, 2, 32])
            S.op("pool", lambda e, cos_t=cos_t: e.tensor_tensor(out=ropA.ap, in0=krn.ap, in1=cos_t, op=ALU.mult),
                 reads=[krn.k, cosA.k], writes=[ropA.k])
            S.op("pool", lambda e, sin_t=sin_t: e.tensor_tensor(out=ropB.ap, in0=krn.ap, in1=sin_t, op=ALU.mult),
                 reads=[krn.k, sinA.k], writes=[ropB.k])
            yield
            S.op("pool", lambda e: e.tensor_tensor(out=kpe.ap[:, 0:32], in0=ropA.ap[:, 0, :], in1=ropB.ap[:, 1, :], op=ALU.subtract),
                 reads=[ropA.k, ropB.k], writes=[kpe.k])
            S.op("pool", lambda e: e.tensor_tensor(out=kpe.ap[:, 32:64], in0=ropB.ap[:, 0, :], in1=ropA.ap[:, 1, :], op=ALU.add),
                 reads=[ropA.k, ropB.k], writes=[kpe.k])
            yield
            S.op("pe", lambda e: e.transpose(out=pmisc_kp.ap[0:64, :], in_=kpe.ap, identity=identb.ap),
                 reads=[kpe.k, identb.k], writes=[pmisc_kp.k])
            yield
            S.op("dve", lambda e, t=t: e.tensor_copy(out=KpeT.ap[0:64, t * 128:(t + 1) * 128], in_=pmisc_kp.ap[0:64, :]),
                 reads=[pmisc_kp.k], writes=[("kpeT", t)])
            yield

    def gen_kv(u):
        hT = hTs[u % 2]
        hkeys = [hT[0].k, hT[1].k]
        for ti in range(4):
            t = 4 * u + ti
            tcol = slice(ti * 128, (ti + 1) * 128)

            yield from pe_group(pkv.ap, pkv.k, (lambda j, tcol=tcol: hsel(hT, j)[:, tcol]),
                                (lambda j: w1.ap[:, j, 1024:1536]), [w1.k] + hkeys)

            S.op("act", lambda e: e.activation(out=U.ap[:, 0:512], in_=pkv.ap, func=AF.Square, accum_out=kvs.ap[:, 0:1]),
                 reads=[pkv.k], writes=[U.k, kvs.sub(0, 4)])
            S.op("act", lambda e: e.activation(out=kvcb.ap, in_=pkv.ap, func=AF.Copy), reads=[pkv.k], writes=[kvcb.k])
            yield
            yield from gen_kr_tile(u, ti)
            S.op("dve", lambda e: e.tensor_scalar(out=kvs.ap[:, 1:2], in0=kvs.ap[:, 0:1], scalar1=1.0 / 512, scalar2=EPS,
                                                  op0=ALU.mult, op1=ALU.add), reads=[kvs.sub(0, 4)], writes=[kvs.sub(4, 4)])
            rsqrt_small(kvs.ap[:, 2:3], kvs.ap[:, 1:2], 1, [kvs.sub(4, 4)], [kvs.sub(8, 4)])
            S.op("dve", lambda e: e.tensor_scalar(out=kvs.ap[:, 3:4], in0=kvs.ap[:, 1:2], scalar1=EPS, scalar2=None,
                                                  op0=ALU.mult), reads=[kvs.sub(4, 4)], writes=[kvs.sub(12, 4)])

            def trk(e):
                for ck in range(4):
                    r = e.transpose(out=pmisc_kT.ap[:, ck, :], in_=kvcb.ap[:, ck * 128:(ck + 1) * 128], identity=identb.ap)
                return r
            S.op("pe", trk, c=0.5, reads=[kvcb.k, identb.k], writes=[pmisc_kT.k])
            yield
            S.op("dve", lambda e: e.tensor_copy(out=kvcT.ap, in_=pmisc_kT.ap), reads=[pmisc_kT.k], writes=[kvcT.k])
            yield
            for hf in range(2):
                for n in range(2):
                    def mup(e, hf=hf, n=n):
                        for ck in range(4):
                            c0 = hf * 1024 + n * 512
                            r = e.matmul(pup[n].ap, lhsT=kvcT.ap[:, ck, :], rhs=wukv.ap[:, ck, c0:c0 + 512],
                                         start=(ck == 0), stop=(ck == 3))
                        return r
                    S.op("pe", mup, c=1.2, reads=[kvcT.k, wukv.k], writes=[pup[n].k])
                    S.op("act", lambda e, n=n: e.activation(out=U.ap[:, n * 512:(n + 1) * 512], in_=pup[n].ap, func=AF.Copy),
                         reads=[pup[n].k], writes=[U.sub(n * 2048, 2048)])
                    yield
                U3 = U.ap.rearrange("p (h c) -> p h c", h=4)
                Un = U3[:, :, 0:128]
                Uv = U3[:, :, 128:256]
                hs = slice(4 * hf, 4 * hf + 4)
                S.op("dve", lambda e, Un=Un: e.tensor_tensor(out=sqs.ap, in0=Un, in1=Un, op=ALU.mult), reads=[U.k], writes=[sqs.k])
                S.op("dve", lambda e, hs=hs: e.tensor_reduce(out=ssn.ap[:, hs], in_=sqs.ap, axis=AX.X, op=ALU.add),
                     reads=[sqs.k], writes=[ssn.k])
                yield
                S.op("dve", lambda e, hs=hs: e.tensor_scalar(out=ssn.ap[:, hs], in0=ssn.ap[:, hs], scalar1=1.0 / 128,
                                                             scalar2=kvs.ap[:, 3:4], op0=ALU.mult, op1=ALU.add),
                     reads=[ssn.k, kvs.sub(12, 4)], writes=[ssn.k])
                rsqrt_small(rkn.ap[:, hs], ssn.ap[:, hs], 4, [ssn.k], [rkn.k])
                yield
                S.op("dve", lambda e, Un=Un, hs=hs: e.tensor_tensor(out=kn.ap[:, hs, :], in0=Un,
                                                                    in1=rkn.ap[:, hs].unsqueeze(2).broadcast_to([128, 4, 128]),
                                                                    op=ALU.mult), reads=[U.k, rkn.k], writes=[kn.k])
                S.op("pool", lambda e, Uv=Uv, ti=ti, hs=hs: e.tensor_scalar(out=vst.ap[:, hs, ti, :], in0=Uv, scalar1=kvs.ap[:, 2:3],
                                                                            scalar2=0.0, op0=ALU.mult, op1=ALU.add),
                     reads=[U.k, kvs.sub(8, 4)], writes=[vst.k])
                yield

            def trK(e):
                for h in range(8):
                    r = e.transpose(out=pKT.ap[:, h, :], in_=kn.ap[:, h, :], identity=identb.ap)
                return r
            S.op("pe", trK, c=0.9, reads=[kn.k, identb.k], writes=[pKT.k])
            yield
            S.op("act", lambda e, tcol=tcol: e.activation(out=KTst.ap[:, :, tcol], in_=pKT.ap, func=AF.Identity, scale=gkn),
                 reads=[pKT.k, par.k], writes=[KTst.k])
            yield
        S.dma("sp", "ktd", KTd[:, :, u * 512:(u + 1) * 512].rearrange("h p n -> p h n"), KTst.ap,
              reads=[KTst.k], writes=[("KTd", u)])
        S.dma("sp", "vd", Vd[:, :, 4 * u:4 * u + 4, :].rearrange("h p t d -> p h t d"), vst.ap,
              reads=[vst.k], writes=[("Vd", u)])
        yield

    interleave([gen_norm_super(0)])
    for u in range(NSUP):
        interleave([gen_rnn(u), gen_kv(u), gen_norm_super(u + 1) if u + 1 < NSUP else None])

    if STOP == 1:
        return finish()

    HTd = dscr("HTd", [NSUPO, 2, 128, 8, 512])
    state["ptr"] = PERS_END
    w2 = sb([16, 1536], BF16)
    wuq = sb([4, 1536], BF16)
    xts2 = [sb([2048]), sb([2048])]
    xn2 = sb([2048], BF16)
    ssb2 = sb([4])
    hT2s = [(sb([8, 512], BF16), sb([8, 512], BF16)), (sb([8, 512], BF16), sb([8, 512], BF16))]
    hown = sb([8, 512], BF16)
    sgt = [sb([512]), sb([512])]
    yst = sb([8, 512], BF16)
    qcb = sb([512], BF16)
    qcT = sb([4, 128], BF16)
    Uq = sb([1536])
    sqq = sb([1536])
    qs = sb([8])
    ssqn = sb([8])
    ssqr = sb([8])
    rqn = sb([8])
    rqr = sb([8])
    qn = sb([8, 128], BF16)
    qrn = sb([8, 2, 32])
    rqA = sb([8, 2, 32])
    rqB = sb([8, 2, 32])
    qpe = sb([8, 2, 32], BF16)
    QTst = sb([8, 512], BF16)
    QRst = sb([4, 512], BF16)

    S.dma("pool", "w2", w2.ap, w_in_v[:, :, 1024:2560], writes=[w2.k])
    w_uq_v = w_uq.rearrange("(c p) n -> p c n", p=128)
    for ck in range(4):
        stg = xts2[ck % 2]
        S.dma("sp", "wst2%d" % (ck % 2), stg.ap[:, 0:1536], w_uq_v[:, ck, :], writes=[stg.k])
        S.op("act", lambda e, ck=ck, stg=stg: e.activation(out=wuq.ap[:, ck, :], in_=stg.ap[:, 0:1536], func=AF.Identity,
                                                          scale=gq_a[:, ck:ck + 1]),
             reads=[stg.k, par.k], writes=[wuq.k])

    pst2 = [ps(0, dt=BF16, shape=[8, 128]), ps(1, dt=BF16, shape=[8, 128])]
    pfm = [ps(2), ps(3)]
    pqc = ps(4)
    pqT = ps(5, off=0, nbytes=1024, dt=BF16, shape=[4, 128])
    pqrT = ps(5, off=1024, nbytes=1024, dt=BF16, shape=[4, 128])
    puq = [ps(6), ps(7), ps(4)]
    pQT = ps(5, dt=BF16, shape=[8, 128])

    def gen_norm_own(v):
        for ti in range(4):
            j = 4 * v + ti
            yield from gen_norm_tile(x_own[j * 128:(j + 1) * 128, :], xts2[j % 2], xn2, ssb2, hT2s[v % 2], ti, pst2)
        for par_ in range(2):
            S.dma("sp", "htw%d" % par_, HTd[v, par_], hT2s[v % 2][par_].ap, reads=[hT2s[v % 2][par_].k], writes=[("HTd", v, par_)])
        yield

    def gen_gr(v):
        hT2 = hT2s[v % 2]
        hkeys = [hT2[0].k, hT2[1].k]
        S.dma("sp", "hol", hown.ap, HOd[:, :, v * 512:(v + 1) * 512], reads=[("HOd", 2 * v), ("HOd", 2 * v + 1)],
              writes=[hown.k])
        for cc in range(8):
            pf = pfm[cc % 2]

            yield from pe_group(pf.ap, pf.k, (lambda j, cc=cc: w2.ap[:, j, cc * 128:(cc + 1) * 128]),
                                (lambda j: hsel(hT2, j)), [w2.k] + hkeys)
            sg = sgt[cc % 2]
            S.op("act", lambda e, pf=pf, sg=sg: e.activation(out=sg.ap, in_=pf.ap, func=AF.Silu),
                 reads=[pf.k], writes=[sg.k])
            yield
            S.op("dve", lambda e, sg=sg, cc=cc: e.tensor_tensor(out=yst.ap[:, cc, :], in0=sg.ap, in1=hown.ap[:, cc, :],
                                                               op=ALU.mult),
                 reads=[sg.k, hown.k], writes=[yst.sub(cc * 1024, 1024)])
            yield
        S.dma("sp", "ycw", YCd[0:8, :, v * 512:(v + 1) * 512].rearrange("c p n -> p c n"), yst.ap,
              reads=[yst.k], writes=[("YCd", 0, v)])
        yield

    def gen_q(v):
        hT2 = hT2s[v % 2]
        hkeys = [hT2[0].k, hT2[1].k]
        for ti in range(4):
            j = 4 * v + ti
            tcol = slice(ti * 128, (ti + 1) * 128)

            yield from pe_group(pqc.ap, pqc.k, (lambda jj, tcol=tcol: hsel(hT2, jj)[:, tcol]),
                                (lambda jj: w2.ap[:, jj, 1024:1536]), [w2.k] + hkeys)
            S.op("act", lambda e: e.activation(out=sqq.ap[:, 0:512], in_=pqc.ap, func=AF.Square, accum_out=qs.ap[:, 0:1]),
                 reads=[pqc.k], writes=[sqq.k, qs.sub(0, 4)])
            S.op("act", lambda e: e.activation(out=qcb.ap, in_=pqc.ap, func=AF.Copy), reads=[pqc.k], writes=[qcb.k])
            yield
            S.op("dve", lambda e: e.tensor_scalar(out=qs.ap[:, 1:2], in0=qs.ap[:, 0:1], scalar1=1.0 / 512, scalar2=EPS,
                                                  op0=ALU.mult, op1=ALU.add), reads=[qs.sub(0, 4)], writes=[qs.sub(4, 4)])
            S.op("dve", lambda e: e.tensor_scalar(out=qs.ap[:, 3:4], in0=qs.ap[:, 1:2], scalar1=EPS, scalar2=None,
                                                  op0=ALU.mult), reads=[qs.sub(4, 4)], writes=[qs.sub(12, 4)])

            def trq(e):
                for ck in range(4):
                    r = e.transpose(out=pqT.ap[:, ck, :], in_=qcb.ap[:, ck * 128:(ck + 1) * 128], identity=identb.ap)
                return r
            S.op("pe", trq, c=0.5, reads=[qcb.k, identb.k], writes=[pqT.k])
            yield
            S.op("dve", lambda e: e.tensor_copy(out=qcT.ap, in_=pqT.ap), reads=[pqT.k], writes=[qcT.k])
            yield
            for n in range(3):
                def muq(e, n=n):
                    for ck in range(4):
                        r = e.matmul(puq[n].ap, lhsT=qcT.ap[:, ck, :], rhs=wuq.ap[:, ck, n * 512:(n + 1) * 512],
                                     start=(ck == 0), stop=(ck == 3))
                    return r
                S.op("pe", muq, c=1.2, reads=[qcT.k, wuq.k], writes=[puq[n].k])
                S.op("act", lambda e, n=n: e.activation(out=Uq.ap[:, n * 512:(n + 1) * 512], in_=puq[n].ap, func=AF.Copy),
                     reads=[puq[n].k], writes=[Uq.sub(n * 2048, 2048)])
                yield
            U3 = Uq.ap.rearrange("p (h c) -> p h c", h=8)
            S3 = sqq.ap.rearrange("p (h c) -> p h c", h=8)
            S.op("dve", lambda e: e.tensor_tensor(out=sqq.ap, in0=Uq.ap, in1=Uq.ap, op=ALU.mult), reads=[Uq.k], writes=[sqq.k])
            yield
            S.op("dve", lambda e, S3=S3: e.tensor_reduce(out=ssqn.ap, in_=S3[:, :, 0:128], axis=AX.X, op=ALU.add),
                 reads=[sqq.k], writes=[ssqn.k])
            S.op("dve", lambda e, S3=S3: e.tensor_reduce(out=ssqr.ap, in_=S3[:, :, 128:192], axis=AX.X, op=ALU.add),
                 reads=[sqq.k], writes=[ssqr.k])
            yield
            S.op("dve", lambda e: e.tensor_scalar(out=ssqn.ap, in0=ssqn.ap, scalar1=1.0 / 128, scalar2=qs.ap[:, 3:4],
                                                  op0=ALU.mult, op1=ALU.add), reads=[ssqn.k, qs.sub(12, 4)], writes=[ssqn.k])
            S.op("dve", lambda e: e.tensor_scalar(out=ssqr.ap, in0=ssqr.ap, scalar1=1.0 / 64, scalar2=qs.ap[:, 3:4],
                                                  op0=ALU.mult, op1=ALU.add), reads=[ssqr.k, qs.sub(12, 4)], writes=[ssqr.k])
            yield
            rsqrt_small(rqn.ap, ssqn.ap, 8, [ssqn.k], [rqn.k])
            rsqrt_small(rqr.ap, ssqr.ap, 8, [ssqr.k], [rqr.k])
            yield
            S.op("dve", lambda e, U3=U3: e.tensor_tensor(out=qn.ap, in0=U3[:, :, 0:128],
                                                         in1=rqn.ap.unsqueeze(2).broadcast_to([128, 8, 128]), op=ALU.mult),
                 reads=[Uq.k, rqn.k], writes=[qn.k])
            qr2 = qrn.ap.rearrange("p h a n -> p h (a n)")
            S.op("dve", lambda e, U3=U3, qr2=qr2: e.tensor_tensor(out=qr2, in0=U3[:, :, 128:192],
                                                                  in1=rqr.ap.unsqueeze(2).broadcast_to([128, 8, 64]), op=ALU.mult),
                 reads=[Uq.k, rqr.k], writes=[qrn.k])
            yield

            def trQ(e):
                for h in range(8):
                    r = e.transpose(out=pQT.ap[:, h, :], in_=qn.ap[:, h, :], identity=identb.ap)
                return r
            S.op("pe", trQ, c=0.9, reads=[qn.k, identb.k], writes=[pQT.k])
            S.op("pool", lambda e, qr2=qr2: e.tensor_tensor(out=qr2, in0=qr2, in1=gqr_bc.unsqueeze(1).broadcast_to([128, 8, 64]),
                                                            op=ALU.mult), reads=[qrn.k, par.k], writes=[qrn.k])
            yield
            S.op("act", lambda e, tcol=tcol: e.activation(out=QTst.ap[:, :, tcol], in_=pQT.ap, func=AF.Identity, scale=gqn),
                 reads=[pQT.k, par.k], writes=[QTst.k])
            cos_t = cosO.ap[:, j, :].unsqueeze(1).unsqueeze(1).broadcast_to([128, 8, 2, 32])
            sin_t = sinO.ap[:, j, :].unsqueeze(1).unsqueeze(1).broadcast_to([128, 8, 2, 32])
            S.op("pool", lambda e, cos_t=cos_t: e.tensor_tensor(out=rqA.ap, in0=qrn.ap, in1=cos_t, op=ALU.mult),
                 reads=[qrn.k, cosO.k], writes=[rqA.k])
            S.op("pool", lambda e, sin_t=sin_t: e.tensor_tensor(out=rqB.ap, in0=qrn.ap, in1=sin_t, op=ALU.mult),
                 reads=[qrn.k, sinO.k], writes=[rqB.k])
            yield
            S.op("pool", lambda e: e.tensor_tensor(out=qpe.ap[:, :, 0, :], in0=rqA.ap[:, :, 0, :], in1=rqB.ap[:, :, 1, :],
                                                   op=ALU.subtract), reads=[rqA.k, rqB.k], writes=[qpe.k])
            S.op("pool", lambda e: e.tensor_tensor(out=qpe.ap[:, :, 1, :], in0=rqB.ap[:, :, 0, :], in1=rqA.ap[:, :, 1, :],
                                                   op=ALU.add), reads=[rqA.k, rqB.k], writes=[qpe.k])
            yield
            qpf = qpe.ap.rearrange("p h a n -> p (h a n)")

            def trR(e, qpf=qpf):
                for i in range(4):
                    r = e.transpose(out=pqrT.ap[:, i, :], in_=qpf[:, i * 128:(i + 1) * 128], identity=identb.ap)
                return r
            S.op("pe", trR, c=0.5, reads=[qpe.k, identb.k], writes=[pqrT.k])
            yield
            S.op("dve", lambda e, tcol=tcol: e.tensor_copy(out=QRst.ap[:, :, tcol], in_=pqrT.ap),
                 reads=[pqrT.k], writes=[QRst.k])
            yield
        S.dma("sp", "qtw", QTd[:, :, v * 512:(v + 1) * 512].rearrange("h p n -> p h n"), QTst.ap,
              reads=[QTst.k], writes=[("QTd", v)])
        S.dma("sp", "qrw", QRd[:, :, v * 512:(v + 1) * 512].rearrange("(i hh) d n -> (hh d) i n", hh=2), QRst.ap,
              reads=[QRst.k], writes=[("QRd", v)])
        yield

    interleave([gen_norm_own(0)])
    for v in range(NSUPO):
        interleave([gen_gr(v), gen_q(v), gen_norm_own(v + 1) if v + 1 < NSUPO else None])

    if STOP == 2:
        return finish()

    state["ptr"] = PERS_END
    w2b = sb([16, 1024], BF16)
    hT3 = [(sb([8, 512], BF16), sb([8, 512], BF16)), (sb([8, 512], BF16), sb([8, 512], BF16))]
    sgst = [sb([8, 512], BF16), sb([8, 512], BF16)]
    S.dma("pool", "w2b", w2b.ap, w_in_v[:, :, 3136:4160], writes=[w2b.k])
    pfb = [ps(i) for i in range(4)]

    def ht3_loads(v):
        hb = hT3[v % 2]
        for par_ in range(2):
            S.dma("sp", "htr%d%d" % (v % 2, par_), hb[par_].ap, HTd[v, par_], reads=[("HTd", v, par_)], writes=[hb[par_].k])
    ht3_loads(0)
    for v in range(NSUPO):
        hb3 = hT3[v % 2]
        if v + 1 < NSUPO:
            ht3_loads(v + 1)
        sgb = sgst[v % 2]
        for cc in range(8):
            pf = pfb[cc % 4]

            def mmg(e, cc=cc, pf=pf, hb3=hb3):
                for j in range(16):
                    r = e.matmul(pf.ap, lhsT=w2b.ap[:, j, cc * 128:(cc + 1) * 128], rhs=hsel(hb3, j),
                                 start=(j == 0), stop=(j == 15))
                return r
            S.op("pe", mmg, c=4.8, reads=[w2b.k, hb3[0].k, hb3[1].k], writes=[pf.k])
            S.op("act", lambda e, pf=pf, cc=cc, sgb=sgb: e.activation(out=sgb.ap[:, cc, :], in_=pf.ap, func=AF.Silu),
                 reads=[pf.k], writes=[sgb.sub(cc * 1024, 1024)])
        S.dma("sp", "sgw%d" % (v % 2), SGd[:, :, v * 512:(v + 1) * 512].rearrange("c p n -> p c n"), sgb.ap,
              reads=[sgb.k], writes=[("SGd", v)])

    if STOP == 3:
        return finish()

    state["ptr"] = ARENA_BYTES - 65536
    wout = sb([16, 2048], BF16)
    w_out_v = w_out.rearrange("(c p) n -> p c n", p=128)
    S.dma("pool", "wout", wout.ap[:, 0:8, :], w_out_v[:, 0:8, :], writes=[wout.k])
    S.dma("pool", "wout", wout.ap[:, 8:16, :], w_out_v[:, 8:16, :], writes=[wout.k], cont=True)
    state["ptr"] = PERS_END
    KTh = [sb([SEQ], BF16), sb([SEQ], BF16)]
    Vh = [sb([NT, 128], BF16), sb([NT, 128], BF16)]
    QN = [sb([2048], BF16), sb([2048], BF16)]
    QR = [sb([2048], BF16), sb([2048], BF16)]
    SG = [sb([2048], BF16), sb([2048], BF16)]
    PT = [sb([512], BF16) for _ in range(8)]
    rden = sb([512])
    o1 = sb([512])
    yast = [sb([512], BF16), sb([512], BF16)]
    acc = [sb([512]), sb([512])]
    acc_hi = sb([512], BF16)
    acc_lo = sb([512], BF16)
    pS = [ps(i) for i in range(4)]
    pO = [ps(4), ps(5)]
    pD = [ps(6), ps(7)]
    for sl_ in range(2):
        S.op("pool", lambda e, sl_=sl_: e.memset(QR[sl_].ap[64:128, :], 0.0), writes=[QR[sl_].k])
    all_kv_keys = [("KTd", u) for u in range(NSUP)] + [("Vd", u) for u in range(NSUP)]
    all_q_keys = [("QTd", v) for v in range(NSUPO)] + [("QRd", v) for v in range(NSUPO)] + [("SGd", v) for v in range(NSUPO)]
    kpe_keys = [("kpeT", t) for t in range(NT)]

    def head_loads(h):
        sl = h % 2
        S.dma("sp", "kth%d" % sl, KTh[sl].ap, KTd[h], reads=all_kv_keys, writes=[KTh[sl].k])
        S.dma("sp", "vh%d" % sl, Vh[sl].ap, Vd[h], reads=all_kv_keys, writes=[Vh[sl].k])
        S.dma("sp", "qn%d" % sl, QN[sl].ap, QTd[h], reads=all_q_keys, writes=[QN[sl].k])
        S.dma("sp", "qr%d" % sl, QR[sl].ap[0:64, :], QRd[h], reads=all_q_keys + [QR[sl].k], writes=[("qrlo", sl)])
        S.dma("sp", "sg%d" % sl, SG[sl].ap, SGd[h], reads=all_q_keys, writes=[SG[sl].k])

    blk = 0
    head_loads(0)
    for h in range(8):
        sl = h % 2
        for qb in range(4):
            items = []
            for kt in range(8 * qb):
                items.append((kt, 0, None))
            for i in range(4):
                for e_ in range(2):
                    items.append((8 * qb + 2 * i + e_, i * 128, maskA if e_ == 0 else maskB))
            n_it = len(items)
            po = pO[blk % 2]
            pd = pD[blk % 2]
            blk += 1
            LAG = 2

            def qk(idx, kt, c0, sl=sl, qb=qb):
                N = 512 - c0
                psb = pS[idx % 4]
                pt = PT[idx % 8]
                q0 = qb * 512 + c0

                def f(e):
                    e.matmul(psb.ap[:, 0:N], lhsT=KTh[sl].ap[:, kt * 128:(kt + 1) * 128], rhs=QN[sl].ap[:, q0:q0 + N],
                             start=True, stop=False)
                    return e.matmul(psb.ap[:, 0:N], lhsT=KpeT.ap[:, kt * 128:(kt + 1) * 128],
                                    rhs=QR[sl].ap[:, q0:q0 + N], start=False, stop=True)
                S.op("pe", f, reads=[KTh[sl].k, QN[sl].k, QR[sl].k, ("qrlo", sl), "kpeT_pad"] + kpe_keys, writes=[psb.k])
                S.op("act", lambda e: e.activation(out=pt.ap[:, 0:N], in_=psb.ap[:, 0:N], func=AF.Exp, scale=SM_SCALE),
                     reads=[psb.k], writes=[pt.k])

            def pv(idx, kt, c0, mask, sl=sl, po=po, pd=pd, n_it=n_it, blkpar=(blk % 2)):
                N = 512 - c0
                pt = PT[idx % 8]
                if mask is not None:
                    S.op("pool", lambda e: e.tensor_tensor(out=pt.ap[:, 0:128], in0=pt.ap[:, 0:128], in1=mask.ap, op=ALU.mult),
                         reads=[pt.k, mask.k], writes=[pt.k])
                first = (idx == 0)
                last = (idx == n_it - 1)
                ac = acc[blkpar]
                if first:
                    S.op("dve", lambda e: e.tensor_copy(out=ac.ap, in_=pt.ap), reads=[pt.k], writes=[ac.k])
                else:
                    S.op("dve", lambda e: e.tensor_tensor(out=ac.ap[:, c0:512], in0=ac.ap[:, c0:512], in1=pt.ap[:, 0:N], op=ALU.add),
                         reads=[pt.k, ac.k], writes=[ac.k])

                def f(e):
                    return e.matmul(po.ap[:, c0:512], lhsT=Vh[sl].ap[:, kt, :], rhs=pt.ap[:, 0:N], start=first, stop=last,
                                    skip_group_check=True)
                S.op("pe", f, reads=[Vh[sl].k, pt.k], writes=[po.k])
                if last:
                    S.op("dve", lambda e: e.tensor_copy(out=acc_hi.ap, in_=ac.ap), reads=[ac.k], writes=[acc_hi.k])
                    S.op("dve", lambda e: e.tensor_tensor(out=acc_lo.ap, in0=ac.ap, in1=acc_hi.ap, op=ALU.subtract),
                         reads=[ac.k, acc_hi.k], writes=[acc_lo.k])

                    def fd(e):
                        e.matmul(pd.ap, lhsT=onesb.ap, rhs=acc_hi.ap, start=True, stop=False)
                        return e.matmul(pd.ap, lhsT=onesb.ap, rhs=acc_lo.ap, start=False, stop=True)
                    S.op("pe", fd, reads=[acc_hi.k, acc_lo.k, onesb.k], writes=[pd.k])

            for idx in range(n_it + LAG):
                if idx < n_it:
                    qk(idx, items[idx][0], items[idx][1])
                if idx - LAG >= 0:
                    it = items[idx - LAG]
                    pv(idx - LAG, it[0], it[1], it[2])
            ya = yast[qb % 2]
            S.op("dve", lambda e, pd=pd: e.reciprocal(out=rden.ap, in_=pd.ap), reads=[pd.k], writes=[rden.k])
            S.op("dve", lambda e, po=po: e.tensor_tensor(out=o1.ap, in0=po.ap, in1=rden.ap, op=ALU.mult),
                 reads=[po.k, rden.k], writes=[o1.k])
            S.op("pool", lambda e, ya=ya, qb=qb, sl=sl: e.tensor_tensor(out=ya.ap, in0=o1.ap,
                                                                       in1=SG[sl].ap[:, qb * 512:(qb + 1) * 512], op=ALU.mult),
                 reads=[o1.k, SG[sl].k], writes=[ya.k])
            S.dma("sp", "yaw%d" % (qb % 2), YCd[8 + h, :, qb * 512:(qb + 1) * 512], ya.ap, reads=[ya.k],
                  writes=[("YCd", 8 + h, qb)])
            if qb == 0 and h + 1 < 8:
                head_loads(h + 1)

    if STOP == 4:
        return finish()

    state["ptr"] = PERS_END
    gate_bc = sb([2048])
    S.dma("sp", "gbr", gate_bc.ap, GBd, reads=["GBd"], writes=[gate_bc.k])
    ycT = [sb([16, 512], BF16), sb([16, 512], BF16)]
    xo = [sb([2048]), sb([2048])]
    ot = [sb([2048]), sb([2048])]
    pout = [ps(i) for i in range(8)]
    yc_keys = [("YCd", 0, v) for v in range(NSUPO)] + [("YCd", 8 + h, qb) for h in range(8) for qb in range(4)]
    out_keys = []
    def yc_load(v):
        S.dma("sp", "ycl%d" % (v % 2), ycT[v % 2].ap, YCd[:, :, v * 512:(v + 1) * 512].rearrange("c p n -> p c n"),
              reads=yc_keys, writes=[ycT[v % 2].k])

    def xo_load(j):
        S.dma("sp", "xo%d" % (j % 2), xo[j % 2].ap, x_own[j * 128:(j + 1) * 128, :], writes=[xo[j % 2].k])
    yc_load(0)
    xo_load(0)
    for v in range(NSUPO):
        yb = ycT[v % 2]
        if v + 1 < NSUPO:
            yc_load(v + 1)
        for ti in range(4):
            j = 4 * v + ti
            xb = xo[j % 2]
            ob = ot[j % 2]
            if j + 1 < NTO:
                xo_load(j + 1)
            for n in range(4):
                pb = pout[(j % 2) * 4 + n]

                def mo(e, n=n, pb=pb, ti=ti, yb=yb):
                    for c in range(16):
                        r = e.matmul(pb.ap, lhsT=yb.ap[:, c, ti * 128:(ti + 1) * 128], rhs=wout.ap[:, c, n * 512:(n + 1) * 512],
                                     start=(c == 0), stop=(c == 15))
                    return r
                S.op("pe", mo, c=4.8, reads=[yb.k, wout.k], writes=[pb.k])
                cs = slice(n * 512, (n + 1) * 512)
                S.op("dve", lambda e, pb=pb, ob=ob, cs=cs: e.tensor_tensor(out=ob.ap[:, cs], in0=pb.ap, in1=gate_bc.ap[:, cs],
                                                                          op=ALU.mult),
                     reads=[pb.k, gate_bc.k], writes=[ob.sub(n * 2048, 2048)])
                S.op("pool", lambda e, ob=ob, xb=xb, cs=cs: e.tensor_tensor(out=ob.ap[:, cs], in0=ob.ap[:, cs], in1=xb.ap[:, cs],
                                                                           op=ALU.add),
                     reads=[ob.sub(n * 2048, 2048), xb.k], writes=[ob.sub(n * 2048, 2048)])
            S.dma("sp", "ow%d" % (j % 2), out_d[j * 128:(j + 1) * 128, :], ob.ap,
                  reads=[ob.k], writes=[("out", j)])
            out_keys.append(("out", j))
    return finish()


_NC_CACHE = {}


def _own_tiles(s):
    return [2 * j + s for j in range(NTO)]


def _prep(x, c, positions, w_ada, b_ada, w_in, conv_w, conv_b, w_rg_a, b_rg_a, w_rg_x, b_rg_x, lru_lambda,
           q_a_norm, w_uq, kv_a_norm, w_ukv, q_norm_nope, q_norm_rope, k_norm_nope, k_norm_rope, w_out):
    f32 = np.float32
    x = np.asarray(x, f32)
    c = np.asarray(c, f32)
    positions = np.asarray(positions, np.int32)
    g = lambda a: np.ascontiguousarray(np.asarray(a, f32)[0])
    w_ada_, b_ada_, w_in_ = g(w_ada), g(b_ada), g(w_in)
    conv_w_, conv_b_ = g(conv_w), g(conv_b)
    w_rg_a_, b_rg_a_, w_rg_x_, b_rg_x_, lam_ = g(w_rg_a), g(b_rg_a), g(w_rg_x), g(b_rg_x), g(lru_lambda)
    q_a_, w_uq_, kv_a_, w_ukv_ = g(q_a_norm), g(w_uq), g(kv_a_norm), g(w_ukv)
    qnn_, qnr_, knn_, knr_, w_out_ = g(q_norm_nope), g(q_norm_rope), g(k_norm_nope), g(k_norm_rope), g(w_out)

    ident = np.eye(128, dtype=f32)
    tri = (np.arange(128)[:, None] <= np.arange(128)[None, :]).astype(f32)
    ones = np.ones((128, 128), f32)
    zeros = np.zeros((128, 128), f32)
    inv_freq = (1.0 / (10000.0 ** (np.arange(0, 64, 2, dtype=np.float32) / 64.0))).astype(f32)

    chan = lambda v: np.ascontiguousarray(v.reshape(8, 128).T)
    rnn = np.stack([chan(conv_w_[0]), chan(conv_w_[1]), chan(conv_w_[2]), chan(conv_w_[3]), chan(conv_b_),
                    chan(b_rg_a_), chan(b_rg_x_), chan(lam_)], axis=2)
    b_ada_rep = np.ascontiguousarray(np.broadcast_to(b_ada_[None, :], (128, 3 * D)))

    in_maps = []
    for core in range(8):
        b, s = core // 2, core % 2
        own = _own_tiles(s)
        xb = x[b]
        x_own = np.ascontiguousarray(xb.reshape(NT, 128, D)[own].reshape(NTO * 128, D))
        par = np.zeros((128, 256), f32)
        par[:, 0:16] = c[b].reshape(128, 16)
        par[:, 16:80] = rnn.reshape(128, 64)
        par[:, 80:84] = q_a_.reshape(4, 128).T
        par[:, 84:88] = kv_a_.reshape(4, 128).T
        par[:, 88] = qnn_
        par[:, 89] = knn_
        par[:, 90] = 1.0 if s == 0 else 0.0
        par[:, 91] = 0.0 if s == 0 else 1.0
        par[:, 92:156] = qnr_[None, :]
        par[:, 156:220] = knr_[None, :]
        par[:, 220:252] = inv_freq[None, :]
        cst = np.concatenate([ident, tri if s == 0 else ones, zeros if s == 0 else tri], axis=1)
        pos_t = positions[b].reshape(NT, 128).T
        posi = np.ascontiguousarray(np.concatenate([pos_t, pos_t[:, own]], axis=1)).astype(np.int32)
        in_maps.append(dict(
            x_all=np.ascontiguousarray(xb), x_own=x_own, par=par, cst=np.ascontiguousarray(cst), posi=posi,
            w_ada=w_ada_, b_ada=b_ada_rep, w_in=w_in_, w_rg_a=w_rg_a_, w_rg_x=w_rg_x_, w_uq=w_uq_, w_ukv=w_ukv_,
            w_out=w_out_))
    return in_maps


def kernel(**inputs):
    f32 = np.float32
    in_maps = _prep(**inputs)
    if "nc" not in _NC_CACHE:
        _NC_CACHE["nc"] = build_program()
    nc = _NC_CACHE["nc"]
    res = run_bass_kernel_spmd(nc, in_maps, core_ids=list(range(8)))
    out = np.empty((4, SEQ, D), f32)
    for core in range(8):
        b, s = core // 2, core % 2
        o = np.asarray(res.results[core]["out"], f32).reshape(NTO, 128, D)
        out[b].reshape(NT, 128, D)[_own_tiles(s)] = o
    return out
```

```python
import bisect
import contextlib
import math
import numpy as np
import concourse.bass as bass
import concourse.mybir as mybir
from concourse.bass_utils import run_bass_kernel_spmd

F32 = mybir.dt.float32
BF16 = mybir.dt.bfloat16
I32 = mybir.dt.int32
AF = mybir.ActivationFunctionType
ALU = mybir.AluOpType
AX = mybir.AxisListType

D = 2048
SEQ = 4096
NT = 32
NTO = 16
NSUP = 8
NSUPO = 4
EPS = 1e-6
SM_SCALE = 192.0 ** -0.5
USE_POW = True
STOP = 99


class _Op:
    __slots__ = ("eng", "fn", "deps", "needs_inc", "cum", "dma_sem", "is_dma", "dma_grp", "tfin")


class _Rng:
    __slots__ = ("lo", "hi", "last_w", "readers")


class Sched:
    PAGE = 1024

    def __init__(self, nc):
        self.nc = nc
        self.ops = {e: [] for e in ("pe", "act", "dve", "pool", "sp")}
        self.named = {}
        self.ranges = {}
        self.pages = {}
        self.streams = {}
        self.n_streams = 0
        self.eng_free = {e: 0.0 for e in ("pe", "act", "dve", "pool", "sp")}
        self.step_fin = 0.0

    DEF_COST = {"pe": 1.0, "act": 0.65, "dve": 0.55, "pool": 0.8, "sp": 0.1}
    HOP = 0.4

    def _resolve(self, k):
        if not (isinstance(k, tuple) and len(k) == 3 and k[0] in ("sb", "ps")):
            r = self.named.get(k)
            if r is None:
                r = _Rng()
                r.lo = r.hi = 0
                r.last_w = None
                r.readers = []
                self.named[k] = r
            return [r]
        space, lo, hi = k
        sm = self.ranges.get(space)
        if sm is None:
            r = _Rng()
            r.last_w, r.readers = None, []
            sm = ([0, 1 << 30], [r])
            self.ranges[space] = sm
        bounds, segs = sm
        for x in (lo, hi):
            i = bisect.bisect_right(bounds, x) - 1
            if bounds[i] != x:
                o = segs[i]
                n = _Rng()
                n.last_w, n.readers = o.last_w, list(o.readers)
                bounds.insert(i + 1, x)
                segs.insert(i + 1, n)
        i = bisect.bisect_left(bounds, lo)
        j = bisect.bisect_left(bounds, hi)
        return segs[i:j]

    def _add(self, eng, fn, reads, writes, c=None, lat=0.0):
        op = _Op()
        op.eng, op.fn, op.deps = eng, fn, []
        op.needs_inc = False
        op.is_dma = False
        op.dma_sem = None
        op.dma_grp = None
        rr = []
        ww = []
        for k in reads:
            if isinstance(k, tuple) and len(k) == 3 and k[0] == "ps":
                ww.extend(self._resolve(k))
            else:
                rr.extend(self._resolve(k))
        for k in writes:
            ww.extend(self._resolve(k))
        deps = []
        for r in rr + ww:
            if r.last_w is not None:
                deps.append(r.last_w)
        for r in ww:
            deps.extend(r.readers)
        seen = set()
        for d in deps:
            if d is op or id(d) in seen:
                continue
            seen.add(id(d))
            if (not d.is_dma) and d.eng == "pe" and eng == "pe":
                continue
            op.deps.append(d)
            if not d.is_dma:
                d.needs_inc = True
        for r in ww:
            r.last_w = op
            r.readers = []
        wset = set(id(r) for r in ww)
        for r in rr:
            if id(r) not in wset:
                r.readers.append(op)
        self.ops[eng].append(op)
        ready = 0.0
        for d in op.deps:
            if d.tfin > ready:
                ready = d.tfin
        if op.deps:
            ready += self.HOP
        start = max(ready, self.eng_free[eng])
        cost = self.DEF_COST[eng] if c is None else c
        self.eng_free[eng] = start + cost
        op.tfin = start + cost + lat
        if op.tfin > self.step_fin:
            self.step_fin = op.tfin
        return op

    def op(self, eng, fn, reads=(), writes=(), c=None):
        return self._add(eng, fn, reads, writes, c=c)

    def dma(self, queue, stream, out, in_, reads=(), writes=(), cont=False, **kw):
        st = self.streams.get(stream)
        if st is None:
            st = dict(idx=self.n_streams, total=0, last=None, group_prev=None, grp=None)
            self.n_streams += 1
            self.streams[stream] = st

        def fn(e, out=out, in_=in_, kw=kw):
            return e.dma_start(out=out, in_=in_, **kw)

        op = self._add(queue, fn, reads, writes, c=0.1, lat=2.5)
        op.is_dma = True
        op.dma_sem = st["idx"]
        st["total"] += 16
        if cont and st["grp"] is not None:
            prev_group = st["group_prev"]
        else:
            prev_group = st["last"]
            st["group_prev"] = prev_group
            st["grp"] = [0]
        op.dma_grp = st["grp"]
        op.dma_grp[0] = st["total"]
        if prev_group is not None and all(d is not prev_group for d in op.deps):
            op.deps.append(prev_group)
        st["last"] = op
        return op

    def final_wait(self):
        op = self._add("sp", lambda e: None, (), ())
        for st in self.streams.values():
            if st["last"] is not None:
                op.deps.append(st["last"])
        for e in ("pe", "act", "dve", "pool"):
            if self.ops[e]:
                last = self.ops[e][-1]
                if not last.is_dma:
                    last.needs_inc = True
                    op.deps.append(last)

    def emit(self):
        nc = self.nc
        with contextlib.ExitStack() as es:
            esem = {e: es.enter_context(nc.semaphore("s_" + e)) for e in ("pe", "act", "dve", "pool")}
            dsem = [es.enter_context(nc.semaphore("d_%d" % i)) for i in range(self.n_streams)]
            for e in self.ops:
                c = 0
                for op in self.ops[e]:
                    if op.is_dma:
                        continue
                    if op.needs_inc:
                        c += 1
                    op.cum = c
            block = es.enter_context(nc.Block())

            def run(ename, e):
                waited = {}
                for op in self.ops[ename]:
                    for d in op.deps:
                        if d.is_dma and op.is_dma and d.dma_grp is op.dma_grp:
                            continue
                        if d.is_dma:
                            key, val, sem = ("d", d.dma_sem), d.dma_grp[0], dsem[d.dma_sem]
                        else:
                            key, val, sem = ("e", d.eng), d.cum, esem[d.eng]
                        if waited.get(key, 0) >= val:
                            continue
                        waited[key] = val
                        e.wait_ge(sem, val)
                    ins = op.fn(e)
                    if ins is None:
                        assert not op.needs_inc and not op.is_dma
                        continue
                    if op.is_dma:
                        ins.then_inc(dsem[op.dma_sem], 16)
                    elif op.needs_inc:
                        ins.then_inc(esem[op.eng], 1)

            @block.sync
            def _(e):
                run("sp", e)

            @block.tensor
            def _(e):
                run("pe", e)

            @block.scalar
            def _(e):
                run("act", e)

            @block.vector
            def _(e):
                run("dve", e)

            @block.gpsimd
            def _(e):
                run("pool", e)


class Buf:
    def __init__(self, ap, key):
        self.ap = ap
        self.k = key

    def sub(self, off, n):
        return (self.k[0], self.k[1] + off, self.k[1] + off + n)


def build_program():
    nc = bass.Bass("TRN2", target_bir_lowering=False)

    def din(name, shape, dt=F32):
        return nc.dram_tensor(name, list(shape), dt, kind="ExternalInput").ap()

    def dscr(name, shape, dt=BF16):
        return nc.dram_tensor(name, list(shape), dt, kind="Internal").ap()

    x_all = din("x_all", [SEQ, D])
    x_own = din("x_own", [NTO * 128, D])
    par_d = din("par", [128, 256])
    cst_d = din("cst", [128, 384])
    posi_d = din("posi", [128, 48], I32)
    w_ada = din("w_ada", [D, 3 * D])
    b_ada = din("b_ada", [128, 3 * D])
    w_in = din("w_in", [D, 4160])
    w_rg_a = din("w_rg_a", [16, 64, 64])
    w_rg_x = din("w_rg_x", [16, 64, 64])
    w_uq = din("w_uq", [512, 1536])
    w_ukv = din("w_ukv", [512, 2048])
    w_out = din("w_out", [D, D])
    out_d = nc.dram_tensor("out", [NTO * 128, D], F32, kind="ExternalOutput").ap()

    KTd = dscr("KTd", [8, 128, SEQ])
    Vd = dscr("Vd", [8, 128, NT, 128])
    QTd = dscr("QTd", [8, 128, NTO * 128])
    QRd = dscr("QRd", [8, 64, NTO * 128])
    SGd = dscr("SGd", [8, 128, NTO * 128])
    HOd = dscr("HOd", [128, 8, NTO * 128])
    YCd = dscr("YCd", [16, 128, NTO * 128])
    GBd = dscr("GBd", [128, 2048], F32)

    ARENA_BYTES = 207 * 1024
    arena = nc.alloc_sbuf_tensor("arena", [128, ARENA_BYTES // 4], F32)
    A = arena.ap() if hasattr(arena, "ap") else arena[:]
    banks = []
    for i in range(8):
        t = nc.alloc_psum_tensor("bank%d" % i, [128, 512], F32)
        banks.append(t.ap() if hasattr(t, "ap") else t[:])

    S = Sched(nc)

    def finish():
        S.final_wait()
        S.emit()
        return nc

    state = dict(ptr=0)

    def sb(shape, dt=F32, parts=128):
        n = 1
        for d_ in shape:
            n *= d_
        esz = 4 if dt in (F32, I32) else 2
        nbytes = (n * esz + 31) // 32 * 32
        lo = state["ptr"]
        hi = lo + nbytes
        assert hi <= ARENA_BYTES, ("SBUF overflow", hi)
        state["ptr"] = hi
        ap = A[:, lo // 4: lo // 4 + (n * esz + 3) // 4]
        if dt != F32:
            ap = ap.bitcast(dt)
            if esz == 2 and (n * esz) % 4:
                ap = ap[:, 0:n]
        if len(shape) == 2:
            ap = ap.rearrange("p (a b) -> p a b", a=shape[0])
        elif len(shape) == 3:
            ap = ap.rearrange("p (a b c) -> p a b c", a=shape[0], b=shape[1])
        return Buf(ap, ("sb", lo, hi))

    def ps(bank, nbanks=1, dt=F32, shape=None, off=0, nbytes=None):
        lo = bank * 2048 + off
        if nbytes is None:
            nbytes = nbanks * 2048 - off
        if nbanks == 1:
            ap = banks[bank]
            ap = ap[:, off // 4: (off + nbytes) // 4]
        else:
            raise AssertionError
        if dt != F32:
            ap = ap.bitcast(dt)
        if shape is not None and len(shape) == 2:
            ap = ap.rearrange("p (a b) -> p a b", a=shape[0])
        return Buf(ap, ("ps", bank * 2048, bank * 2048 + 2048))

    par = sb([256])
    cstf = sb([384])
    posi = sb([48], I32)
    identb = sb([128], BF16)
    maskA = sb([128], BF16)
    maskB = sb([128], BF16)
    onesb = sb([128], BF16)
    neghalf = sb([16])
    onesf = sb([8])
    hb_a = sb([8])
    hb_x = sb([8])
    scale1 = sb([16])
    shiftp = sb([16])
    cl_half = sb([8])
    cl_one = sb([8])
    KpeT = sb([SEQ], BF16)
    cosA = sb([32, 32])
    sinA = sb([32, 32])
    cosO = sb([16, 32])
    sinO = sb([16, 32])
    hist = sb([8, 3])
    hstate = sb([8])
    PERS_END = state["ptr"]

    P_ = par.ap
    c_par = P_[:, 0:16]

    def rnn_par(kidx):
        return P_[:, 16:80].rearrange("p (c k) -> p c k", k=8)[:, :, kidx]

    gq_a = P_[:, 80:84]
    gkv_a = P_[:, 84:88]
    gqn = P_[:, 88:89]
    gkn = P_[:, 89:90]
    sel0 = P_[:, 90:91]
    sel1 = P_[:, 91:92]
    gqr_bc = P_[:, 92:156]
    gkr_bc = P_[:, 156:220]
    invf = P_[:, 220:252]
    identf = cstf.ap[:, 0:128]

    S.dma("sp", "ld0", par.ap, par_d, writes=[par.k])
    S.dma("sp", "ld0", cstf.ap, cst_d, writes=[cstf.k], cont=True)
    S.dma("sp", "ld0", posi.ap, posi_d, writes=[posi.k], cont=True)
    S.op("dve", lambda e: e.tensor_copy(out=identb.ap, in_=cstf.ap[:, 0:128]), reads=[cstf.k], writes=[identb.k])
    S.op("dve", lambda e: e.tensor_copy(out=maskA.ap, in_=cstf.ap[:, 128:256]), reads=[cstf.k], writes=[maskA.k])
    S.op("dve", lambda e: e.tensor_copy(out=maskB.ap, in_=cstf.ap[:, 256:384]), reads=[cstf.k], writes=[maskB.k])
    S.op("pool", lambda e: e.memset(onesb.ap, 1.0), writes=[onesb.k])
    S.op("pool", lambda e: e.memset(neghalf.ap, -0.5), writes=[neghalf.k])
    S.op("pool", lambda e: e.memset(onesf.ap, 1.0), writes=[onesf.k])
    S.op("pool", lambda e: e.memset(hist.ap, 0.0), writes=[hist.k])
    S.op("pool", lambda e: e.memset(hstate.ap, 0.0), writes=[hstate.k])
    S.op("pool", lambda e: e.memset(KpeT.ap[64:128, :], 0.0), writes=["kpeT_pad"])

    def rsqrt_small(out_ap, in_ap, n, rk, wk):
        if USE_POW:
            S.op("pool", lambda e: e.tensor_tensor(out=out_ap, in0=in_ap, in1=neghalf.ap[:, 0:n], op=ALU.pow),
                 reads=list(rk) + [neghalf.k], writes=wk)
        else:
            S.op("act", lambda e: e.activation(out=out_ap, in_=in_ap, func=AF.Sqrt), reads=rk, writes=wk)
            S.op("dve", lambda e: e.reciprocal(out=out_ap, in_=out_ap), reads=wk, writes=wk)

    state["ptr"] = PERS_END + 71680
    cact = sb([16])
    cbc = sb([16, 128], BF16)
    tmpe = sb([8])
    wad = [sb([16, 512], BF16), sb([16, 512], BF16)]
    bad = [sb([512]), sb([512])]
    modbc = sb([4096])
    gate_tmp = sb([2048])
    dtmp = sb([16, 128])
    angt = sb([32, 32])
    angk = sb([32, 32], I32)
    angf = sb([32, 32])
    posf = sb([48])

    lam = rnn_par(7)
    S.op("act", lambda e: e.activation(out=tmpe.ap, in_=lam, func=AF.Exp, scale=-1.0), reads=[par.k], writes=[tmpe.k])
    S.op("act", lambda e: e.activation(out=tmpe.ap, in_=tmpe.ap, func=AF.Ln, bias=onesf.ap[:, 0:1]), reads=[tmpe.k, onesf.k], writes=[tmpe.k])
    S.op("dve", lambda e: e.tensor_scalar(out=hb_a.ap, in0=rnn_par(5), scalar1=0.5, scalar2=None, op0=ALU.mult),
         reads=[par.k], writes=[hb_a.k])
    S.op("dve", lambda e: e.tensor_scalar(out=hb_x.ap, in0=rnn_par(6), scalar1=0.5, scalar2=None, op0=ALU.mult),
         reads=[par.k], writes=[hb_x.k])
    S.op("dve", lambda e: e.tensor_scalar(out=cl_one.ap, in0=tmpe.ap, scalar1=-8.0, scalar2=None, op0=ALU.mult),
         reads=[tmpe.k], writes=[cl_one.k])
    S.op("dve", lambda e: e.tensor_scalar(out=cl_half.ap, in0=tmpe.ap, scalar1=-4.0, scalar2=None, op0=ALU.mult),
         reads=[tmpe.k], writes=[cl_half.k])
    S.op("act", lambda e: e.activation(out=cact.ap, in_=c_par, func=AF.Silu), reads=[par.k], writes=[cact.k])
    S.op("dve", lambda e: e.tensor_copy(out=cbc.ap, in_=cact.ap.unsqueeze(2).broadcast_to([128, 16, 128])),
         reads=[cact.k], writes=[cbc.k])

    def rope_tables(pos_ap, n, cos_b, sin_b):
        pf = posf.ap[:, 0:n]
        S.op("dve", lambda e: e.tensor_copy(out=pf, in_=pos_ap), reads=[posi.k], writes=[posf.k])
        a3 = angt.ap[:, 0:n, :]
        k3 = angk.ap[:, 0:n, :]
        f3 = angf.ap[:, 0:n, :]
        S.op("dve", lambda e: e.tensor_tensor(out=a3, in0=pf.unsqueeze(2).broadcast_to([128, n, 32]),
                                              in1=invf.unsqueeze(1).broadcast_to([128, n, 32]), op=ALU.mult),
             reads=[posf.k, par.k], writes=[angt.k])
        inv2pi = 1.0 / (2.0 * math.pi)
        for (dst, shift) in ((sin_b, 0.0), (cos_b, 0.25)):
            S.op("dve", lambda e, shift=shift: e.tensor_scalar(out=f3, in0=a3, scalar1=inv2pi, scalar2=shift,
                                                              op0=ALU.mult, op1=ALU.add),
                 reads=[angt.k], writes=[angf.k])
            S.op("dve", lambda e: e.tensor_copy(out=k3, in_=f3), reads=[angf.k], writes=[angk.k])
            S.op("dve", lambda e, dst=dst: e.tensor_copy(out=dst.ap, in_=k3), reads=[angk.k], writes=[dst.k])
            S.op("dve", lambda e, dst=dst: e.tensor_tensor(out=dst.ap, in0=f3, in1=dst.ap, op=ALU.subtract),
                 reads=[angf.k, dst.k], writes=[dst.k])
            S.op("act", lambda e, dst=dst: e.activation(out=dst.ap, in_=dst.ap, func=AF.Sin, scale=6.283185),
                 reads=[dst.k], writes=[dst.k])

    rope_tables(posi.ap[:, 0:32], 32, cosA, sinA)
    rope_tables(posi.ap[:, 32:48], 16, cosO, sinO)

    w_ada_v = w_ada.rearrange("(p j) n -> p j n", j=16)
    pmod = [ps(0), ps(1)]
    for nci in range(12):
        sl = nci % 2
        S.dma("pool", "wad%d" % sl, wad[sl].ap, w_ada_v[:, :, nci * 512:(nci + 1) * 512], writes=[wad[sl].k])
        S.dma("sp", "bad%d" % sl, bad[sl].ap, b_ada[:, nci * 512:(nci + 1) * 512], writes=[bad[sl].k])

        def mm(e, sl=sl):
            for j in range(16):
                r = e.matmul(pmod[sl].ap, lhsT=cbc.ap[:, j, :], rhs=wad[sl].ap[:, j, :], start=(j == 0), stop=(j == 15))
            return r
        S.op("pe", mm, c=4.8, reads=[cbc.k, wad[sl].k], writes=[pmod[sl].k])
        if nci < 8:
            dst, dk = modbc.ap[:, nci * 512:(nci + 1) * 512], modbc.k
        else:
            dst, dk = gate_tmp.ap[:, (nci - 8) * 512:(nci - 7) * 512], gate_tmp.k
        S.op("dve", lambda e, sl=sl, dst=dst: e.tensor_tensor(out=dst, in0=pmod[sl].ap, in1=bad[sl].ap, op=ALU.add),
             reads=[pmod[sl].k, bad[sl].k], writes=[dk])
    S.dma("sp", "gbw", GBd, gate_tmp.ap, reads=[gate_tmp.k], writes=["GBd"])
    for (dst, off, addone) in ((shiftp, 0, False), (scale1, 2048, True)):
        src = modbc.ap[:, off:off + 2048].rearrange("p (j n) -> p j n", j=16)
        S.op("dve", lambda e, src=src: e.tensor_tensor(out=dtmp.ap, in0=src,
                                                        in1=identf.unsqueeze(1).broadcast_to([128, 16, 128]), op=ALU.mult),
             reads=[modbc.k, cstf.k], writes=[dtmp.k])
        S.op("dve", lambda e, dst=dst: e.tensor_reduce(out=dst.ap, in_=dtmp.ap, axis=AX.X, op=ALU.add),
             reads=[dtmp.k], writes=[dst.k])
        if addone:
            S.op("dve", lambda e, dst=dst: e.tensor_scalar(out=dst.ap, in0=dst.ap, scalar1=1.0, scalar2=None, op0=ALU.add),
                 reads=[dst.k], writes=[dst.k])

    if STOP == 0:
        return finish()

    def interleave(gens, weights=None):
        active = [g for g in gens if g is not None]
        now = min(S.eng_free[e] for e in ("pe", "act", "dve", "pool"))
        chain_t = {id(g): now for g in active}
        while active:
            g = min(active, key=lambda gg: chain_t[id(gg)])
            S.step_fin = chain_t[id(g)]
            try:
                next(g)
            except StopIteration:
                active.remove(g)
                continue
            chain_t[id(g)] = S.step_fin

    def gen_norm_tile(xsrc_ap, xt, xn, ssb, hT, ti, pst):
        S.dma("sp", "x%d" % id(xt), xt.ap, xsrc_ap, writes=[xt.k])
        S.op("act", lambda e: e.activation(out=xn.ap, in_=xt.ap, func=AF.Square, accum_out=ssb.ap[:, 0:1]),
             reads=[xt.k], writes=[xn.k, ssb.k], c=1.6)
        yield
        S.op("dve", lambda e: e.tensor_scalar(out=ssb.ap[:, 1:2], in0=ssb.ap[:, 0:1], scalar1=1.0 / D, scalar2=EPS,
                                              op0=ALU.mult, op1=ALU.add), reads=[ssb.k], writes=[ssb.k])
        rsqrt_small(ssb.ap[:, 2:3], ssb.ap[:, 1:2], 1, [ssb.k], [ssb.k])
        yield
        S.op("pool", lambda e: e.tensor_scalar(out=xn.ap, in0=xt.ap, scalar1=ssb.ap[:, 2:3], scalar2=0.0, op0=ALU.mult, op1=ALU.add),
             reads=[xt.k, ssb.k], writes=[xn.k], c=4.0)
        yield
        hTe, hTo = hT

        def tr0(e):
            for j in range(8):
                r = e.transpose(out=pst[0].ap[:, j, :], in_=xn.ap[:, j * 128:(j + 1) * 128], identity=identb.ap)
            return r

        def tr1(e):
            for j in range(8, 16):
                r = e.transpose(out=pst[1].ap[:, j - 8, :], in_=xn.ap[:, j * 128:(j + 1) * 128], identity=identb.ap)
            return r
        S.op("pe", tr0, c=0.9, reads=[xn.k, identb.k], writes=[pst[0].k])
        S.op("pe", tr1, c=0.9, reads=[xn.k, identb.k], writes=[pst[1].k])
        yield

        def ev_act(e):
            for j in range(0, 8):
                r = e.activation(out=hTe.ap[:, j, ti * 128:(ti + 1) * 128], in_=pst[0].ap[:, j, :],
                                 func=AF.Identity, scale=scale1.ap[:, j:j + 1], bias=shiftp.ap[:, j:j + 1])
            return r

        def ev_dve(e):
            for j in range(8, 16):
                r = e.tensor_scalar(out=hTo.ap[:, j - 8, ti * 128:(ti + 1) * 128], in0=pst[1].ap[:, j - 8, :],
                                    scalar1=scale1.ap[:, j:j + 1], scalar2=shiftp.ap[:, j:j + 1],
                                    op0=ALU.mult, op1=ALU.add)
            return r
        S.op("act", ev_act, reads=[pst[0].k, scale1.k, shiftp.k], writes=[hTe.k], c=2.0)
        S.op("dve", ev_dve, reads=[pst[1].k, scale1.k, shiftp.k], writes=[hTo.k], c=2.0)
        yield

    def hsel(hT, j):
        return hT[j // 8].ap[:, j % 8]

    def pe_group(out_ap, key, lhs_fn, rhs_fn, reads, nsplit=4):
        per = 16 // nsplit
        for q in range(nsplit):
            def f(e, q=q):
                for j in range(q * per, (q + 1) * per):
                    r = e.matmul(out_ap, lhsT=lhs_fn(j), rhs=rhs_fn(j), start=(j == 0), stop=(j == 15))
                return r
            S.op("pe", f, c=0.3 * per, reads=reads, writes=[key])
            yield

    state["ptr"] = PERS_END
    w1 = sb([16, 1600], BF16)
    wukv = sb([4, 2048], BF16)
    wrg = sb([8, 2, 128], BF16)
    xts = [sb([2048]), sb([2048])]
    xn = sb([2048], BF16)
    ssb = sb([4])
    hTs = [(sb([8, 512], BF16), sb([8, 512], BF16)), (sb([8, 512], BF16), sb([8, 512], BF16))]
    XR = sb([2, 515])
    XC = sb([2, 512])
    XCB = sb([2, 512], BF16)
    THR = sb([2, 512])
    THI = sb([2, 512])
    A2 = sb([2, 512])
    HB = sb([2, 512])
    tmpo = sb([2, 2, 128])
    hst = sb([8, 256], BF16)
    kvcb = sb([512], BF16)
    kvcT = sb([4, 128], BF16)
    U = sb([1024])
    sqs = sb([4, 128])
    kvs = sb([16])
    ssn = sb([8])
    rkn = sb([8])
    kn = sb([8, 128], BF16)
    vst = sb([8, 4, 128], BF16)
    KTst = sb([8, 512], BF16)
    krn = sb([2, 32])
    ropA = sb([2, 32])
    ropB = sb([2, 32])
    kpe = sb([64], BF16)

    def cik(buf, ci, n, esz=4):
        return buf.sub(ci * n * esz, n * esz)

    w_in_v = w_in.rearrange("(j p) n -> p j n", p=128)
    S.dma("pool", "w1", w1.ap[:, :, 0:1024], w_in_v[:, :, 0:1024], writes=[w1.k])
    S.dma("pool", "w1", w1.ap[:, :, 1024:1600], w_in_v[:, :, 2560:3136], writes=[w1.k], cont=True)
    S.op("pool", lambda e: e.memset(wrg.ap, 0.0), writes=[wrg.k])
    for gi, wsrc in enumerate((w_rg_a, w_rg_x)):
        for hh in range(2):
            src = wsrc.rearrange("(c two) i j -> two i c j", two=2)[hh]
            S.dma("pool", "wrg", wrg.ap[hh * 64:(hh + 1) * 64, :, gi, hh * 64:(hh + 1) * 64], src, writes=[wrg.k])
    w_ukv_v = w_ukv.rearrange("(c p) n -> p c n", p=128)
    for ck in range(4):
        stg = xts[ck % 2]
        S.dma("sp", "wst%d" % (ck % 2), stg.ap, w_ukv_v[:, ck, :], writes=[stg.k])
        S.op("act", lambda e, ck=ck, stg=stg: e.activation(out=wukv.ap[:, ck, :], in_=stg.ap, func=AF.Identity,
                                                          scale=gkv_a[:, ck:ck + 1]),
             reads=[stg.k, par.k], writes=[wukv.k])

    pst = [ps(0, dt=BF16, shape=[8, 128]), ps(1, dt=BF16, shape=[8, 128])]
    pxr = [ps(2), ps(3)]
    pgr = ps(4)
    pgi = ps(5)
    pkv = ps(6)
    pmisc_kr = ps(7, off=0, nbytes=256)
    pmisc_kT = ps(7, off=256, nbytes=1024, dt=BF16, shape=[4, 128])
    pmisc_kp = ps(7, off=1280, nbytes=256, dt=BF16)
    pup = [ps(6), ps(7)]
    pKT = ps(6, dt=BF16, shape=[8, 128])

    cw = [rnn_par(k) for k in range(4)]
    cb_ = rnn_par(4)

    def gen_norm_super(u):
        for ti in range(4):
            t = 4 * u + ti
            yield from gen_norm_tile(x_all[t * 128:(t + 1) * 128, :], xts[t % 2], xn, ssb, hTs[u % 2], ti, pst)

    def gen_rnn(u):
        hT = hTs[u % 2]
        hkeys = [hT[0].k, hT[1].k]
        for g in range(4):
            ccs = [2 * g, 2 * g + 1]
            for ci, cc in enumerate(ccs):
                S.op("pool", lambda e, ci=ci, cc=cc: e.tensor_copy(out=XR.ap[:, ci, 0:3], in_=hist.ap[:, cc, :]),
                     reads=[hist.sub(cc * 12, 12)], writes=[XR.sub(ci * 2060, 12)])
                px = pxr[ci]

                yield from pe_group(px.ap, px.k, (lambda j, cc=cc: w1.ap[:, j, cc * 128:(cc + 1) * 128]),
                                    (lambda j: hsel(hT, j)), [w1.k] + hkeys)
            for ci, cc in enumerate(ccs):
                px = pxr[ci]
                S.op("act", lambda e, ci=ci, px=px: e.activation(out=XR.ap[:, ci, 3:515], in_=px.ap, func=AF.Copy),
                     reads=[px.k], writes=[XR.sub(ci * 2060 + 12, 2048)])
                yield
            for ci, cc in enumerate(ccs):
                S.op("pool", lambda e, ci=ci, cc=cc: e.tensor_copy(out=hist.ap[:, cc, :], in_=XR.ap[:, ci, 512:515]),
                     reads=[XR.sub(ci * 2060 + 12, 2048)], writes=[hist.sub(cc * 12, 12)])
                S.op("dve", lambda e, ci=ci, cc=cc: e.tensor_scalar(out=XC.ap[:, ci, :], in0=XR.ap[:, ci, 3:515],
                                                                  scalar1=cw[3][:, cc:cc + 1], scalar2=cb_[:, cc:cc + 1],
                                                                  op0=ALU.mult, op1=ALU.add),
                     reads=[cik(XR, ci, 515), par.k], writes=[cik(XC, ci, 512)])
                yield
            for k in (2, 1, 0):
                for ci, cc in enumerate(ccs):
                    S.op("dve", lambda e, ci=ci, cc=cc, k=k: e.scalar_tensor_tensor(
                        out=XC.ap[:, ci, :], in0=XR.ap[:, ci, k:k + 512], scalar=cw[k][:, cc:cc + 1],
                        in1=XC.ap[:, ci, :], op0=ALU.mult, op1=ALU.add),
                        reads=[cik(XR, ci, 515), cik(XC, ci, 512), par.k], writes=[cik(XC, ci, 512)])
                yield
            for ci, cc in enumerate(ccs):
                S.op("dve", lambda e, ci=ci: e.tensor_copy(out=XCB.ap[:, ci, :], in_=XC.ap[:, ci, :]),
                     reads=[cik(XC, ci, 512)], writes=[cik(XCB, ci, 512, 2)])
                yield
            for ci, cc in enumerate(ccs):
                S.op("pe", lambda e, ci=ci, cc=cc: e.matmul(pgr.ap, lhsT=wrg.ap[:, cc, 0, :], rhs=XCB.ap[:, ci, :],
                                                            start=True, stop=True),
                     reads=[wrg.k, cik(XCB, ci, 512, 2)], writes=[pgr.k])
                S.op("act", lambda e, ci=ci, cc=cc: e.activation(out=THR.ap[:, ci, :], in_=pgr.ap, func=AF.Tanh, scale=0.5,
                                                                bias=hb_a.ap[:, cc:cc + 1]),
                     reads=[pgr.k, hb_a.k], writes=[cik(THR, ci, 512)])
                S.op("pe", lambda e, ci=ci, cc=cc: e.matmul(pgi.ap, lhsT=wrg.ap[:, cc, 1, :], rhs=XCB.ap[:, ci, :],
                                                            start=True, stop=True),
                     reads=[wrg.k, cik(XCB, ci, 512, 2)], writes=[pgi.k])
                S.op("act", lambda e, ci=ci, cc=cc: e.activation(out=THI.ap[:, ci, :], in_=pgi.ap, func=AF.Tanh, scale=0.5,
                                                                bias=hb_x.ap[:, cc:cc + 1]),
                     reads=[pgi.k, hb_x.k], writes=[cik(THI, ci, 512)])
                yield
            for ci, cc in enumerate(ccs):
                S.op("act", lambda e, ci=ci, cc=cc: e.activation(out=A2.ap[:, ci, :], in_=THR.ap[:, ci, :], func=AF.Exp,
                                                               scale=cl_one.ap[:, cc:cc + 1], bias=cl_one.ap[:, cc:cc + 1]),
                     reads=[cik(THR, ci, 512), cl_one.k], writes=[cik(A2, ci, 512)])
                S.op("act", lambda e, ci=ci, cc=cc: e.activation(out=THR.ap[:, ci, :], in_=THR.ap[:, ci, :], func=AF.Exp,
                                                               scale=cl_half.ap[:, cc:cc + 1], bias=cl_half.ap[:, cc:cc + 1]),
                     reads=[cik(THR, ci, 512), cl_half.k], writes=[cik(THR, ci, 512)])
                S.op("dve", lambda e, ci=ci: e.scalar_tensor_tensor(out=THI.ap[:, ci, :], in0=THI.ap[:, ci, :], scalar=1.0,
                                                                  in1=XC.ap[:, ci, :], op0=ALU.add, op1=ALU.mult),
                     reads=[cik(THI, ci, 512), cik(XC, ci, 512)], writes=[cik(THI, ci, 512)])
                yield
            S.op("act", lambda e: e.activation(out=A2.ap, in_=A2.ap, func=AF.Sqrt, scale=-1.0, bias=onesf.ap[:, 0:1]),
                 reads=[A2.k, onesf.k], writes=[A2.k])
            yield
            S.op("dve", lambda e: e.scalar_tensor_tensor(out=THI.ap, in0=A2.ap, scalar=0.5, in1=THI.ap,
                                                         op0=ALU.mult, op1=ALU.mult),
                 reads=[A2.k, THI.k], writes=[THI.k])
            yield
            for ci, cc in enumerate(ccs):
                S.op("dve", lambda e, ci=ci, cc=cc: e.tensor_tensor_scan(
                    out=HB.ap[:, ci, :], data0=THR.ap[:, ci, :], data1=THI.ap[:, ci, :],
                    initial=hstate.ap[:, cc:cc + 1], op0=ALU.mult, op1=ALU.add),
                    reads=[cik(THR, ci, 512), cik(THI, ci, 512), hstate.sub(cc * 4, 4)], writes=[cik(HB, ci, 512)])
                yield
            for ci, cc in enumerate(ccs):
                S.op("pool", lambda e, ci=ci, cc=cc: e.tensor_copy(out=hstate.ap[:, cc:cc + 1], in_=HB.ap[:, ci, 511:512]),
                     reads=[cik(HB, ci, 512)], writes=[hstate.sub(cc * 4, 4)])
                hv = HB.ap[:, ci, :].rearrange("p (a two n) -> p a two n", two=2, n=128)
                S.op("pool", lambda e, hv=hv, ci=ci: e.tensor_scalar(out=tmpo.ap[:, ci], in0=hv[:, :, 0, :], scalar1=sel0,
                                                                    scalar2=0.0, op0=ALU.mult, op1=ALU.add),
                     reads=[cik(HB, ci, 512), par.k], writes=[cik(tmpo, ci, 256)])
                yield
            for ci, cc in enumerate(ccs):
                hv = HB.ap[:, ci, :].rearrange("p (a two n) -> p a two n", two=2, n=128)
                S.op("dve", lambda e, hv=hv, cc=cc, ci=ci: e.scalar_tensor_tensor(
                    out=hst.ap[:, cc, :].rearrange("p (a n) -> p a n", n=128), in0=hv[:, :, 1, :], scalar=sel1,
                    in1=tmpo.ap[:, ci], op0=ALU.mult, op1=ALU.add),
                    reads=[cik(HB, ci, 512), cik(tmpo, ci, 256), par.k], writes=[hst.sub(cc * 512, 512)])
                yield
        S.dma("sp", "hod", HOd[:, :, u * 256:(u + 1) * 256], hst.ap, reads=[hst.k], writes=[("HOd", u)])
        yield

    def gen_kr_tile(u, ti):
        hT = hTs[u % 2]
        hkeys = [hT[0].k, hT[1].k]
        if True:
            t = 4 * u + ti
            tcol = slice(ti * 128, (ti + 1) * 128)

            def mkr(e, tcol=tcol):
                for j in range(16):
                    r = e.matmul(pmisc_kr.ap, lhsT=hsel(hT, j)[:, tcol], rhs=w1.ap[:, j, 1536:1600], start=(j == 0), stop=(j == 15))
                return r
            S.op("pe", mkr, c=0.6, reads=[w1.k] + hkeys, writes=[pmisc_kr.k])
            yield
            S.op("act", lambda e: e.activation(out=ropA.ap, in_=pmisc_kr.ap.rearrange("p (a n) -> p a n", a=2),
                                               func=AF.Square, accum_out=kvs.ap[:, 4:5]),
                 reads=[pmisc_kr.k], writes=[ropA.k, kvs.sub(16, 4)])
            S.op("dve", lambda e: e.tensor_scalar(out=kvs.ap[:, 5:6], in0=kvs.ap[:, 4:5], scalar1=1.0 / 64, scalar2=EPS,
                                                  op0=ALU.mult, op1=ALU.add), reads=[kvs.sub(16, 4)], writes=[kvs.sub(20, 4)])
            yield
            rsqrt_small(kvs.ap[:, 6:7], kvs.ap[:, 5:6], 1, [kvs.sub(20, 4)], [kvs.sub(24, 4)])
            yield
            S.op("dve", lambda e: e.scalar_tensor_tensor(out=krn.ap.rearrange("p a n -> p (a n)"), in0=pmisc_kr.ap,
                                                         scalar=kvs.ap[:, 6:7], in1=gkr_bc, op0=ALU.mult, op1=ALU.mult),
                 reads=[pmisc_kr.k, kvs.sub(24, 4), par.k], writes=[krn.k])
            yield
            cos_t = cosA.ap[:, t, :].unsqueeze(1).broadcast_to([128, 2, 32])
            sin_t = sinA.ap[:, t, :].unsqueeze(1).broadcast_to([128, 2, 32])
            S.op("pool", lambda e, cos_t=cos_t: e.tensor_tensor(out=ropA.ap, in0=krn.ap, in1=cos_t, op=ALU.mult),
                 reads=[krn.k, cosA.k], writes=[ropA.k])
            S.op("pool", lambda e, sin_t=sin_t: e.tensor_tensor(out=ropB.ap, in0=krn.ap, in1=sin_t, op=ALU.mult),
                 reads=[krn.k, sinA.k], writes=[ropB.k])
            yield
            S.op("pool", lambda e: e.tensor_tensor(out=kpe.ap[:, 0:32], in0=ropA.ap[:, 0, :], in1=ropB.ap[:, 1, :], op=ALU.subtract),
                 reads=[ropA.k, ropB.k], writes=[kpe.k])
            S.op("pool", lambda e: e.tensor_tensor(out=kpe.ap[:, 32:64], in0=ropB.ap[:, 0, :], in1=ropA.ap[:, 1, :], op=ALU.add),
                 reads=[ropA.k, ropB.k], writes=[kpe.k])
            yield
            S.op("pe", lambda e: e.transpose(out=pmisc_kp.ap[0:64, :], in_=kpe.ap, identity=identb.ap),
                 reads=[kpe.k, identb.k], writes=[pmisc_kp.k])
            yield
            S.op("dve", lambda e, t=t: e.tensor_copy(out=KpeT.ap[0:64, t * 128:(t + 1) * 128], in_=pmisc_kp.ap[0:64, :]),
                 reads=[pmisc_kp.k], writes=[("kpeT", t)])
            yield

    def gen_kv(u):
        hT = hTs[u % 2]
        hkeys = [hT[0].k, hT[1].k]
        for ti in range(4):
            t = 4 * u + ti
            tcol = slice(ti * 128, (ti + 1) * 128)

            yield from pe_group(pkv.ap, pkv.k, (lambda j, tcol=tcol: hsel(hT, j)[:, tcol]),
                                (lambda j: w1.ap[:, j, 1024:1536]), [w1.k] + hkeys)

            S.op("act", lambda e: e.activation(out=U.ap[:, 0:512], in_=pkv.ap, func=AF.Square, accum_out=kvs.ap[:, 0:1]),
                 reads=[pkv.k], writes=[U.k, kvs.sub(0, 4)])
            S.op("act", lambda e: e.activation(out=kvcb.ap, in_=pkv.ap, func=AF.Copy), reads=[pkv.k], writes=[kvcb.k])
            yield
            yield from gen_kr_tile(u, ti)
            S.op("dve", lambda e: e.tensor_scalar(out=kvs.ap[:, 1:2], in0=kvs.ap[:, 0:1], scalar1=1.0 / 512, scalar2=EPS,
                                                  op0=ALU.mult, op1=ALU.add), reads=[kvs.sub(0, 4)], writes=[kvs.sub(4, 4)])
            rsqrt_small(kvs.ap[:, 2:3], kvs.ap[:, 1:2], 1, [kvs.sub(4, 4)], [kvs.sub(8, 4)])
            S.op("dve", lambda e: e.tensor_scalar(out=kvs.ap[:, 3:4], in0=kvs.ap[:, 1:2], scalar1=EPS, scalar2=None,
                                                  op0=ALU.mult), reads=[kvs.sub(4, 4)], writes=[kvs.sub(12, 4)])

            def trk(e):
                for ck in range(4):
                    r = e.transpose(out=pmisc_kT.ap[:, ck, :], in_=kvcb.ap[:, ck * 128:(ck + 1) * 128], identity=identb.ap)
                return r
            S.op("pe", trk, c=0.5, reads=[kvcb.k, identb.k], writes=[pmisc_kT.k])
            yield
            S.op("dve", lambda e: e.tensor_copy(out=kvcT.ap, in_=pmisc_kT.ap), reads=[pmisc_kT.k], writes=[kvcT.k])
            yield
            for hf in range(2):
                for n in range(2):
                    def mup(e, hf=hf, n=n):
                        for ck in range(4):
                            c0 = hf * 1024 + n * 512
                            r = e.matmul(pup[n].ap, lhsT=kvcT.ap[:, ck, :], rhs=wukv.ap[:, ck, c0:c0 + 512],
                                         start=(ck == 0), stop=(ck == 3))
                        return r
                    S.op("pe", mup, c=1.2, reads=[kvcT.k, wukv.k], writes=[pup[n].k])
                    S.op("act", lambda e, n=n: e.activation(out=U.ap[:, n * 512:(n + 1) * 512], in_=pup[n].ap, func=AF.Copy),
                         reads=[pup[n].k], writes=[U.sub(n * 2048, 2048)])
                    yield
                U3 = U.ap.rearrange("p (h c) -> p h c", h=4)
                Un = U3[:, :, 0:128]
                Uv = U3[:, :, 128:256]
                hs = slice(4 * hf, 4 * hf + 4)
                S.op("dve", lambda e, Un=Un: e.tensor_tensor(out=sqs.ap, in0=Un, in1=Un, op=ALU.mult), reads=[U.k], writes=[sqs.k])
                S.op("dve", lambda e, hs=hs: e.tensor_reduce(out=ssn.ap[:, hs], in_=sqs.ap, axis=AX.X, op=ALU.add),
                     reads=[sqs.k], writes=[ssn.k])
                yield
                S.op("dve", lambda e, hs=hs: e.tensor_scalar(out=ssn.ap[:, hs], in0=ssn.ap[:, hs], scalar1=1.0 / 128,
                                                             scalar2=kvs.ap[:, 3:4], op0=ALU.mult, op1=ALU.add),
                     reads=[ssn.k, kvs.sub(12, 4)], writes=[ssn.k])
                rsqrt_small(rkn.ap[:, hs], ssn.ap[:, hs], 4, [ssn.k], [rkn.k])
                yield
                S.op("dve", lambda e, Un=Un, hs=hs: e.tensor_tensor(out=kn.ap[:, hs, :], in0=Un,
                                                                    in1=rkn.ap[:, hs].unsqueeze(2).broadcast_to([128, 4, 128]),
                                                                    op=ALU.mult), reads=[U.k, rkn.k], writes=[kn.k])
                S.op("pool", lambda e, Uv=Uv, ti=ti, hs=hs: e.tensor_scalar(out=vst.ap[:, hs, ti, :], in0=Uv, scalar1=kvs.ap[:, 2:3],
                                                                            scalar2=0.0, op0=ALU.mult, op1=ALU.add),
                     reads=[U.k, kvs.sub(8, 4)], writes=[vst.k])
                yield

            def trK(e):
                for h in range(8):
                    r = e.transpose(out=pKT.ap[:, h, :], in_=kn.ap[:, h, :], identity=identb.ap)
                return r
            S.op("pe", trK, c=0.9, reads=[kn.k, identb.k], writes=[pKT.k])
            yield
            S.op("act", lambda e, tcol=tcol: e.activation(out=KTst.ap[:, :, tcol], in_=pKT.ap, func=AF.Identity, scale=gkn),
                 reads=[pKT.k, par.k], writes=[KTst.k])
            yield
        S.dma("sp", "ktd", KTd[:, :, u * 512:(u + 1) * 512].rearrange("h p n -> p h n"), KTst.ap,
              reads=[KTst.k], writes=[("KTd", u)])
        S.dma("sp", "vd", Vd[:, :, 4 * u:4 * u + 4, :].rearrange("h p t d -> p h t d"), vst.ap,
              reads=[vst.k], writes=[("Vd", u)])
        yield

    interleave([gen_norm_super(0)])
    for u in range(NSUP):
        interleave([gen_rnn(u), gen_kv(u), gen_norm_super(u + 1) if u + 1 < NSUP else None])

    if STOP == 1:
        return finish()

    HTd = dscr("HTd", [NSUPO, 2, 128, 8, 512])
    state["ptr"] = PERS_END
    w2 = sb([16, 1536], BF16)
    wuq = sb([4, 1536], BF16)
    xts2 = [sb([2048]), sb([2048])]
    xn2 = sb([2048], BF16)
    ssb2 = sb([4])
    hT2s = [(sb([8, 512], BF16), sb([8, 512], BF16)), (sb([8, 512], BF16), sb([8, 512], BF16))]
    hown = sb([8, 512], BF16)
    sgt = [sb([512]), sb([512])]
    yst = sb([8, 512], BF16)
    qcb = sb([512], BF16)
    qcT = sb([4, 128], BF16)
    Uq = sb([1536])
    sqq = sb([1536])
    qs = sb([8])
    ssqn = sb([8])
    ssqr = sb([8])
    rqn = sb([8])
    rqr = sb([8])
    qn = sb([8, 128], BF16)
    qrn = sb([8, 2, 32])
    rqA = sb([8, 2, 32])
    rqB = sb([8, 2, 32])
    qpe = sb([8, 2, 32], BF16)
    QTst = sb([8, 512], BF16)
    QRst = sb([4, 512], BF16)

    S.dma("pool", "w2", w2.ap, w_in_v[:, :, 1024:2560], writes=[w2.k])
    w_uq_v = w_uq.rearrange("(c p) n -> p c n", p=128)
    for ck in range(4):
        stg = xts2[ck % 2]
        S.dma("sp", "wst2%d" % (ck % 2), stg.ap[:, 0:1536], w_uq_v[:, ck, :], writes=[stg.k])
        S.op("act", lambda e, ck=ck, stg=stg: e.activation(out=wuq.ap[:, ck, :], in_=stg.ap[:, 0:1536], func=AF.Identity,
                                                          scale=gq_a[:, ck:ck + 1]),
             reads=[stg.k, par.k], writes=[wuq.k])

    pst2 = [ps(0, dt=BF16, shape=[8, 128]), ps(1, dt=BF16, shape=[8, 128])]
    pfm = [ps(2), ps(3)]
    pqc = ps(4)
    pqT = ps(5, off=0, nbytes=1024, dt=BF16, shape=[4, 128])
    pqrT = ps(5, off=1024, nbytes=1024, dt=BF16, shape=[4, 128])
    puq = [ps(6), ps(7), ps(4)]
    pQT = ps(5, dt=BF16, shape=[8, 128])

    def gen_norm_own(v):
        for ti in range(4):
            j = 4 * v + ti
            yield from gen_norm_tile(x_own[j * 128:(j + 1) * 128, :], xts2[j % 2], xn2, ssb2, hT2s[v % 2], ti, pst2)
        for par_ in range(2):
            S.dma("sp", "htw%d" % par_, HTd[v, par_], hT2s[v % 2][par_].ap, reads=[hT2s[v % 2][par_].k], writes=[("HTd", v, par_)])
        yield

    def gen_gr(v):
        hT2 = hT2s[v % 2]
        hkeys = [hT2[0].k, hT2[1].k]
        S.dma("sp", "hol", hown.ap, HOd[:, :, v * 512:(v + 1) * 512], reads=[("HOd", 2 * v), ("HOd", 2 * v + 1)],
              writes=[hown.k])
        for cc in range(8):
            pf = pfm[cc % 2]

            yield from pe_group(pf.ap, pf.k, (lambda j, cc=cc: w2.ap[:, j, cc * 128:(cc + 1) * 128]),
                                (lambda j: hsel(hT2, j)), [w2.k] + hkeys)
            sg = sgt[cc % 2]
            S.op("act", lambda e, pf=pf, sg=sg: e.activation(out=sg.ap, in_=pf.ap, func=AF.Silu),
                 reads=[pf.k], writes=[sg.k])
            yield
            S.op("dve", lambda e, sg=sg, cc=cc: e.tensor_tensor(out=yst.ap[:, cc, :], in0=sg.ap, in1=hown.ap[:, cc, :],
                                                               op=ALU.mult),
                 reads=[sg.k, hown.k], writes=[yst.sub(cc * 1024, 1024)])
            yield
        S.dma("sp", "ycw", YCd[0:8, :, v * 512:(v + 1) * 512].rearrange("c p n -> p c n"), yst.ap,
              reads=[yst.k], writes=[("YCd", 0, v)])
        yield

    def gen_q(v):
        hT2 = hT2s[v % 2]
        hkeys = [hT2[0].k, hT2[1].k]
        for ti in range(4):
            j = 4 * v + ti
            tcol = slice(ti * 128, (ti + 1) * 128)

            yield from pe_group(pqc.ap, pqc.k, (lambda jj, tcol=tcol: hsel(hT2, jj)[:, tcol]),
                                (lambda jj: w2.ap[:, jj, 1024:1536]), [w2.k] + hkeys)
            S.op("act", lambda e: e.activation(out=sqq.ap[:, 0:512], in_=pqc.ap, func=AF.Square, accum_out=qs.ap[:, 0:1]),
                 reads=[pqc.k], writes=[sqq.k, qs.sub(0, 4)])
            S.op("act", lambda e: e.activation(out=qcb.ap, in_=pqc.ap, func=AF.Copy), reads=[pqc.k], writes=[qcb.k])
            yield
            S.op("dve", lambda e: e.tensor_scalar(out=qs.ap[:, 1:2], in0=qs.ap[:, 0:1], scalar1=1.0 / 512, scalar2=EPS,
                                                  op0=ALU.mult, op1=ALU.add), reads=[qs.sub(0, 4)], writes=[qs.sub(4, 4)])
            S.op("dve", lambda e: e.tensor_scalar(out=qs.ap[:, 3:4], in0=qs.ap[:, 1:2], scalar1=EPS, scalar2=None,
                                                  op0=ALU.mult), reads=[qs.sub(4, 4)], writes=[qs.sub(12, 4)])

            def trq(e):
                for ck in range(4):
                    r = e.transpose(out=pqT.ap[:, ck, :], in_=qcb.ap[:, ck * 128:(ck + 1) * 128], identity=identb.ap)
                return r
            S.op("pe", trq, c=0.5, reads=[qcb.k, identb.k], writes=[pqT.k])
            yield
            S.op("dve", lambda e: e.tensor_copy(out=qcT.ap, in_=pqT.ap), reads=[pqT.k], writes=[qcT.k])
            yield
            for n in range(3):
                def muq(e, n=n):
                    for ck in range(4):
                        r = e.matmul(puq[n].ap, lhsT=qcT.ap[:, ck, :], rhs=wuq.ap[:, ck, n * 512:(n + 1) * 512],
                                     start=(ck == 0), stop=(ck == 3))
                    return r
                S.op("pe", muq, c=1.2, reads=[qcT.k, wuq.k], writes=[puq[n].k])
                S.op("act", lambda e, n=n: e.activation(out=Uq.ap[:, n * 512:(n + 1) * 512], in_=puq[n].ap, func=AF.Copy),
                     reads=[puq[n].k], writes=[Uq.sub(n * 2048, 2048)])
                yield
            U3 = Uq.ap.rearrange("p (h c) -> p h c", h=8)
            S3 = sqq.ap.rearrange("p (h c) -> p h c", h=8)
            S.op("dve", lambda e: e.tensor_tensor(out=sqq.ap, in0=Uq.ap, in1=Uq.ap, op=ALU.mult), reads=[Uq.k], writes=[sqq.k])
            yield
            S.op("dve", lambda e, S3=S3: e.tensor_reduce(out=ssqn.ap, in_=S3[:, :, 0:128], axis=AX.X, op=ALU.add),
                 reads=[sqq.k], writes=[ssqn.k])
            S.op("dve", lambda e, S3=S3: e.tensor_reduce(out=ssqr.ap, in_=S3[:, :, 128:192], axis=AX.X, op=ALU.add),
                 reads=[sqq.k], writes=[ssqr.k])
            yield
            S.op("dve", lambda e: e.tensor_scalar(out=ssqn.ap, in0=ssqn.ap, scalar1=1.0 / 128, scalar2=qs.ap[:, 3:4],
                                                  op0=ALU.mult, op1=ALU.add), reads=[ssqn.k, qs.sub(12, 4)], writes=[ssqn.k])
            S.op("dve", lambda e: e.tensor_scalar(out=ssqr.ap, in0=ssqr.ap, scalar1=1.0 / 64, scalar2=qs.ap[:, 3:4],
                                                  op0=ALU.mult, op1=ALU.add), reads=[ssqr.k, qs.sub(12, 4)], writes=[ssqr.k])
            yield
            rsqrt_small(rqn.ap, ssqn.ap, 8, [ssqn.k], [rqn.k])
            rsqrt_small(rqr.ap, ssqr.ap, 8, [ssqr.k], [rqr.k])
            yield
            S.op("dve", lambda e, U3=U3: e.tensor_tensor(out=qn.ap, in0=U3[:, :, 0:128],
                                                         in1=rqn.ap.unsqueeze(2).broadcast_to([128, 8, 128]), op=ALU.mult),
                 reads=[Uq.k, rqn.k], writes=[qn.k])
            qr2 = qrn.ap.rearrange("p h a n -> p h (a n)")
            S.op("dve", lambda e, U3=U3, qr2=qr2: e.tensor_tensor(out=qr2, in0=U3[:, :, 128:192],
                                                                  in1=rqr.ap.unsqueeze(2).broadcast_to([128, 8, 64]), op=ALU.mult),
                 reads=[Uq.k, rqr.k], writes=[qrn.k])
            yield

            def trQ(e):
                for h in range(8):
                    r = e.transpose(out=pQT.ap[:, h, :], in_=qn.ap[:, h, :], identity=identb.ap)
                return r
            S.op("pe", trQ, c=0.9, reads=[qn.k, identb.k], writes=[pQT.k])
            S.op("pool", lambda e, qr2=qr2: e.tensor_tensor(out=qr2, in0=qr2, in1=gqr_bc.unsqueeze(1).broadcast_to([128, 8, 64]),
                                                            op=ALU.mult), reads=[qrn.k, par.k], writes=[qrn.k])
            yield
            S.op("act", lambda e, tcol=tcol: e.activation(out=QTst.ap[:, :, tcol], in_=pQT.ap, func=AF.Identity, scale=gqn),
                 reads=[pQT.k, par.k], writes=[QTst.k])
            cos_t = cosO.ap[:, j, :].unsqueeze(1).unsqueeze(1).broadcast_to([128, 8, 2, 32])
            sin_t = sinO.ap[:, j, :].unsqueeze(1).unsqueeze(1).broadcast_to([128, 8, 2, 32])
            S.op("pool", lambda e, cos_t=cos_t: e.tensor_tensor(out=rqA.ap, in0=qrn.ap, in1=cos_t, op=ALU.mult),
                 reads=[qrn.k, cosO.k], writes=[rqA.k])
            S.op("pool", lambda e, sin_t=sin_t: e.tensor_tensor(out=rqB.ap, in0=qrn.ap, in1=sin_t, op=ALU.mult),
                 reads=[qrn.k, sinO.k], writes=[rqB.k])
            yield
            S.op("pool", lambda e: e.tensor_tensor(out=qpe.ap[:, :, 0, :], in0=rqA.ap[:, :, 0, :], in1=rqB.ap[:, :, 1, :],
                                                   op=ALU.subtract), reads=[rqA.k, rqB.k], writes=[qpe.k])
            S.op("pool", lambda e: e.tensor_tensor(out=qpe.ap[:, :, 1, :], in0=rqB.ap[:, :, 0, :], in1=rqA.ap[:, :, 1, :],
                                                   op=ALU.add), reads=[rqA.k, rqB.k], writes=[qpe.k])
            yield
            qpf = qpe.ap.rearrange("p h a n -> p (h a n)")

            def trR(e, qpf=qpf):
                for i in range(4):
                    r = e.transpose(out=pqrT.ap[:, i, :], in_=qpf[:, i * 128:(i + 1) * 128], identity=identb.ap)
                return r
            S.op("pe", trR, c=0.5, reads=[qpe.k, identb.k], writes=[pqrT.k])
            yield
            S.op("dve", lambda e, tcol=tcol: e.tensor_copy(out=QRst.ap[:, :, tcol], in_=pqrT.ap),
                 reads=[pqrT.k], writes=[QRst.k])
            yield
        S.dma("sp", "qtw", QTd[:, :, v * 512:(v + 1) * 512].rearrange("h p n -> p h n"), QTst.ap,
              reads=[QTst.k], writes=[("QTd", v)])
        S.dma("sp", "qrw", QRd[:, :, v * 512:(v + 1) * 512].rearrange("(i hh) d n -> (hh d) i n", hh=2), QRst.ap,
              reads=[QRst.k], writes=[("QRd", v)])
        yield

    interleave([gen_norm_own(0)])
    for v in range(NSUPO):
        interleave([gen_gr(v), gen_q(v), gen_norm_own(v + 1) if v + 1 < NSUPO else None])

    if STOP == 2:
        return finish()

    state["ptr"] = PERS_END
    w2b = sb([16, 1024], BF16)
    hT3 = [(sb([8, 512], BF16), sb([8, 512], BF16)), (sb([8, 512], BF16), sb([8, 512], BF16))]
    sgst = [sb([8, 512], BF16), sb([8, 512], BF16)]
    S.dma("pool", "w2b", w2b.ap, w_in_v[:, :, 3136:4160], writes=[w2b.k])
    pfb = [ps(i) for i in range(4)]

    def ht3_loads(v):
        hb = hT3[v % 2]
        for par_ in range(2):
            S.dma("sp", "htr%d%d" % (v % 2, par_), hb[par_].ap, HTd[v, par_], reads=[("HTd", v, par_)], writes=[hb[par_].k])
    ht3_loads(0)
    for v in range(NSUPO):
        hb3 = hT3[v % 2]
        if v + 1 < NSUPO:
            ht3_loads(v + 1)
        sgb = sgst[v % 2]
        for cc in range(8):
            pf = pfb[cc % 4]

            def mmg(e, cc=cc, pf=pf, hb3=hb3):
                for j in range(16):
                    r = e.matmul(pf.ap, lhsT=w2b.ap[:, j, cc * 128:(cc + 1) * 128], rhs=hsel(hb3, j),
                                 start=(j == 0), stop=(j == 15))
                return r
            S.op("pe", mmg, c=4.8, reads=[w2b.k, hb3[0].k, hb3[1].k], writes=[pf.k])
            S.op("act", lambda e, pf=pf, cc=cc, sgb=sgb: e.activation(out=sgb.ap[:, cc, :], in_=pf.ap, func=AF.Silu),
                 reads=[pf.k], writes=[sgb.sub(cc * 1024, 1024)])
        S.dma("sp", "sgw%d" % (v % 2), SGd[:, :, v * 512:(v + 1) * 512].rearrange("c p n -> p c n"), sgb.ap,
              reads=[sgb.k], writes=[("SGd", v)])

    if STOP == 3:
        return finish()

    state["ptr"] = ARENA_BYTES - 65536
    wout = sb([16, 2048], BF16)
    w_out_v = w_out.rearrange("(c p) n -> p c n", p=128)
    S.dma("pool", "wout", wout.ap[:, 0:8, :], w_out_v[:, 0:8, :], writes=[wout.k])
    S.dma("pool", "wout", wout.ap[:, 8:16, :], w_out_v[:, 8:16, :], writes=[wout.k], cont=True)
    state["ptr"] = PERS_END
    KTh = [sb([SEQ], BF16), sb([SEQ], BF16)]
    Vh = [sb([NT, 128], BF16), sb([NT, 128], BF16)]
    QN = [sb([2048], BF16), sb([2048], BF16)]
    QR = [sb([2048], BF16), sb([2048], BF16)]
    SG = [sb([2048], BF16), sb([2048], BF16)]
    PT = [sb([512], BF16) for _ in range(8)]
    rden = sb([512])
    o1 = sb([512])
    yast = [sb([512], BF16), sb([512], BF16)]
    acc = [sb([512]), sb([512])]
    acc_hi = sb([512], BF16)
    acc_lo = sb([512], BF16)
    pS = [ps(i) for i in range(4)]
    pO = [ps(4), ps(5)]
    pD = [ps(6), ps(7)]
    for sl_ in range(2):
        S.op("pool", lambda e, sl_=sl_: e.memset(QR[sl_].ap[64:128, :], 0.0), writes=[QR[sl_].k])
    all_kv_keys = [("KTd", u) for u in range(NSUP)] + [("Vd", u) for u in range(NSUP)]
    all_q_keys = [("QTd", v) for v in range(NSUPO)] + [("QRd", v) for v in range(NSUPO)] + [("SGd", v) for v in range(NSUPO)]
    kpe_keys = [("kpeT", t) for t in range(NT)]

    def head_loads(h):
        sl = h % 2
        S.dma("sp", "kth%d" % sl, KTh[sl].ap, KTd[h], reads=all_kv_keys, writes=[KTh[sl].k])
        S.dma("sp", "vh%d" % sl, Vh[sl].ap, Vd[h], reads=all_kv_keys, writes=[Vh[sl].k])
        S.dma("sp", "qn%d" % sl, QN[sl].ap, QTd[h], reads=all_q_keys, writes=[QN[sl].k])
        S.dma("sp", "qr%d" % sl, QR[sl].ap[0:64, :], QRd[h], reads=all_q_keys + [QR[sl].k], writes=[("qrlo", sl)])
        S.dma("sp", "sg%d" % sl, SG[sl].ap, SGd[h], reads=all_q_keys, writes=[SG[sl].k])

    blk = 0
    head_loads(0)
    for h in range(8):
        sl = h % 2
        for qb in range(4):
            items = []
            for kt in range(8 * qb):
                items.append((kt, 0, None))
            for i in range(4):
                for e_ in range(2):
                    items.append((8 * qb + 2 * i + e_, i * 128, maskA if e_ == 0 else maskB))
            n_it = len(items)
            po = pO[blk % 2]
            pd = pD[blk % 2]
            blk += 1
            LAG = 3

            def qk(idx, kt, c0, sl=sl, qb=qb):
                N = 512 - c0
                psb = pS[idx % 4]
                pt = PT[idx % 8]
                q0 = qb * 512 + c0

                def f(e):
                    e.matmul(psb.ap[:, 0:N], lhsT=KTh[sl].ap[:, kt * 128:(kt + 1) * 128], rhs=QN[sl].ap[:, q0:q0 + N],
                             start=True, stop=False)
                    return e.matmul(psb.ap[:, 0:N], lhsT=KpeT.ap[:, kt * 128:(kt + 1) * 128],
                                    rhs=QR[sl].ap[:, q0:q0 + N], start=False, stop=True)
                S.op("pe", f, reads=[KTh[sl].k, QN[sl].k, QR[sl].k, ("qrlo", sl), "kpeT_pad"] + kpe_keys, writes=[psb.k])
                S.op("act", lambda e: e.activation(out=pt.ap[:, 0:N], in_=psb.ap[:, 0:N], func=AF.Exp, scale=SM_SCALE),
                     reads=[psb.k], writes=[pt.k])

            def pv(idx, kt, c0, mask, sl=sl, po=po, pd=pd, n_it=n_it, blkpar=(blk % 2)):
                N = 512 - c0
                pt = PT[idx % 8]
                if mask is not None:
                    S.op("pool", lambda e: e.tensor_tensor(out=pt.ap[:, 0:128], in0=pt.ap[:, 0:128], in1=mask.ap, op=ALU.mult),
                         reads=[pt.k, mask.k], writes=[pt.k])
                first = (idx == 0)
                last = (idx == n_it - 1)
                ac = acc[blkpar]
                if first:
                    S.op("dve", lambda e: e.tensor_copy(out=ac.ap, in_=pt.ap), reads=[pt.k], writes=[ac.k])
                else:
                    S.op("dve", lambda e: e.tensor_tensor(out=ac.ap[:, c0:512], in0=ac.ap[:, c0:512], in1=pt.ap[:, 0:N], op=ALU.add),
                         reads=[pt.k, ac.k], writes=[ac.k])

                def f(e):
                    return e.matmul(po.ap[:, c0:512], lhsT=Vh[sl].ap[:, kt, :], rhs=pt.ap[:, 0:N], start=first, stop=last,
                                    skip_group_check=True)
                S.op("pe", f, reads=[Vh[sl].k, pt.k], writes=[po.k])
                if last:
                    S.op("dve", lambda e: e.tensor_copy(out=acc_hi.ap, in_=ac.ap), reads=[ac.k], writes=[acc_hi.k])
                    S.op("dve", lambda e: e.tensor_tensor(out=acc_lo.ap, in0=ac.ap, in1=acc_hi.ap, op=ALU.subtract),
                         reads=[ac.k, acc_hi.k], writes=[acc_lo.k])

                    def fd(e):
                        e.matmul(pd.ap, lhsT=onesb.ap, rhs=acc_hi.ap, start=True, stop=False)
                        return e.matmul(pd.ap, lhsT=onesb.ap, rhs=acc_lo.ap, start=False, stop=True)
                    S.op("pe", fd, reads=[acc_hi.k, acc_lo.k, onesb.k], writes=[pd.k])

            for idx in range(n_it + LAG):
                if idx < n_it:
                    qk(idx, items[idx][0], items[idx][1])
                if idx - LAG >= 0:
                    it = items[idx - LAG]
                    pv(idx - LAG, it[0], it[1], it[2])
            ya = yast[qb % 2]
            S.op("dve", lambda e, pd=pd: e.reciprocal(out=rden.ap, in_=pd.ap), reads=[pd.k], writes=[rden.k])
            S.op("dve", lambda e, po=po: e.tensor_tensor(out=o1.ap, in0=po.ap, in1=rden.ap, op=ALU.mult),
                 reads=[po.k, rden.k], writes=[o1.k])
            S.op("pool", lambda e, ya=ya, qb=qb, sl=sl: e.tensor_tensor(out=ya.ap, in0=o1.ap,
                                                                       in1=SG[sl].ap[:, qb * 512:(qb + 1) * 512], op=ALU.mult),
                 reads=[o1.k, SG[sl].k], writes=[ya.k])
            S.dma("sp", "yaw%d" % (qb % 2), YCd[8 + h, :, qb * 512:(qb + 1) * 512], ya.ap, reads=[ya.k],
                  writes=[("YCd", 8 + h, qb)])
            if qb == 0 and h + 1 < 8:
                head_loads(h + 1)

    if STOP == 4:
        return finish()

    state["ptr"] = PERS_END
    gate_bc = sb([2048])
    S.dma("sp", "gbr", gate_bc.ap, GBd, reads=["GBd"], writes=[gate_bc.k])
    ycT = [sb([16, 512], BF16), sb([16, 512], BF16)]
    xo = [sb([2048]), sb([2048])]
    ot = [sb([2048]), sb([2048])]
    pout = [ps(i) for i in range(8)]
    yc_keys = [("YCd", 0, v) for v in range(NSUPO)] + [("YCd", 8 + h, qb) for h in range(8) for qb in range(4)]
    out_keys = []
    def yc_load(v):
        S.dma("sp", "ycl%d" % (v % 2), ycT[v % 2].ap, YCd[:, :, v * 512:(v + 1) * 512].rearrange("c p n -> p c n"),
              reads=yc_keys, writes=[ycT[v % 2].k])

    def xo_load(j):
        S.dma("sp", "xo%d" % (j % 2), xo[j % 2].ap, x_own[j * 128:(j + 1) * 128, :], writes=[xo[j % 2].k])
    yc_load(0)
    xo_load(0)
    for v in range(NSUPO):
        yb = ycT[v % 2]
        if v + 1 < NSUPO:
            yc_load(v + 1)
        for ti in range(4):
            j = 4 * v + ti
            xb = xo[j % 2]
            ob = ot[j % 2]
            if j + 1 < NTO:
                xo_load(j + 1)
            for n in range(4):
                pb = pout[(j % 2) * 4 + n]

                def mo(e, n=n, pb=pb, ti=ti, yb=yb):
                    for c in range(16):
                        r = e.matmul(pb.ap, lhsT=yb.ap[:, c, ti * 128:(ti + 1) * 128], rhs=wout.ap[:, c, n * 512:(n + 1) * 512],
                                     start=(c == 0), stop=(c == 15))
                    return r
                S.op("pe", mo, c=4.8, reads=[yb.k, wout.k], writes=[pb.k])
                cs = slice(n * 512, (n + 1) * 512)
                S.op("dve", lambda e, pb=pb, ob=ob, cs=cs: e.tensor_tensor(out=ob.ap[:, cs], in0=pb.ap, in1=gate_bc.ap[:, cs],
                                                                          op=ALU.mult),
                     reads=[pb.k, gate_bc.k], writes=[ob.sub(n * 2048, 2048)])
                S.op("pool", lambda e, ob=ob, xb=xb, cs=cs: e.tensor_tensor(out=ob.ap[:, cs], in0=ob.ap[:, cs], in1=xb.ap[:, cs],
                                                                           op=ALU.add),
                     reads=[ob.sub(n * 2048, 2048), xb.k], writes=[ob.sub(n * 2048, 2048)])
            S.dma("sp", "ow%d" % (j % 2), out_d[j * 128:(j + 1) * 128, :], ob.ap,
                  reads=[ob.k], writes=[("out", j)])
            out_keys.append(("out", j))
    return finish()


_NC_CACHE = {}


def _own_tiles(s):
    return [2 * j + s for j in range(NTO)]


def _prep(x, c, positions, w_ada, b_ada, w_in, conv_w, conv_b, w_rg_a, b_rg_a, w_rg_x, b_rg_x, lru_lambda,
           q_a_norm, w_uq, kv_a_norm, w_ukv, q_norm_nope, q_norm_rope, k_norm_nope, k_norm_rope, w_out):
    f32 = np.float32
    x = np.asarray(x, f32)
    c = np.asarray(c, f32)
    positions = np.asarray(positions, np.int32)
    g = lambda a: np.ascontiguousarray(np.asarray(a, f32)[0])
    w_ada_, b_ada_, w_in_ = g(w_ada), g(b_ada), g(w_in)
    conv_w_, conv_b_ = g(conv_w), g(conv_b)
    w_rg_a_, b_rg_a_, w_rg_x_, b_rg_x_, lam_ = g(w_rg_a), g(b_rg_a), g(w_rg_x), g(b_rg_x), g(lru_lambda)
    q_a_, w_uq_, kv_a_, w_ukv_ = g(q_a_norm), g(w_uq), g(kv_a_norm), g(w_ukv)
    qnn_, qnr_, knn_, knr_, w_out_ = g(q_norm_nope), g(q_norm_rope), g(k_norm_nope), g(k_norm_rope), g(w_out)

    ident = np.eye(128, dtype=f32)
    tri = (np.arange(128)[:, None] <= np.arange(128)[None, :]).astype(f32)
    ones = np.ones((128, 128), f32)
    zeros = np.zeros((128, 128), f32)
    inv_freq = (1.0 / (10000.0 ** (np.arange(0, 64, 2, dtype=np.float32) / 64.0))).astype(f32)

    chan = lambda v: np.ascontiguousarray(v.reshape(8, 128).T)
    rnn = np.stack([chan(conv_w_[0]), chan(conv_w_[1]), chan(conv_w_[2]), chan(conv_w_[3]), chan(conv_b_),
                    chan(b_rg_a_), chan(b_rg_x_), chan(lam_)], axis=2)
    b_ada_rep = np.ascontiguousarray(np.broadcast_to(b_ada_[None, :], (128, 3 * D)))

    in_maps = []
    for core in range(8):
        b, s = core // 2, core % 2
        own = _own_tiles(s)
        xb = x[b]
        x_own = np.ascontiguousarray(xb.reshape(NT, 128, D)[own].reshape(NTO * 128, D))
        par = np.zeros((128, 256), f32)
        par[:, 0:16] = c[b].reshape(128, 16)
        par[:, 16:80] = rnn.reshape(128, 64)
        par[:, 80:84] = q_a_.reshape(4, 128).T
        par[:, 84:88] = kv_a_.reshape(4, 128).T
        par[:, 88] = qnn_
        par[:, 89] = knn_
        par[:, 90] = 1.0 if s == 0 else 0.0
        par[:, 91] = 0.0 if s == 0 else 1.0
        par[:, 92:156] = qnr_[None, :]
        par[:, 156:220] = knr_[None, :]
        par[:, 220:252] = inv_freq[None, :]
        cst = np.concatenate([ident, tri if s == 0 else ones, zeros if s == 0 else tri], axis=1)
        pos_t = positions[b].reshape(NT, 128).T
        posi = np.ascontiguousarray(np.concatenate([pos_t, pos_t[:, own]], axis=1)).astype(np.int32)
        in_maps.append(dict(
            x_all=np.ascontiguousarray(xb), x_own=x_own, par=par, cst=np.ascontiguousarray(cst), posi=posi,
            w_ada=w_ada_, b_ada=b_ada_rep, w_in=w_in_, w_rg_a=w_rg_a_, w_rg_x=w_rg_x_, w_uq=w_uq_, w_ukv=w_ukv_,
            w_out=w_out_))
    return in_maps


def kernel(**inputs):
    f32 = np.float32
    in_maps = _prep(**inputs)
    if "nc" not in _NC_CACHE:
        _NC_CACHE["nc"] = build_program()
    nc = _NC_CACHE["nc"]
    res = run_bass_kernel_spmd(nc, in_maps, core_ids=list(range(8)))
    out = np.empty((4, SEQ, D), f32)
    for core in range(8):
        b, s = core // 2, core % 2
        o = np.asarray(res.results[core]["out"], f32).reshape(NTO, 128, D)
        out[b].reshape(NT, 128, D)[_own_tiles(s)] = o
    return out
```
